# Optimizing a Trainium2 kernel written in Bass

```python
import math
import jax, jax.numpy as jnp
from jax import lax
import numpy as np

D_MODEL = 2048
BATCH = 16
SEQ = 256
DEPTH = 2
DEC_BATCH = 4
DEC_SEQ = 1024
PAST_LEN = 256

GRID_W = 64
HEAD_DIM = 128
EPS = 1e-6
GLA_H = 4
GLA_DK = 64
GLA_DV = 128
GLA_W = GLA_H * GLA_DV
GLA_LR = 16
GLA_GATE_NORM = 16.0
GLA_CHUNK = 64
ATT_HQ = 8
ATT_HKV = 2
ATT_W = ATT_HQ * HEAD_DIM
Q_BLOCK = 128
ROPE_THETA = 10000.0
DN_H = 4
DN_DK = 128
DN_DV = 128
DN_W = DN_H * DN_DV
DN_CONV = 3
DN_CHUNK = 64
MIX_W = GLA_W + ATT_W + DN_W
FF = 4 * D_MODEL
N_MOD = 6
SPLITS = (GLA_H * GLA_DK, GLA_H * GLA_DK, GLA_W, GLA_W, 2 * GLA_LR,
          ATT_HQ * HEAD_DIM, ATT_HKV * HEAD_DIM, ATT_HKV * HEAD_DIM,
          3 * DN_W, DN_W, 2 * DN_H, 2 * DN_H)
IN_COLS = sum(SPLITS)

kernel_name = 'hybrid_gla_gqa_deltanet_diffusion_step'


def _rms_norm(x, g):
    xf = x.astype(jnp.float32)
    y = xf * lax.rsqrt(jnp.mean(xf * xf, axis=-1, keepdims=True) + EPS)
    return (y * g.astype(jnp.float32)).astype(x.dtype)


def _l2norm(x):
    return x * lax.rsqrt(jnp.sum(x * x, axis=-1, keepdims=True) + EPS)


def _flip(t):
    return jnp.flip(t, axis=1)


def _split_cols(p):
    idx = np.cumsum(SPLITS)[:-1].tolist()
    return jnp.split(p, idx, axis=-1)


def _to_chunks(t, C):
    B, L, H = t.shape[:3]
    t = t.reshape((B, L // C, C, H) + t.shape[3:])
    return jnp.moveaxis(t, (1, 3), (0, 2))


def _from_chunks(o):
    o = jnp.moveaxis(o, (0, 2), (1, 3))
    B, n, C, H, d = o.shape
    return o.reshape(B, n * C, H, d)


def _modulation(cond, w_mod, b_mod):
    m = jax.nn.silu(cond) @ w_mod + b_mod
    return [t[:, None, :] for t in jnp.split(m, N_MOD, axis=-1)]


def _gla_scan(q, k, v, la, S0):
    C = GLA_CHUNK
    mask = jnp.tril(jnp.ones((C, C), bool))[:, :, None]

    def step(S, xs):
        qc, kc, vc, gc = xs
        b = jnp.cumsum(gc, axis=-2)
        diff = b[..., :, None, :] - b[..., None, :, :]
        dec = jnp.where(mask, jnp.exp(jnp.where(mask, diff, 0.0)), 0.0)
        A = jnp.einsum('bhid,bhijd,bhjd->bhij', qc, dec, kc)
        o = jnp.einsum('bhij,bhjv->bhiv', A, vc) + jnp.einsum('bhid,bhdv->bhiv', qc * jnp.exp(b), S)
        bl = b[..., -1, :]
        S = S * jnp.exp(bl)[..., None] + jnp.einsum('bhjd,bhjv->bhdv', kc * jnp.exp(bl[..., None, :] - b), vc)
        return S, o

    S, o = lax.scan(step, S0, (_to_chunks(q, C), _to_chunks(k, C), _to_chunks(v, C), _to_chunks(la, C)))
    return _from_chunks(o), S


def _delta_scan(q, k, v, g, beta, S0):
    C = DN_CHUNK
    incl = jnp.tril(jnp.ones((C, C), bool))
    strict = jnp.tril(jnp.ones((C, C), bool), -1)
    dv = v.shape[-1]

    def step(S, xs):
        qc, kc, vc, gc, bc = xs
        G = jnp.cumsum(gc, axis=-1)
        diff = G[..., :, None] - G[..., None, :]
        gam = jnp.where(incl, jnp.exp(jnp.where(incl, diff, 0.0)), 0.0)
        kb = kc * bc[..., None]
        A = jnp.where(strict, jnp.einsum('bhid,bhjd->bhij', kb, kc) * gam, 0.0)
        rhs = jnp.concatenate([vc * bc[..., None], kb * jnp.exp(G)[..., None]], axis=-1)
        sol = lax.linalg.triangular_solve(A, rhs, left_side=True, lower=True, unit_diagonal=True)
        u, w = sol[..., :dv], sol[..., dv:]
        v_new = u - jnp.einsum('bhcd,bhdv->bhcv', w, S)
        att = jnp.einsum('bhid,bhjd->bhij', qc, kc) * gam
        o = jnp.einsum('bhid,bhdv->bhiv', qc * jnp.exp(G)[..., None], S) + jnp.einsum('bhij,bhjv->bhiv', att, v_new)
        Gl = G[..., -1:]
        S = S * jnp.exp(Gl)[..., None] + jnp.einsum('bhjd,bhjv->bhdv', kc * jnp.exp(Gl - G)[..., None], v_new)
        return S, o

    S, o = lax.scan(step, S0, (_to_chunks(q, C), _to_chunks(k, C), _to_chunks(v, C),
                                _to_chunks(g, C), _to_chunks(beta, C)))
    return _from_chunks(o), S


def _gla_mixer(gq, gk, gv, gr, glr, w2, bias, norm_g, S0):
    B, L, _ = gq.shape
    f32 = jnp.float32
    q = gq.reshape(B, L, GLA_H, GLA_DK).astype(f32) * GLA_DK ** -0.5
    k = gk.reshape(B, L, GLA_H, GLA_DK).astype(f32)
    v = gv.reshape(B, L, GLA_H, GLA_DV).astype(f32)
    lr = glr.reshape(B, L, 2, GLA_LR).astype(f32)
    la = jax.nn.log_sigmoid(jnp.einsum('blzr,zrk->blzk', lr, w2.astype(f32)) + bias.astype(f32)) / GLA_GATE_NORM
    la = la.reshape(B, L, 2, GLA_H, GLA_DK)
    S0 = S0.astype(f32)
    of, Sf = _gla_scan(q, k, v, la[:, :, 0], S0[:, 0])
    ob, Sb = _gla_scan(_flip(q), _flip(k), _flip(v), _flip(la[:, :, 1]), S0[:, 1])
    o = of + _flip(ob)
    o = _rms_norm(o, norm_g) * jax.nn.silu(gr.reshape(B, L, GLA_H, GLA_DV).astype(f32))
    return o.reshape(B, L, GLA_W).astype(gq.dtype), jnp.stack([Sf, Sb], axis=1)


def _axial_rope(x):
    B, L, H, hd = x.shape
    f32 = jnp.float32
    n_rows = L // GRID_W
    row = jnp.repeat(jnp.arange(n_rows), GRID_W)
    col = jnp.tile(jnp.arange(GRID_W), n_rows)
    half = hd // 2
    inv = ROPE_THETA ** (-jnp.arange(0, half, 2, dtype=f32) / half)

    def rot(xa, pos):
        ang = pos.astype(f32)[:, None] * inv
        cos = jnp.cos(ang)[None, :, None, :]
        sin = jnp.sin(ang)[None, :, None, :]
        x1, x2 = jnp.split(xa.astype(f32), 2, axis=-1)
        return jnp.concatenate([x1 * cos - x2 * sin, x2 * cos + x1 * sin], axis=-1)

    return jnp.concatenate([rot(x[..., :half], row), rot(x[..., half:], col)], axis=-1).astype(x.dtype)


def _attention(q, k, v):
    B, L, HQ, hd = q.shape
    hkv = k.shape[2]
    G = HQ // hkv
    nb = L // Q_BLOCK
    qb = jnp.moveaxis(q.reshape(B, nb, Q_BLOCK, hkv, G, hd), 1, 0)
    scale = hd ** -0.5

    def block(qi):
        s = jnp.einsum('bqhgd,bkhd->bhgqk', qi, k).astype(jnp.float32) * scale
        p = jax.nn.softmax(s, axis=-1)
        return jnp.einsum('bhgqk,bkhd->bqhgd', p.astype(v.dtype), v)

    o = lax.map(block, qb)
    return jnp.moveaxis(o, 0, 1).reshape(B, L, HQ, hd)


def _gqa_mixer(aq, ak, av, q_g, k_g, kv_ctx):
    B, L, _ = aq.shape
    q = _rms_norm(aq.reshape(B, L, ATT_HQ, HEAD_DIM), q_g)
    k = _rms_norm(ak.reshape(B, L, ATT_HKV, HEAD_DIM), k_g)
    v = av.reshape(B, L, ATT_HKV, HEAD_DIM)
    if kv_ctx is None:
        o = _attention(q, k, v)
    else:
        k_ctx, v_ctx = kv_ctx
        k_all = jnp.concatenate([k_ctx.astype(k.dtype), _axial_rope(k)], axis=1)
        v_all = jnp.concatenate([v_ctx.astype(v.dtype), v], axis=1)
        o = _attention(_axial_rope(q), k_all, v_all)
    return o.reshape(B, L, ATT_W), k, v


def _short_conv(x, w):
    Ch = x.shape[-1]
    y = lax.conv_general_dilated(x, w[:, None, :], window_strides=(1,),
                                 padding=[(DN_CONV // 2, DN_CONV // 2)],
                                 dimension_numbers=('NWC', 'WIO', 'NWC'),
                                 feature_group_count=Ch)
    return jax.nn.silu(y)


def _deltanet_mixer(dqkv, dz, da, db, conv_w, a_log, dt_bias, norm_g, S0):
    B, L, _ = dqkv.shape
    f32 = jnp.float32
    qkv = _short_conv(dqkv.astype(f32), conv_w.astype(f32))
    q, k, v = jnp.split(qkv, 3, axis=-1)
    q = _l2norm(q.reshape(B, L, DN_H, DN_DK)) * DN_DK ** -0.5
    k = _l2norm(k.reshape(B, L, DN_H, DN_DK))
    v = v.reshape(B, L, DN_H, DN_DV)
    a = da.reshape(B, L, 2, DN_H).astype(f32)
    b = db.reshape(B, L, 2, DN_H).astype(f32)
    g = -jnp.exp(a_log.astype(f32)) * jax.nn.softplus(a + dt_bias.astype(f32))
    beta = jax.nn.sigmoid(b)
    S0 = S0.astype(f32)
    of, Sf = _delta_scan(q, k, v, g[:, :, 0], beta[:, :, 0], S0[:, 0])
    ob, Sb = _delta_scan(_flip(q), _flip(k), _flip(v), _flip(g[:, :, 1]), _flip(beta[:, :, 1]), S0[:, 1])
    o = of + _flip(ob)
    o = _rms_norm(o, norm_g) * jax.nn.silu(dz.reshape(B, L, DN_H, DN_DV).astype(f32))
    return o.reshape(B, L, DN_W).astype(dqkv.dtype), jnp.stack([Sf, Sb], axis=1)


def _layer(x, mods, lw, ctx):
    shift1, scale1, gate1, shift2, scale2, gate2 = mods
    B, L, _ = x.shape
    h = _rms_norm(x, lw['norm1_g']) * (1.0 + scale1) + shift1
    gq, gk, gv, gr, glr, aq, ak, av, dqkv, dz, da, db = _split_cols(h @ lw['w_in'])
    if ctx is None:
        kv_ctx = None
        s_gla0 = jnp.zeros((B, 2, GLA_H, GLA_DK, GLA_DV), jnp.float32)
        s_dn0 = jnp.zeros((B, 2, DN_H, DN_DK, DN_DV), jnp.float32)
    else:
        k_ctx, v_ctx, s_gla0, s_dn0 = ctx
        kv_ctx = (k_ctx, v_ctx)
    o_gla, s_gla = _gla_mixer(gq, gk, gv, gr, glr, lw['gla_w2'], lw['gla_b'], lw['gla_norm_g'], s_gla0)
    o_att, k, v = _gqa_mixer(aq, ak, av, lw['q_norm_g'], lw['k_norm_g'], kv_ctx)
    o_dn, s_dn = _deltanet_mixer(dqkv, dz, da, db, lw['dn_conv'], lw['dn_a_log'], lw['dn_dt_bias'],
                                 lw['dn_norm_g'], s_dn0)
    mix = jnp.concatenate([o_gla, o_att, o_dn], axis=-1) @ lw['w_out']
    x = x + gate1 * mix
    h = _rms_norm(x, lw['norm2_g']) * (1.0 + scale2) + shift2
    ff = jnp.square(jax.nn.relu(h @ lw['w_ff1'])) @ lw['w_ff2']
    x = x + gate2 * ff
    return x, (k, v, s_gla.astype(x.dtype), s_dn.astype(x.dtype))


def setup_inputs(seed: int = 0) -> dict:
    key = jax.random.key(seed)
    ks = jax.random.split(key, 32)
    f32 = jnp.float32

    def nrm(k, shape, s):
        return jax.random.normal(k, shape, f32) * s

    dt = jnp.exp(jax.random.uniform(ks[20], (DEPTH, 2, DN_H), f32, math.log(1e-3), math.log(1e-1)))
    return {
        'x_prompt': nrm(ks[0], (BATCH, SEQ, D_MODEL), 1.0),
        'x_sample': nrm(ks[1], (DEC_BATCH, DEC_SEQ, D_MODEL), 1.0),
        'cache_k': nrm(ks[2], (DEC_BATCH, DEPTH, PAST_LEN, ATT_HKV, HEAD_DIM), 1.0),
        'cache_v': nrm(ks[3], (DEC_BATCH, DEPTH, PAST_LEN, ATT_HKV, HEAD_DIM), 1.0),
        'state_gla': nrm(ks[4], (DEC_BATCH, DEPTH, 2, GLA_H, GLA_DK, GLA_DV), 1.0),
        'state_dn': nrm(ks[5], (DEC_BATCH, DEPTH, 2, DN_H, DN_DK, DN_DV), 0.3),
        'c': nrm(ks[6], (DEC_BATCH, D_MODEL), 1.0),
        'c_ctx': nrm(ks[7], (D_MODEL,), 1.0),
        'norm1_g': 1.0 + nrm(ks[8], (DEPTH, D_MODEL), 0.02),
        'norm2_g': 1.0 + nrm(ks[9], (DEPTH, D_MODEL), 0.02),
        'w_mod': nrm(ks[10], (DEPTH, D_MODEL, N_MOD * D_MODEL), 0.5 * D_MODEL ** -0.5),
        'b_mod': nrm(ks[11], (DEPTH, N_MOD * D_MODEL), 0.02),
        'w_in': nrm(ks[12], (DEPTH, D_MODEL, IN_COLS), D_MODEL ** -0.5),
        'gla_w2': nrm(ks[13], (DEPTH, 2, GLA_LR, GLA_H * GLA_DK), GLA_LR ** -0.5),
        'gla_b': nrm(ks[14], (DEPTH, 2, GLA_H * GLA_DK), 0.1),
        'gla_norm_g': 1.0 + nrm(ks[15], (DEPTH, GLA_DV), 0.02),
        'q_norm_g': 1.0 + nrm(ks[16], (DEPTH, HEAD_DIM), 0.02),
        'k_norm_g': 1.0 + nrm(ks[17], (DEPTH, HEAD_DIM), 0.02),
        'dn_conv': nrm(ks[18], (DEPTH, DN_CONV, 3 * DN_W), DN_CONV ** -0.5),
        'dn_a_log': jnp.log(jax.random.uniform(ks[19], (DEPTH, 2, DN_H), f32, 1.0, 16.0)),
        'dn_dt_bias': dt + jnp.log(-jnp.expm1(-dt)),
        'dn_norm_g': 1.0 + nrm(ks[21], (DEPTH, DN_DV), 0.02),
        'w_out': nrm(ks[22], (DEPTH, MIX_W, D_MODEL), MIX_W ** -0.5),
        'w_ff1': nrm(ks[23], (DEPTH, D_MODEL, FF), D_MODEL ** -0.5),
        'w_ff2': nrm(ks[24], (DEPTH, FF, D_MODEL), FF ** -0.5),
    }


def reference(x_prompt, x_sample, cache_k, cache_v, state_gla, state_dn, c, c_ctx,
              norm1_g, norm2_g, w_mod, b_mod, w_in, gla_w2, gla_b, gla_norm_g,
              q_norm_g, k_norm_g, dn_conv, dn_a_log, dn_dt_bias, dn_norm_g,
              w_out, w_ff1, w_ff2):
    stacked = dict(norm1_g=norm1_g, norm2_g=norm2_g, w_in=w_in, gla_w2=gla_w2, gla_b=gla_b,
                   gla_norm_g=gla_norm_g, q_norm_g=q_norm_g, k_norm_g=k_norm_g, dn_conv=dn_conv,
                   dn_a_log=dn_a_log, dn_dt_bias=dn_dt_bias, dn_norm_g=dn_norm_g,
                   w_out=w_out, w_ff1=w_ff1, w_ff2=w_ff2)

    x = x_prompt
    ks, vs, sgs, sds = [], [], [], []
    for l in range(DEPTH):
        lw = {n: a[l] for n, a in stacked.items()}
        mods = _modulation(c_ctx[None, :], w_mod[l], b_mod[l])
        x, (k, v, s_gla, s_dn) = _layer(x, mods, lw, None)
        ks.append(k)
        vs.append(v)
        sgs.append(s_gla)
        sds.append(s_dn)
    y_prompt = x
    new_cache_k = jnp.stack(ks, axis=1)
    new_cache_v = jnp.stack(vs, axis=1)
    new_state_gla = jnp.stack(sgs, axis=1)
    new_state_dn = jnp.stack(sds, axis=1)

    x = x_sample
    for l in range(DEPTH):
        lw = {n: a[l] for n, a in stacked.items()}
        mods = _modulation(c, w_mod[l], b_mod[l])
        ctx = (cache_k[:, l], cache_v[:, l], state_gla[:, l], state_dn[:, l])
        x, _ = _layer(x, mods, lw, ctx)
    y_sample = x
    return (y_prompt, y_sample, new_cache_k, new_cache_v, new_state_gla, new_state_dn)
```

```python
import os
import numpy as np
from contextlib import ExitStack
import concourse.bass as bass
import concourse.mybir as mybir
from concourse.bass_utils import run_bass_kernel_spmd

F32 = mybir.dt.float32
BF16 = mybir.dt.bfloat16
AF = mybir.ActivationFunctionType
ALU = mybir.AluOpType

T = 1024
D = 2048
KC = 16
NCH = 16
EPS = 1e-6
NSLOT = 3
ENGS = ("pe", "act", "dve", "pool", "sp")
SEM_CAP = 30000
KDBG = os.environ.get("KDBG", "")
KSTOP = os.environ.get("KSTOP", "")
KSK = os.environ.get("KSK", "")
KGS = os.environ.get("KGS", "")
LAZYMOD = False
CHD = F32 if os.environ.get("KCHAIN", "bf16") == "f32" else BF16


class Res:
    __slots__ = ("w", "rs")

    def __init__(self):
        self.w = None
        self.rs = []


class Op:
    __slots__ = ("eng", "idx", "fn", "waits", "signal", "clock", "dma", "sig_no")

    def __init__(self, eng, idx, fn):
        self.eng, self.idx, self.fn = eng, idx, fn
        self.waits = []
        self.signal = False
        self.clock = None
        self.dma = None
        self.sig_no = None


class Sched:
    def __init__(self):
        self.ops = {e: [] for e in ENGS}
        self.clock = {e: {} for e in ENGS}
        self.dma_val = {}
        self.dma_clock = {}

    def _need(self, eng, dep, same_ok):
        ck = self.clock[eng]
        if dep[0] == "e":
            if dep[1] == eng and same_ok:
                return False
            return ck.get(dep[1], -1) < dep[2]
        return ck.get(("d", dep[1]), 0) < dep[2]

    def _merge(self, eng, dep):
        ck = self.clock[eng]
        if dep[0] == "e":
            op2 = self.ops[dep[1]][dep[2]]
            op2.signal = True
            src = op2.clock
            if ck.get(dep[1], -1) < dep[2]:
                ck[dep[1]] = dep[2]
        else:
            src = self.dma_clock[(dep[1], dep[2])]
            ck[("d", dep[1])] = dep[2]
        for k, v in src.items():
            if ck.get(k, -1) < v:
                ck[k] = v

    def op(self, eng, fn, reads=(), writes=(), dma_sem=None):
        lst = self.ops[eng]
        o = Op(eng, len(lst), fn)
        deps = []
        for r in reads:
            if r.w is not None:
                deps.append((r.w, False))
        for w in writes:
            if w.w is not None:
                deps.append((w.w, True))
            for d in w.rs:
                deps.append((d, True))
        agg = {}
        for dep, same_ok in deps:
            if dep[0] == "e":
                if dep[1] == eng and (same_ok or eng == "pe"):
                    continue
                k = ("e", dep[1])
            else:
                k = ("d", dep[1])
            if k not in agg or agg[k][2] < dep[2]:
                agg[k] = dep
        for dep in agg.values():
            if self._need(eng, dep, False):
                o.waits.append(dep)
                self._merge(eng, dep)
        o.clock = dict(self.clock[eng])
        lst.append(o)
        if dma_sem is not None:
            v = self.dma_val.get(dma_sem, 0) + 16
            self.dma_val[dma_sem] = v
            o.dma = (dma_sem, v)
            self.dma_clock[(dma_sem, v)] = dict(o.clock)
            me = ("d", dma_sem, v)
        else:
            me = ("e", eng, o.idx)
        for r in reads:
            r.rs.append(me)
        for w in writes:
            w.w = me
            w.rs = []
        return o

    def barrier(self, engs=("pe", "act", "dve", "sp", "pool")):
        last = {}
        for e in engs:
            for o in reversed(self.ops[e]):
                if o.fn is not None and o.dma is None:
                    last[e] = o.idx
                    break
        dmas = dict(self.dma_val)
        for e in engs:
            lst = self.ops[e]
            o = Op(e, len(lst), None)
            for e2, i2 in last.items():
                dep = ("e", e2, i2)
                if self._need(e, dep, False):
                    o.waits.append(dep)
                    self._merge(e, dep)
            for sk, v in dmas.items():
                if sk.startswith("w"):
                    continue
                dep = ("d", sk, v)
                if self._need(e, dep, False):
                    o.waits.append(dep)
                    self._merge(e, dep)
            o.clock = dict(self.clock[e])
            lst.append(o)

    def wait_all_dma(self, eng="sp"):
        lst = self.ops[eng]
        o = Op(eng, len(lst), None)
        for sk, v in self.dma_val.items():
            o.waits.append(("d", sk, v))
        o.clock = dict(self.clock[eng])
        lst.append(o)

    def emit(self, nc, stack):
        nsig = {}
        for e in ENGS:
            n = 0
            for o in self.ops[e]:
                if o.signal:
                    n += 1
                    o.sig_no = n
            nsig[e] = n
        esems = {}
        for e in ENGS:
            k = max(1, (nsig[e] + SEM_CAP - 1) // SEM_CAP)
            esems[e] = [stack.enter_context(nc.semaphore(f"s_{e}{i}")) for i in range(k)]
        dsems = {sk: stack.enter_context(nc.semaphore(f"d_{sk}")) for sk in self.dma_val}
        ops = self.ops

        def sem_of(e, signo):
            return esems[e][(signo - 1) // SEM_CAP], (signo - 1) % SEM_CAP + 1

        def run(e, engobj):
            for o in ops[e]:
                for dep in o.waits:
                    if dep[0] == "e":
                        s, v = sem_of(dep[1], ops[dep[1]][dep[2]].sig_no)
                        engobj.wait_ge(s, v)
                    else:
                        engobj.wait_ge(dsems[dep[1]], dep[2])
                if o.fn is None:
                    continue
                ins = o.fn(engobj)
                if o.dma is not None:
                    ins.then_inc(dsems[o.dma[0]], 16)
                elif o.signal:
                    s, _ = sem_of(e, o.sig_no)
                    ins.then_inc(s, 1)

        block = stack.enter_context(nc.Block())

        @block.tensor
        def _(eng):
            run("pe", eng)

        @block.scalar
        def _(eng):
            run("act", eng)

        @block.vector
        def _(eng):
            run("dve", eng)

        @block.gpsimd
        def _(eng):
            run("pool", eng)

        @block.sync
        def _(eng):
            run("sp", eng)
        return {e: len(ops[e]) for e in ENGS}, nsig


W_IN_OFF = dict(gq=0, gk=256, gv=512, gr=1024, glr=1536, aq=1568, ak=2592, av=2848,
                dqkv=3104, dz=4640, dab=5152)


def weight_plan():
    plan = []
    for l in range(2):
        for j in range(96):
            plan.append(("wm", ("mod", l, j), "w_mod", l, 0, 16, j * 128, 128))
    for l in range(2):
        a = f"w{l}"

        def wi(key, col0, ncols=128):
            plan.append((a, key, "w_in", l, 0, 16, col0, ncols))

        def wo(key, row0, nk, dc):
            plan.append((a, key, "w_out", l, row0, nk, dc * 128, 128))
        for h in range(8):
            wi(("aq", l, h), W_IN_OFF["aq"] + h * 128)
        for h in range(2):
            wi(("ak", l, h), W_IN_OFF["ak"] + h * 128)
        for h in range(2):
            wi(("av", l, h), W_IN_OFF["av"] + h * 128)
        for dc in range(16):
            wo(("wo_att", l, dc), 512, 8, dc)
        for p in range(2):
            wi(("gq", l, p), W_IN_OFF["gq"] + p * 128)
        for p in range(2):
            wi(("gk", l, p), W_IN_OFF["gk"] + p * 128)
        for h in range(4):
            wi(("gr", l, h), W_IN_OFF["gr"] + h * 128)
        wi(("glr", l), W_IN_OFF["glr"], 32)
        for h in range(4):
            wi(("gv", l, h), W_IN_OFF["gv"] + h * 128)
        for dc in range(16):
            wo(("wo_gla", l, dc), 0, 4, dc)
        for cc in range(12):
            wi(("dqkv", l, cc), W_IN_OFF["dqkv"] + cc * 128)
        wi(("dab", l), W_IN_OFF["dab"], 16)
        for h in range(4):
            wi(("dz", l, h), W_IN_OFF["dz"] + h * 128)
        for dc in range(16):
            wo(("wo_dn", l, dc), 1536, 4, dc)
        for g in range(4):
            for j in range(16):
                plan.append((a, ("ff1", l, g, j), "w_ff1", l, 0, 16, (g * 16 + j) * 128, 128))
            for dc in range(16):
                plan.append((a, ("ff2", l, g, dc), "w_ff2", l, g * 2048, 16, dc * 128, 128))
    offs = {}
    out = []
    for (a, key, src, l, row0, nk, col0, ncols) in plan:
        off = offs.get(a, 0)
        out.append((a, key, src, l, row0, nk, col0, ncols, off))
        offs[a] = off + 128 * nk * 128
    return out, offs


CF = {}
_o = 0
for _n, _w in [("ident", 128), ("cond", 16), ("bmod0", 96), ("bmod1", 96), ("n1g0", 16), ("n1g1", 16),
               ("n2g0", 16), ("n2g1", 16), ("glag0", 1), ("glag1", 1), ("qg0", 1), ("qg1", 1),
               ("kg0", 1), ("kg1", 1), ("dng0", 1), ("dng1", 1), ("conv0", 36), ("conv1", 36),
               ("carry", 1), ("cflag", 1), ("onesf", 128), ("identb", 128), ("Rm", 128)]:
    CF[_n] = (_o, _w)
    _o += _w
NCF = _o
C64 = {}
_o = 0
for _n, _w in [("TriL", 64), ("TriU", 64), ("ones", 128), ("TriCL", 64), ("TriCU", 64), ("TriS0", 64),
               ("TriS1", 64), ("mbS0", 64), ("mbS1", 64), ("mbIT0", 64), ("mbIT1", 64), ("mT0", 64),
               ("mT1", 64), ("alog0", 8), ("alog1", 8), ("dtb0", 8), ("dtb1", 8), ("negones", 64),
               ("nTriL", 64), ("nTriU", 64), ("ident", 64)]:
    C64[_n] = (_o, _w)
    _o += _w
NC64 = _o


def build_program():
    nc = bass.Bass("TRN2", target_bir_lowering=False)
    plan, woffs = weight_plan()
    S = Sched()
    dbg_outs = {}

    def din(name, shape, dt=F32):
        return nc.dram_tensor(name, list(shape), dt, kind="ExternalInput").ap()

    def dout(name, shape, dt=F32):
        return nc.dram_tensor(name, list(shape), dt, kind="ExternalOutput").ap()

    wdram = {a: din(a, [n]) for a, n in woffs.items()}
    x_in = din("x", [T, D])
    cf_in = din("cf", [128, NCF])
    c64_in = din("c64", [64, NC64])
    mz_in = din("mz", [64, 2 * 6 * 128])
    w2aug_in = din("w2aug", [2, 33, 512])
    cs_in = din("cossin", [128, 2, T])
    atab_in = din("atab", [128, 1280])
    btab_in = din("btab", [128, T])
    ckT_in = din("ckT", [2, 128, 2, 256])
    cv_in = din("cv", [2, 128, 2, 256])
    sg_in = din("sgla", [2, 64, 2, 512])
    sd_in = din("sdn", [2, 128, 2, 512])
    y_out = dout("y", [T, D])
    kT_out = dout("kTo", [2, 2, 2, 128, 512])
    v_out = dout("vo", [2, 2, 8, 128, 128])
    sg_out = dout("sgo", [2, 2, 4, 64, 512])
    sd_out = dout("sdo", [2, 2, 4, 128, 512])

    with ExitStack() as st:
        def sbt(name, shape, dt):
            return st.enter_context(nc.sbuf_tensor(name, list(shape), dt))

        xT = sbt("xT", [128, KC, T], F32)
        wring = sbt("wring", [128, NSLOT, 16, 128], BF16)
        cf = sbt("cf_sb", [128, NCF], F32)
        c64 = sbt("c64_sb", [64, NC64], F32)
        cb = sbt("cb", [128, 3, 128], BF16)
        c64b = sbt("c64b", [64, 128], BF16)
        mzb_t = sbt("mzb", [64, 2 * 6 * 128], BF16)
        mzb = mzb_t[:].rearrange("p (z m v j) -> p z m v j", z=2, m=6, v=2)
        mods = sbt("mods", [128, 2, 96], F32)
        geff = sbt("geff", [128, 2, 2, 16], F32)
        gsm = sbt("gsm", [128, 16], F32)
        s_bf = sbt("s_bf", [128, 16], BF16)
        ARW = (nc.sbuf_bytes_remaining - 2048) // 4
        big = sbt("big", [128, ARW], F32)
        ps_t = [st.enter_context(nc.psum_tensor(f"ps{i}", [128, 512], F32)) for i in range(8)]
        ps_r = [Res() for _ in range(8)]
        ps_i = [0]

        def nextps():
            i = ps_i[0] % 7
            ps_i[0] += 1
            return ps_t[i], ps_r[i]

        class Arena:
            def __init__(self, base_words):
                self.p = base_words

            def alloc(self, nelem, dt, pat=None, **kw):
                sz = 4 if dt == F32 else 2
                nw = (nelem * sz + 3) // 4
                assert self.p + nw <= ARW, ("arena overflow", self.p + nw, ARW)
                ap = big[:, self.p:self.p + nw]
                self.p += nw
                if dt != F32:
                    ap = ap.bitcast(dt)
                    if ap.shape[1] != nelem:
                        ap = ap[:, 0:nelem]
                if pat:
                    ap = ap.rearrange(pat, **kw)
                return ap
        HTW = KC * T // 2
        hT = big[:, 0:HTW].bitcast(BF16).rearrange("p (k t) -> p k t", k=KC)

        def MM(out, lhsT, rhs, start=True, stop=True, rd=(), wr=()):
            S.op("pe", lambda e: e.matmul(out, lhsT=lhsT, rhs=rhs, start=start, stop=stop), rd, wr)

        def TR(out, in_, ident, rd=(), wr=()):
            S.op("pe", lambda e: e.transpose(out, in_, ident), rd, wr)

        def ACT(out, in_, func, rd=(), wr=(), scale=1.0, bias=0.0):
            S.op("act", lambda e: e.activation(out=out, in_=in_, func=func, bias=bias, scale=scale), rd, wr)

        def TT(out, in0, in1, op, rd=(), wr=(), eng="dve"):
            S.op(eng, lambda e: e.tensor_tensor(out=out, in0=in0, in1=in1, op=op), rd, wr)

        def TS(out, in0, s1, s2, op0, op1=None, rd=(), wr=(), eng="dve"):
            if op1 is None:
                S.op(eng, lambda e: e.tensor_scalar(out=out, in0=in0, scalar1=s1, scalar2=None, op0=op0), rd, wr)
            else:
                S.op(eng, lambda e: e.tensor_scalar(out=out, in0=in0, scalar1=s1, scalar2=s2, op0=op0, op1=op1), rd, wr)

        def STT(out, in0, scalar, in1, op0, op1, rd=(), wr=(), eng="dve"):
            S.op(eng, lambda e: e.scalar_tensor_tensor(out=out, in0=in0, scalar=scalar, in1=in1, op0=op0, op1=op1), rd, wr)

        def CP(out, in_, rd=(), wr=(), eng="dve"):
            if eng == "act":
                S.op("act", lambda e: e.copy(out=out, in_=in_), rd, wr)
            else:
                S.op(eng, lambda e: e.tensor_copy(out=out, in_=in_), rd, wr)

        def DMA(q, out, in_, rd=(), wr=(), sem="init"):
            S.op(q, lambda e: e.dma_start(out=out, in_=in_), rd, wr, dma_sem=sem)

        def dump(name, ap, rd, dt=F32):
            if name not in KDBG.split(","):
                return
            o = dout("dbg_" + name, list(ap.shape), dt)
            dbg_outs[name] = True
            DMA("sp", o, ap, rd=rd, sem="dbg")

        def cfs(name, a=None, b=None):
            o, w = CF[name]
            return cf[:, o + (a or 0): o + (w if b is None else b)]

        def c6(name, a=None, b=None, rows=64):
            o, w = C64[name]
            return c64[0:rows, o + (a or 0): o + (w if b is None else b)]

        r_c = Res()
        epsb = sbt("epsb", [128, 4], F32)
        r_eps = Res()
        for _i, _v in enumerate((D * EPS, 128.0 * EPS, EPS, 1.0)):
            S.op("dve", (lambda t_, v_: (lambda e: e.memset(t_, v_)))(epsb[:, _i:_i + 1], float(_v)), [], [r_eps])
        EPSCOL = {float(D * EPS): 0, float(128.0 * EPS): 1, float(EPS): 2}

        def RSTD(out, in_, eps_tot, rd, wr):
            c_ = EPSCOL[float(eps_tot)]
            ACT(out, in_, AF.Ln, rd=list(rd) + [r_eps], wr=wr, bias=epsb[:, c_:c_ + 1])
            ACT(out, out, AF.Exp, rd=wr, wr=wr, scale=-0.5)
        wres = [Res() for _ in range(NSLOT)]
        wi_ = [0]

        plan_by_key = {e[1]: e for e in plan}
        plan_rec = []
        offs_rec = {}

        def next_w(key):
            i = wi_[0]
            wi_[0] += 1
            a, k2, src, l, row0, nk, col0, ncols, _off = plan_by_key[key]
            off = offs_rec.get(a, 0)
            offs_rec[a] = off + 128 * nk * 128
            plan_rec.append((a, key, src, l, row0, nk, col0, ncols, off))
            slot = i % NSLOT
            srcap = wdram[a][off: off + 128 * nk * 128].rearrange("(p k n) -> p k n", p=128, k=nk)
            DMA("pool", wring[:, slot, 0:nk, :], srcap, wr=[wres[slot]], sem=f"w{slot}")
            return wring[:, slot], wres[slot]

        DMA("sp", cf[:], cf_in, wr=[r_c])
        DMA("sp", c64[:], c64_in, wr=[r_c])
        r_cb = Res()
        CP(cb[:, 0, :], cfs("onesf"), rd=[r_c], wr=[r_cb])
        CP(cb[:, 1, :], cfs("identb"), rd=[r_c], wr=[r_cb])
        CP(cb[:, 2, :], cfs("Rm"), rd=[r_c], wr=[r_cb])
        CP(c64b[:, 0:64], c6("mT0"), rd=[r_c], wr=[r_cb])
        CP(c64b[:, 64:128], c6("mT1"), rd=[r_c], wr=[r_cb])
        DMA("pool", mzb_t[:], mz_in, wr=[r_cb], sem="mz")
        ones_bf = cb[:, 0, :]
        ident_bf = cb[:, 1, :]
        RmT_bf = cb[:, 2, :]
        ident_f = cfs("ident")

        xT_r = [[Res(), Res()] for _ in range(KC)]
        A = Arena(HTW)
        xin = [A.alloc(D, F32) for _ in range(2)]
        xin_r = [Res(), Res()]
        for tt in range(8):
            sl = tt % 2
            DMA("sp", xin[sl], x_in[tt * 128:(tt + 1) * 128, :], wr=[xin_r[sl]], sem=f"xin{sl}")
            for c4 in range(4):
                ps, pr = nextps()
                for j in range(4):
                    c = c4 * 4 + j
                    TR(ps[:, j * 128:(j + 1) * 128], xin[sl][:, c * 128:(c + 1) * 128], ident_f,
                       rd=[xin_r[sl], r_c], wr=[pr])
                half = tt // 4
                eng = "act" if c4 % 2 else "dve"
                CP(xT[:, c4 * 4:(c4 + 1) * 4, tt * 128:(tt + 1) * 128],
                   ps[:].rearrange("p (j t) -> p j t", j=4), rd=[pr],
                   wr=[xT_r[c4 * 4 + j][half] for j in range(4)], eng=eng)

        NL = 0 if KSTOP in ('x', 'mod') else 2
        NMOD = 0 if KSTOP == 'x' else 2
        r_s = Res()
        sg_t = A.alloc(16, F32)
        ACT(sg_t, cfs("cond"), AF.Silu, rd=[r_c], wr=[r_s])
        CP(s_bf[:], sg_t, rd=[r_s], wr=[r_s])
        mod_r = [[Res() for _ in range(6)] for _ in range(2)]
        geff_r = [[Res(), Res()] for _ in range(2)]
        psM, prM = ps_t[7], ps_r[7]
        mod_next = [0, 0]

        def mod_finalize(l, m):
            TT(mods[:, l, m * 16:(m + 1) * 16], psM[:, l * 96 + m * 16: l * 96 + (m + 1) * 16],
               cfs(f"bmod{l}", m * 16, (m + 1) * 16), ALU.add, rd=[prM, r_c], wr=[mod_r[l][m]])
            if m in (1, 4):
                ni = 0 if m == 1 else 1
                gn = f"n1g{l}" if m == 1 else f"n2g{l}"
                STT(geff[:, l, ni, :], mods[:, l, m * 16:(m + 1) * 16], 1.0, cfs(gn), ALU.add, ALU.mult,
                    rd=[mod_r[l][m], r_c], wr=[geff_r[l][ni]])
                TS(geff[:, l, ni, :], geff[:, l, ni, :], float(np.sqrt(D)), None, ALU.mult,
                   rd=[geff_r[l][ni]], wr=[geff_r[l][ni]])

        def pump_mod(l, n):
            if NMOD == 0:
                return
            for _ in range(n):
                j = mod_next[l]
                if j >= 96:
                    return
                mod_next[l] = j + 1
                wt, wr_ = next_w(("mod", l, j))
                col = l * 96 + j
                for kc in range(KC):
                    MM(psM[:, col:col + 1], wt[:, kc, :], s_bf[:, kc:kc + 1], start=(kc == 0), stop=(kc == KC - 1),
                       rd=[wr_, r_s], wr=[prM])
                if j % 16 == 15:
                    mod_finalize(l, j // 16)

        pump_mod(0, 48)
        if not LAZYMOD:
            pump_mod(0, 96)
            pump_mod(1, 96)

        def mod_gen(n, lag=5):
            for _ in range(n):
                l_ = 0 if mod_next[0] < 96 else 1
                j = mod_next[l_]
                if NMOD == 0 or j >= 96:
                    return
                mod_next[l_] = j + 1
                wt, wr_ = next_w(("mod", l_, j))
                for _k in range(lag):
                    yield
                col = l_ * 96 + j
                for kc in range(KC):
                    MM(psM[:, col:col + 1], wt[:, kc, :], s_bf[:, kc:kc + 1], start=(kc == 0), stop=(kc == KC - 1),
                       rd=[wr_, r_s], wr=[prM])
                if j % 16 == 15:
                    mod_finalize(l_, j // 16)
                yield
        r_g = Res()
        for l in range(2):
            CP(gsm[:, l:l + 1], cfs(f"qg{l}"), rd=[r_c], wr=[r_g])
            TS(gsm[:, 2 + l:3 + l], cfs(f"kg{l}"), float(np.sqrt(128.0)), None, ALU.mult, rd=[r_c], wr=[r_g])
            TS(gsm[:, 4 + l:5 + l], cfs(f"glag{l}"), float(np.sqrt(128.0)), None, ALU.mult, rd=[r_c], wr=[r_g])
            TS(gsm[:, 6 + l:7 + l], cfs(f"dng{l}"), float(np.sqrt(128.0)), None, ALU.mult, rd=[r_c], wr=[r_g])
        if KSTOP in ('mod', 'n1') or 'mods' in KDBG:
            pump_mod(0, 96)
        dump("mods", mods[:].rearrange("p l m -> p (l m)"), mod_r[0])

        hT_r = [Res(), Res()]

        def rmsnorm_to_hT(l, ni, A):
            sh = 0 if ni == 0 else 3
            sq = [A.alloc(512, BF16) for _ in range(2)]
            sq_r = [Res(), Res()]
            rstd = [A.alloc(512, F32) for _ in range(2)]
            rstd_r = [Res(), Res()]
            tmp = [A.alloc(512, F32) for _ in range(2)]
            tmp_r = [Res(), Res()]
            for half in range(2):
                hs = slice(half * 512, (half + 1) * 512)
                ps, pr = nextps()
                for c in range(KC):
                    k = c % 2
                    ACT(sq[k], xT[:, c, hs], AF.Square, rd=[xT_r[c][half]], wr=[sq_r[k]])
                    MM(ps[:], ones_bf, sq[k], start=(c == 0), stop=(c == KC - 1), rd=[sq_r[k], r_cb], wr=[pr])
                RSTD(rstd[half], ps[:], D * EPS, [pr], [rstd_r[half]])
                for c in range(KC):
                    k = c % 2
                    TT(tmp[k], xT[:, c, hs], rstd[half], ALU.mult, rd=[xT_r[c][half], rstd_r[half]], wr=[tmp_r[k]])
                    ACT(hT[:, c, hs], tmp[k], AF.Identity, rd=[tmp_r[k], geff_r[l][ni], mod_r[l][sh]], wr=[hT_r[half]],
                        scale=geff[:, l, ni, c:c + 1], bias=mods[:, l, sh * 16 + c: sh * 16 + c + 1])

        def proj_fm(wt, wr_, half, ps, pr, mcols=128):
            hs = slice(half * 512, (half + 1) * 512)
            for kc in range(KC):
                MM(ps[0:mcols, :], wt[:, kc, 0:mcols], hT[:, kc, hs], start=(kc == 0), stop=(kc == KC - 1),
                   rd=[wr_, hT_r[half]], wr=[pr])

        def wout_apply(l, key, nk, mixT, mix_r):
            for dc in range(16):
                wt, wr_ = next_w((key, l, dc))
                for half in range(2):
                    hs = slice(half * 512, (half + 1) * 512)
                    ps, pr = nextps()
                    for k in range(nk):
                        MM(ps[:], wt[:, k, :], mixT[:, k, hs], start=(k == 0), stop=(k == nk - 1),
                           rd=[wr_, mix_r[k][half]], wr=[pr])
                    STT(xT[:, dc, hs], ps[:], mods[:, l, 2 * 16 + dc: 2 * 16 + dc + 1], xT[:, dc, hs], ALU.mult, ALU.add,
                        rd=[pr, mod_r[l][2], xT_r[dc][half]], wr=[xT_r[dc][half]])

        def headnorm_fm(src_ap, src_r, A_slots, eps_tot):
            sq, sq_r, rstd, rstd_r = A_slots
            ACT(sq, src_ap, AF.Square, rd=src_r, wr=[sq_r])
            ps2, pr2 = nextps()
            MM(ps2[:], ones_bf, sq, rd=[sq_r, r_cb], wr=[pr2])
            RSTD(rstd, ps2[:], eps_tot, [pr2], [rstd_r])
            return rstd, rstd_r


        def run_il(gens):
            act_ = list(gens)
            while act_:
                for g_ in list(act_):
                    try:
                        next(g_)
                    except StopIteration:
                        act_.remove(g_)

        def mix_epilogue(A2, oacc, oacc_r, mixT, mix_r, gcol):
            NS = 3
            slots = [(A2.alloc(512, BF16), Res(), A2.alloc(512, F32), Res(), A2.alloc(512, F32), Res()) for _ in range(NS)]

            def it(h, half, k):
                hs = slice(half * 512, (half + 1) * 512)
                orr = oacc_r[half * 8:(half + 1) * 8]
                sq, sq_r, rstd, rstd_r, tmp, tmp_r = slots[k]
                ACT(sq, oacc[:, h, hs], AF.Square, rd=orr, wr=[sq_r])
                yield
                ps2, pr2 = nextps()
                MM(ps2[:], ones_bf, sq, rd=[sq_r, r_cb], wr=[pr2])
                yield
                ACT(rstd, ps2[:], AF.Ln, rd=[pr2, r_eps], wr=[rstd_r], bias=epsb[:, 1:2])
                yield
                ACT(rstd, rstd, AF.Exp, rd=[rstd_r], wr=[rstd_r], scale=-0.5)
                yield
                TT(tmp, oacc[:, h, hs], rstd, ALU.mult, rd=orr + [rstd_r], wr=[tmp_r])
                yield
                STT(mixT[:, h, hs], tmp, gcol, mixT[:, h, hs], ALU.mult, ALU.mult,
                    rd=[tmp_r, r_g, mix_r[h][half]], wr=[mix_r[h][half]])
            items = [(h, half) for h in range(4) for half in range(2)]
            for g0 in range(0, 8, NS):
                run_il([it(h, half, k) for k, (h, half) in enumerate(items[g0:g0 + NS])])

        def gla_phase(l):
            A = Arena(HTW)
            mixT = A.alloc(4 * T, BF16, "p (k t) -> p k t", k=4)
            mix_r = [[Res(), Res()] for _ in range(4)]
            qraw = A.alloc(4 * T, BF16, "p (h t) -> p h t", h=4)
            kraw = A.alloc(4 * T, BF16, "p (h t) -> p h t", h=4)
            qk_r = Res()
            vtok = A.alloc(16 * 512, BF16, "p (c n) -> p c n", c=16)
            vt_r = [Res() for _ in range(16)]
            oacc = A.alloc(4 * T, F32, "p (h t) -> p h t", h=4)
            oacc_r = [Res() for _ in range(16)]
            Sg = A.alloc(2 * 512, F32, "p (z n) -> p z n", z=2)
            Sg_r = [Res(), Res()]
            Sgb = A.alloc(2 * 512, BF16, "p (z n) -> p z n", z=2)
            Sgb_r = [Res(), Res()]
            lrT = A.alloc(T, F32)
            lr_r = Res()
            w2a = A.alloc(512, F32)
            r_w2 = Res()
            e1 = [A.alloc(256, F32) for _ in range(2)]
            sp = [A.alloc(256, F32) for _ in range(2)]
            eb = [A.alloc(256, F32) for _ in range(2)]
            enb = [A.alloc(256, F32) for _ in range(2)]
            qt = [A.alloc(256, BF16) for _ in range(2)]
            kt = [A.alloc(256, BF16) for _ in range(2)]
            ATb = [A.alloc(256, BF16) for _ in range(2)]
            ktok = [A.alloc(256, BF16) for _ in range(2)]
            sst = [A.alloc(512, F32) for _ in range(2)]
            sst_r = [Res(), Res()]
            rr = {n: [Res(), Res()] for n in ("e1", "sp", "eb", "enb", "qt", "kt", "AT", "ktok")}
            DMA("sp", Sg[0:64], sg_in[l], wr=Sg_r, sem="sgi")
            for z in range(2):
                CP(Sgb[0:64, z, :], Sg[0:64, z, :], rd=[Sg_r[z]], wr=[Sgb_r[z]], eng="act")
            DMA("sp", w2a[0:33, :], w2aug_in[l], wr=[r_w2], sem="sgi")
            S.op("dve", lambda e: e.memset(lrT[32:33, :], 1.0), [], [lr_r])
            ei = 0
            for kind, dst in (("gq", qraw), ("gk", kraw)):
                for p in range(2):
                    wt, wr_ = next_w((kind, l, p))
                    for hp in range(2):
                        h = 2 * p + hp
                        for half in range(2):
                            hs = slice(half * 512, (half + 1) * 512)
                            ps, pr = nextps()
                            for kc in range(KC):
                                MM(ps[0:64, :], wt[:, kc, hp * 64:(hp + 1) * 64], hT[:, kc, hs], start=(kc == 0),
                                   stop=(kc == KC - 1), rd=[wr_, hT_r[half]], wr=[pr])
                            CP(dst[0:64, h, hs], ps[0:64, :], rd=[pr], wr=[qk_r], eng=("act" if ei % 2 else "dve"))
                            ei += 1
            for h in range(4):
                wt, wr_ = next_w(("gr", l, h))
                for half in range(2):
                    hs = slice(half * 512, (half + 1) * 512)
                    ps, pr = nextps()
                    proj_fm(wt, wr_, half, ps, pr)
                    ACT(mixT[:, h, hs], ps[:], AF.Silu, rd=[pr], wr=[mix_r[h][half]])
            wt, wr_ = next_w(("glr", l))
            for half in range(2):
                hs = slice(half * 512, (half + 1) * 512)
                ps, pr = nextps()
                proj_fm(wt, wr_, half, ps, pr, mcols=32)
                CP(lrT[0:32, hs], ps[0:32, :], rd=[pr], wr=[lr_r])
            for h in range(4):
                wt, wr_ = next_w(("gv", l, h))
                for c in range(16):
                    ps, pr = nextps()
                    for kc in range(KC):
                        MM(ps[0:64, 0:128], hT[:, kc, c * 64:(c + 1) * 64], wt[:, kc, :], start=(kc == 0),
                           stop=(kc == KC - 1), rd=[wr_, hT_r[c // 8]], wr=[pr])
                    CP(vtok[0:64, c, h * 128:(h + 1) * 128], ps[0:64, 0:128], rd=[pr], wr=[vt_r[c]],
                       eng=("act" if c % 2 else "dve"))
            if KSTOP == 'gp':
                return
            owritten = [False] * 16
            sso = [0]
            h4 = lambda ap: ap.rearrange("p (h i) -> p h i", h=4)
            id64 = ident_bf[0:64, 0:64]

            def cut(n):
                return KGS != "" and n >= int(KGS)

            def step(z, c):
                cs_ = slice(c * 64, (c + 1) * 64)
                ps1, pr1 = nextps()
                MM(ps1[0:64, 0:256], lrT[0:33, cs_], w2a[0:33, z * 256:(z + 1) * 256], rd=[lr_r, r_w2], wr=[pr1])
                ACT(e1[z][0:64, :], ps1[0:64, 0:256], AF.Exp, rd=[pr1], wr=[rr["e1"][z]], scale=-1.0)
                ACT(sp[z][0:64, :], e1[z][0:64, :], AF.Ln, rd=[rr["e1"][z], r_eps], wr=[rr["sp"][z]], bias=epsb[0:64, 3:4])
                if cut(1):
                    return
                yield
                psb, prb = nextps()
                for h in range(4):
                    MM(psb[0:64, h * 64:(h + 1) * 64], sp[z][0:64, h * 64:(h + 1) * 64], c6(f"TriS{z}"),
                       rd=[rr["sp"][z], r_c], wr=[prb])
                ACT(eb[z][0:64, :], psb[0:64, 0:256], AF.Exp, rd=[prb], wr=[rr["eb"][z]])
                ACT(enb[z][0:64, :], psb[0:64, 0:256], AF.Exp, rd=[prb], wr=[rr["enb"][z]], scale=-1.0)
                if cut(2):
                    return
                yield
                STT(h4(qt[z][0:64, :]), h4(eb[z][0:64, :]), 0.125, qraw[0:64, :, cs_], ALU.mult, ALU.mult,
                    rd=[rr["eb"][z], qk_r], wr=[rr["qt"][z]])
                TT(h4(kt[z][0:64, :]), h4(enb[z][0:64, :]), kraw[0:64, :, cs_], ALU.mult,
                   rd=[rr["enb"][z], qk_r], wr=[rr["kt"][z]])
                if cut(3):
                    return
                yield
                psA, prA = nextps()
                for h in range(4):
                    hc = slice(h * 64, (h + 1) * 64)
                    MM(psA[0:64, hc], kt[z][0:64, hc], qt[z][0:64, hc], rd=[rr["kt"][z], rr["qt"][z]], wr=[prA])
                yield
                TT(h4(ATb[z][0:64, :]), h4(psA[0:64, 0:256]),
                   c64b[:, z * 64:(z + 1) * 64].unsqueeze(1).to_broadcast([64, 4, 64]), ALU.mult,
                   rd=[prA, r_cb], wr=[rr["AT"][z]])
                if cut(4):
                    return
                yield
                pst, prt = nextps()
                pstb = pst[:].bitcast(BF16)
                for h in range(4):
                    hc = slice(h * 64, (h + 1) * 64)
                    TR(pstb[0:64, hc], kt[z][0:64, hc], id64, rd=[rr["kt"][z], r_cb], wr=[prt])
                yield
                CP(ktok[z][0:64, :], pstb[0:64, 0:256], rd=[prt], wr=[rr["ktok"][z]], eng="act")
                if cut(5):
                    return
                if (z == 0 and c % 4 == 0 and c > 0) or (z == 1 and c % 4 == 3 and c < 15):
                    TS(Sg[0:64, z, :], Sg[0:64, z, :], cfs("carry")[0:64, :], None, ALU.mult, rd=[Sg_r[z], r_c], wr=[Sg_r[z]])
                    CP(Sgb[0:64, z, :], Sg[0:64, z, :], rd=[Sg_r[z]], wr=[Sgb_r[z]], eng="act")
                yield
                psO, prO = nextps()
                for h in range(4):
                    hc = slice(h * 64, (h + 1) * 64)
                    MM(psO[:, hc], vtok[0:64, c, h * 128:(h + 1) * 128], ATb[z][0:64, hc], start=True, stop=False,
                       rd=[vt_r[c], rr["AT"][z]], wr=[prO])
                    MM(psO[:, hc], Sgb[0:64, z, h * 128:(h + 1) * 128], qt[z][0:64, hc], start=False, stop=True,
                       rd=[Sgb_r[z], rr["qt"][z]], wr=[prO])
                yield
                ov = oacc[:, :, cs_]
                pv = h4(psO[:, 0:256])
                if not owritten[c]:
                    CP(ov, pv, rd=[prO], wr=[oacc_r[c]])
                    owritten[c] = True
                else:
                    TT(ov, pv, ov, ALU.add, rd=[prO, oacc_r[c]], wr=[oacc_r[c]])
                if cut(6):
                    return
                yield
                psS, prS = nextps()
                for h in range(4):
                    MM(psS[0:64, h * 128:(h + 1) * 128], ktok[z][0:64, h * 64:(h + 1) * 64],
                       vtok[0:64, c, h * 128:(h + 1) * 128], rd=[rr["ktok"][z], vt_r[c]], wr=[prS])
                yield
                TT(Sg[0:64, z, :], psS[0:64, :], Sg[0:64, z, :], ALU.add, rd=[prS, Sg_r[z]], wr=[Sg_r[z]])
                col = 63 if z == 0 else 0
                hd = lambda ap: ap.rearrange("p (h d) -> p h d", h=4)
                TT(hd(Sg[0:64, z, :]), hd(Sg[0:64, z, :]),
                   h4(eb[z][0:64, :])[:, :, col:col + 1].to_broadcast([64, 4, 128]), ALU.mult,
                   rd=[Sg_r[z], rr["eb"][z]], wr=[Sg_r[z]])
                CP(Sgb[0:64, z, :], Sg[0:64, z, :], rd=[Sg_r[z]], wr=[Sgb_r[z]], eng="act")
                if (z == 0 and c % 4 == 3) or (z == 1 and c % 4 == 0):
                    k = sso[0] % 2
                    sso[0] += 1
                    CP(sst[k][0:64, :], Sg[0:64, z, :], rd=[Sg_r[z]], wr=[sst_r[k]])
                    DMA("sp", sg_out[l, z, c // 4], sst[k][0:64, :], rd=[sst_r[k]], sem=f"sso{k}")

            for s in range(16):
                run_il([step(0, s), step(1, 15 - s)])
            S.barrier()
            if KGS != "":
                return
            A2 = Arena(HTW + 4 * T // 2)
            mix_epilogue(A2, oacc, oacc_r, mixT, mix_r, gsm[:, 4 + l:5 + l])
            if l == 0:
                dump("mgla", mixT[:, 0, :], [mix_r[0][0], mix_r[0][1]], BF16)
            wout_apply(l, "wo_gla", 4, mixT, mix_r)

        def dn_phase(l):
            A = Arena(HTW)
            mixT = A.alloc(4 * T, BF16, "p (k t) -> p k t", k=4)
            mix_r = [[Res(), Res()] for _ in range(4)]
            qkvT = A.alloc(8 * T, BF16, "p (c t) -> p c t", c=8)
            qkv_r = [Res() for _ in range(12)]
            oacc = A.alloc(4 * T, F32, "p (h t) -> p h t", h=4)
            oacc_r = [Res() for _ in range(16)]
            Sd = A.alloc(2 * 512, F32, "p (z n) -> p z n", z=2)
            Sd_r = [Res(), Res()]
            Sdb = A.alloc(2 * 512, BF16, "p (z n) -> p z n", z=2)
            Sdb_r = [Res(), Res()]
            gab = A.alloc(256, F32, "p (c n) -> p c n", c=16)
            gtmp = A.alloc(128, F32, "p (c n) -> p c n", c=16)
            gall = A.alloc(128, F32, "p (c n) -> p c n", c=16)
            beta = A.alloc(128, F32, "p (c n) -> p c n", c=16)
            nbeta = A.alloc(128, F32, "p (c n) -> p c n", c=16)
            nexpA = A.alloc(8, F32)
            nb = A.alloc(36, F32)
            r_gb = Res()
            vT = A.alloc(4 * T, BF16, "p (c t) -> p c t", c=4)
            mark = A.p
            raw = [A.alloc(1026, F32) for _ in range(2)]
            raw_r = [Res(), Res()]
            yv = [A.alloc(1024, F32) for _ in range(2)]
            yv_r = [Res(), Res()]
            sq_ = A.alloc(512, BF16)
            rstd_ = A.alloc(512, F32)
            nsl = (sq_, Res(), rstd_, Res())
            DMA("sp", Sd, sd_in[l], wr=Sd_r, sem="sgi")
            for z in range(2):
                CP(Sdb[:, z, :], Sd[:, z, :], rd=[Sd_r[z]], wr=[Sdb_r[z]], eng="act")
            TS(nb, cfs(f"conv{l}"), cfs("cflag"), -1.0, ALU.mult, ALU.mult, rd=[r_c], wr=[r_gb])
            for k in range(2):
                S.op("dve", (lambda t_: (lambda e: e.memset(t_, 0.0)))(raw[k][:, 0:1]), [], [raw_r[k]])
                S.op("dve", (lambda t_: (lambda e: e.memset(t_, 0.0)))(raw[k][:, 1025:1026]), [], [raw_r[k]])
            cw = cfs(f"conv{l}")
            def dn_proj(cc, wt, wr_):
                k = cc % 2
                for half in range(2):
                    ps, pr = nextps()
                    proj_fm(wt, wr_, half, ps, pr)
                    CP(raw[k][:, 1 + half * 512: 1 + (half + 1) * 512], ps[:], rd=[pr], wr=[raw_r[k]],
                       eng=("act" if half else "dve"))

            def dn_tail(cc):
                k = cc % 2
                TS(yv[k], raw[k][:, 1:1025], cw[:, cc * 3 + 1: cc * 3 + 2], None, ALU.mult, rd=[raw_r[k], r_c], wr=[yv_r[k]])
                STT(yv[k], raw[k][:, 0:1024], cw[:, cc * 3: cc * 3 + 1], yv[k], ALU.mult, ALU.add,
                    rd=[raw_r[k], r_c, yv_r[k]], wr=[yv_r[k]])
                STT(yv[k], raw[k][:, 2:1026], cw[:, cc * 3 + 2: cc * 3 + 3], yv[k], ALU.mult, ALU.add,
                    rd=[raw_r[k], r_c, yv_r[k]], wr=[yv_r[k]])
                STT(yv[k][:, 256:1024:256], raw[k][:, 256:1024:256], nb[:, cc * 3: cc * 3 + 1], yv[k][:, 256:1024:256],
                    ALU.mult, ALU.add, rd=[raw_r[k], r_gb, yv_r[k]], wr=[yv_r[k]])
                STT(yv[k][:, 255:1023:256], raw[k][:, 257:1025:256], nb[:, cc * 3 + 2: cc * 3 + 3],
                    yv[k][:, 255:1023:256], ALU.mult, ALU.add, rd=[raw_r[k], r_gb, yv_r[k]], wr=[yv_r[k]])
                if cc >= 8:
                    ACT(vT[:, cc - 8, :], yv[k], AF.Silu, rd=[yv_r[k]], wr=[qkv_r[cc]])
                else:
                    ACT(yv[k], yv[k], AF.Silu, rd=[yv_r[k]], wr=[yv_r[k]])
                    for half in range(2):
                        hs = slice(half * 512, (half + 1) * 512)
                        rstd, rstd_r = headnorm_fm(yv[k][:, hs], [yv_r[k]], nsl, EPS)
                        if cc < 4:
                            STT(qkvT[:, cc, hs], yv[k][:, hs], float(128.0 ** -0.5), rstd, ALU.mult, ALU.mult,
                                rd=[yv_r[k], rstd_r], wr=[qkv_r[cc]])
                        else:
                            TT(qkvT[:, cc, hs], yv[k][:, hs], rstd, ALU.mult, rd=[yv_r[k], rstd_r], wr=[qkv_r[cc]])

            w_ = next_w(("dqkv", l, 0))
            dn_proj(0, *w_)
            for cc in range(12):
                if cc + 1 < 12:
                    w_ = next_w(("dqkv", l, cc + 1))
                    dn_proj(cc + 1, *w_)
                dn_tail(cc)
            wt, wr_ = next_w(("dab", l))
            psg, prg = nextps()
            for c in range(16):
                for kc in range(KC):
                    MM(psg[0:64, c * 16:(c + 1) * 16], hT[:, kc, c * 64:(c + 1) * 64], wt[:, kc, 0:16],
                       start=(kc == 0), stop=(kc == KC - 1), rd=[wr_, hT_r[c // 8]], wr=[prg])
            CP(gab[0:64], psg[0:64, 0:256].rearrange("p (c n) -> p c n", c=16), rd=[prg], wr=[r_gb])
            bc8 = lambda ap: ap.unsqueeze(1).to_broadcast([64, 16, 8])
            TT(gtmp[0:64], gab[0:64, :, 0:8], bc8(c6(f"dtb{l}")), ALU.add, rd=[r_gb, r_c], wr=[r_gb])
            ACT(gtmp[0:64], gtmp[0:64], AF.Exp, rd=[r_gb], wr=[r_gb])
            ACT(gtmp[0:64].rearrange("p c n -> p (c n)"), gtmp[0:64].rearrange("p c n -> p (c n)"), AF.Ln, rd=[r_gb, r_eps], wr=[r_gb], bias=epsb[0:64, 3:4])
            ACT(nexpA[0:64, :], c6(f"alog{l}"), AF.Exp, rd=[r_c], wr=[r_gb])
            TS(nexpA[0:64, :], nexpA[0:64, :], -1.0, None, ALU.mult, rd=[r_gb], wr=[r_gb])
            TT(gall[0:64], gtmp[0:64], bc8(nexpA[0:64, :]), ALU.mult, rd=[r_gb], wr=[r_gb])
            ACT(beta[0:64], gab[0:64, :, 8:16], AF.Exp, rd=[r_gb], wr=[r_gb], scale=-1.0)
            TS(beta[0:64], beta[0:64], 1.0, None, ALU.add, rd=[r_gb], wr=[r_gb])
            S.op("dve", lambda e: e.reciprocal(out=beta[0:64], in_=beta[0:64]), [r_gb], [r_gb])
            TS(nbeta[0:64], beta[0:64], -1.0, None, ALU.mult, rd=[r_gb], wr=[r_gb])
            for h in range(4):
                wt, wr_ = next_w(("dz", l, h))
                for half in range(2):
                    hs = slice(half * 512, (half + 1) * 512)
                    ps, pr = nextps()
                    proj_fm(wt, wr_, half, ps, pr)
                    ACT(mixT[:, h, hs], ps[:], AF.Silu, rd=[pr], wr=[mix_r[h][half]])
            if l == 0:
                dump("dq", qkvT[:, 0, :], [qkv_r[0]], BF16)
                dump("dk", qkvT[:, 4, :], [qkv_r[4]], BF16)
                dump("dv", vT[:, 0, :], [qkv_r[8]], BF16)
                dump("dg", gall[0:64].rearrange("p c n -> p (c n)"), [r_gb])
                dump("dbeta", beta[0:64].rearrange("p c n -> p (c n)"), [r_gb])
            S.barrier()
            AH = Arena(0)
            AR = Arena(mark)
            NSL = 3
            NLN = 2

            def alloc2(n, dt):
                nw = (n * (4 if dt == F32 else 2) + 3) // 4
                if AH.p + nw <= HTW:
                    return AH.alloc(n, dt)
                return AR.alloc(n, dt)
            WK = {}
            for z in range(2):
                for ln in range(NLN):
                    for name, n, dt in (("gTri", 256, F32), ("gbc", 256, F32), ("eGbc", 256, F32), ("AA", 512, BF16),
                                        ("Xs", 512, BF16)):
                        WK[(name, z, ln)] = (alloc2(n, dt), Res())
                    for al, sname in (("d1", "gTri"), ("a1", "gTri"), ("d2", "gbc"), ("tmpm", "Xs")):
                        WK[(al, z, ln)] = WK[(sname, z, ln)]
            HO = {}
            for z in range(2):
                for sl in range(NSL):
                    for name, n, dt in (("TTT", 512, BF16), ("attT", 256, BF16), ("qte", 256, BF16), ("kd", 512, BF16),
                                        ("bv", 512, BF16), ("sm", 12, F32), ("eGl", 4, F32)):
                        HO[(name, z, sl)] = (alloc2(n, dt), Res())
                    S.op("dve", (lambda t_: (lambda e: e.memset(t_, 0.0)))(HO[("attT", z, sl)][0][64:128, :]), [],
                         [HO[("attT", z, sl)][1]])
            SW = {}
            for z in range(2):
                for name, n, dt in (("tmpr", 512, F32), ("r", 512, BF16), ("vn", 512, BF16)):
                    SW[(name, z)] = (alloc2(n, dt), Res())
                S.op("dve", (lambda t_: (lambda e: e.memset(t_, 0.0)))(SW[("vn", z)][0][64:128, :]), [], [SW[("vn", z)][1]])
            _s1 = alloc2(512, F32)
            _r1 = Res()
            print("DN ring end", AH.p, HTW, AR.p, ARW)
            owritten = [False] * 16
            h4 = lambda ap: ap.rearrange("p (h i) -> p h i", h=4)
            hd = lambda ap: ap.rearrange("p (h d) -> p h d", h=4)
            v4 = lambda ap: ap.rearrange("p (v h j) -> p v h j", v=2, h=4)
            id64 = ident_bf[0:64, 0:64]

            ps_busy = [False] * 7

            def getps():
                while True:
                    for k_ in range(7):
                        i_ = (ps_i[0] + k_) % 7
                        if not ps_busy[i_]:
                            ps_busy[i_] = True
                            ps_i[0] = i_ + 1
                            return ps_t[i_], ps_r[i_], i_
                    yield

            def relps(i_):
                ps_busy[i_] = False

            def pre(z, c, ln, sl):
                cs_ = slice(c * 64, (c + 1) * 64)
                g = gall[0:64, c, z * 4:(z + 1) * 4]
                bz = beta[0:64, c, z * 4:(z + 1) * 4]
                nbz = nbeta[0:64, c, z * 4:(z + 1) * 4]
                Tri = c6("TriL") if z == 0 else c6("TriU")
                nTri = c6("nTriL") if z == 0 else c6("nTriU")
                TriC = c6("TriCL") if z == 0 else c6("TriCU")
                w = lambda n: WK[(n, z, ln)][0]
                wq = lambda n: WK[(n, z, ln)][1]
                o = lambda n: HO[(n, z, sl)][0]
                oq = lambda n: HO[(n, z, sl)][1]
                mz = lambda m: mzb[:, z, m].unsqueeze(2).to_broadcast([64, 2, 4, 64])
                gb64 = g.unsqueeze(2).to_broadcast([64, 4, 64])
                TT(h4(w("gTri")[0:64, :]), Tri.unsqueeze(1).to_broadcast([64, 4, 64]), gb64, ALU.mult,
                   rd=[r_c, r_gb], wr=[wq("gTri")])
                CP(h4(w("gbc")[0:64, :]), gb64, rd=[r_gb], wr=[wq("gbc")])
                yield
                psD, prD, bD = yield from getps()
                MM(psD[0:64, 0:256], Tri, w("gbc")[0:64, :], start=True, stop=False, rd=[r_c, wq("gbc")], wr=[prD])
                MM(psD[0:64, 0:256], c6("negones"), w("gTri")[0:64, :], start=False, stop=True, rd=[r_c, wq("gTri")], wr=[prD])
                MM(psD[0:64, 256:512], c6("ones", 0, 64), w("gTri")[0:64, :], start=True, stop=False, rd=[r_c, wq("gTri")], wr=[prD])
                MM(psD[0:64, 256:512], nTri, w("gbc")[0:64, :], start=False, stop=True, rd=[r_c, wq("gbc")], wr=[prD])
                psX, prX, bX = yield from getps()
                MM(psX[:, 0:256], c6("ones"), w("gTri")[0:64, :], rd=[r_c, wq("gTri")], wr=[prX])
                MM(psX[0:64, 256:260], Tri, g, rd=[r_c, r_gb], wr=[prX])
                MM(psX[:, 260:264], c6("ones"), g, rd=[r_c, r_gb], wr=[prX])
                MM(psX[0:64, 264:268], TriC, g, rd=[r_c, r_gb], wr=[prX])
                yield
                TT(h4(w("d1")[0:64, :]), h4(psD[0:64, 0:256]), c6(f"mbS{z}").unsqueeze(1).to_broadcast([64, 4, 64]),
                   ALU.add, rd=[prD, r_c], wr=[wq("d1")])
                TT(h4(w("d2")[0:64, :]), h4(psD[0:64, 256:512]), c6(f"mbIT{z}").unsqueeze(1).to_broadcast([64, 4, 64]),
                   ALU.add, rd=[prD, r_c], wr=[wq("d2")])
                ACT(w("eGbc"), psX[:, 0:256], AF.Exp, rd=[prX], wr=[wq("eGbc")])
                ACT(o("sm")[0:64, 0:4], psX[0:64, 256:260], AF.Exp, rd=[prX], wr=[oq("sm")])
                ACT(o("eGl"), psX[:, 260:264], AF.Exp, rd=[prX], wr=[oq("eGl")])
                ACT(o("sm")[0:64, 4:8], psX[0:64, 264:268], AF.Exp, rd=[prX], wr=[oq("sm")])
                relps(bD)
                relps(bX)
                yield
                ACT(w("d1")[0:64, :], w("d1")[0:64, :], AF.Exp, rd=[wq("d1")], wr=[wq("d1")])
                ACT(w("d2")[0:64, :], w("d2")[0:64, :], AF.Exp, rd=[wq("d2")], wr=[wq("d2")])
                TT(o("sm")[0:64, 8:12], o("sm")[0:64, 0:4], nbz, ALU.mult, rd=[oq("sm"), r_gb], wr=[oq("sm")])
                psK, prK, bK = yield from getps()
                for h in range(4):
                    MM(psK[0:64, h * 64:(h + 1) * 64], qkvT[:, 4 + h, cs_], qkvT[:, 4 + h, cs_], rd=[qkv_r[4 + h]], wr=[prK])
                for h in range(4):
                    MM(psK[0:64, 256 + h * 64:256 + (h + 1) * 64], qkvT[:, 4 + h, cs_], qkvT[:, h, cs_],
                       rd=[qkv_r[4 + h], qkv_r[h]], wr=[prK])
                yield
                TT(w("a1")[0:64, :], psK[0:64, 0:256], w("d1")[0:64, :], ALU.mult, rd=[prK, wq("d1")], wr=[wq("a1")])
                TT(h4(w("AA")[0:64, 0:256]), h4(w("a1")[0:64, :]), bz.unsqueeze(2).to_broadcast([64, 4, 64]), ALU.mult,
                   rd=[wq("a1"), r_gb], wr=[wq("AA")])
                TT(o("attT")[0:64, :], psK[0:64, 256:512], w("d2")[0:64, :], ALU.mult, rd=[prK, wq("d2")], wr=[oq("attT")])
                relps(bK)
                yield
                pst, prt, bt = yield from getps()
                pst_v = pst[:].bitcast(BF16)
                for h in range(4):
                    TR(pst_v[0:64, h * 64:(h + 1) * 64], w("AA")[0:64, h * 64:(h + 1) * 64], id64,
                       rd=[wq("AA"), r_cb], wr=[prt])
                psk, prk, bk = yield from getps()
                pskb = psk[:].bitcast(BF16)
                for h in range(4):
                    TR(pskb[0:64, h * 128:(h + 1) * 128], qkvT[:, 4 + h, cs_], ident_bf, rd=[qkv_r[4 + h], r_cb], wr=[prk])
                prv = prk
                for h in range(4):
                    TR(pskb[0:64, 512 + h * 128:512 + (h + 1) * 128], vT[:, h, cs_], ident_bf, rd=[qkv_r[8 + h], r_cb], wr=[prv])
                yield
                CP(w("AA")[0:64, 256:512], pst_v[0:64, 0:256], rd=[prt], wr=[wq("AA")], eng="act")
                TT(hd(o("kd")[0:64, :]), hd(pskb[0:64, 0:512]), o("sm")[0:64, 4:8].unsqueeze(2).to_broadcast([64, 4, 128]),
                   ALU.mult, rd=[prk, oq("sm")], wr=[oq("kd")])
                TT(hd(o("bv")[0:64, :]), hd(pskb[0:64, 512:1024]), bz.unsqueeze(2).to_broadcast([64, 4, 128]), ALU.mult,
                   rd=[prv, r_gb], wr=[oq("bv")])
                TT(h4(o("qte")), qkvT[:, 0:4, cs_], h4(w("eGbc")), ALU.mult, rd=qkv_r[0:4] + [wq("eGbc")], wr=[oq("qte")])
                relps(bt)
                relps(bk)
                yield
                TT(v4(w("tmpm")[0:64, :]), v4(w("AA")[0:64, :]), mz(0), ALU.mult, rd=[wq("AA"), r_cb], wr=[wq("tmpm")])
                TT(v4(o("TTT")[0:64, :]), c6("ident").unsqueeze(1).unsqueeze(1).to_broadcast([64, 2, 4, 64]),
                   v4(w("tmpm")[0:64, :]), ALU.subtract, rd=[wq("tmpm"), r_c], wr=[oq("TTT")])
                for m in range(1, 6):
                    yield
                    psXX, prXX, bXX = yield from getps()
                    for h in range(4):
                        hc = slice(h * 64, (h + 1) * 64)
                        hc2 = slice(256 + h * 64, 256 + (h + 1) * 64)
                        MM(psXX[0:64, hc], w("AA")[0:64, hc2], o("TTT")[0:64, hc], rd=[wq("AA"), oq("TTT")], wr=[prXX])
                        MM(psXX[0:64, hc2], w("AA")[0:64, hc], o("TTT")[0:64, hc2], rd=[wq("AA"), oq("TTT")], wr=[prXX])
                    yield
                    CP(w("Xs")[0:64, :], psXX[0:64, :], rd=[prXX], wr=[wq("Xs")], eng="act")
                    relps(bXX)
                    yield
                    psY, prY, bY = yield from getps()
                    for h in range(4):
                        hc = slice(h * 64, (h + 1) * 64)
                        hc2 = slice(256 + h * 64, 256 + (h + 1) * 64)
                        MM(psY[0:64, hc], o("TTT")[0:64, hc2], w("Xs")[0:64, hc], rd=[wq("Xs"), oq("TTT")], wr=[prY])
                        MM(psY[0:64, hc2], o("TTT")[0:64, hc], w("Xs")[0:64, hc2], rd=[wq("Xs"), oq("TTT")], wr=[prY])
                    yield
                    TT(v4(w("tmpm")[0:64, :]), v4(psY[0:64, :]), mz(m), ALU.mult, rd=[prY, r_cb], wr=[wq("tmpm")])
                    relps(bY)
                    yield
                    TT(o("TTT")[0:64, :], o("TTT")[0:64, :], w("tmpm")[0:64, :], ALU.subtract,
                       rd=[oq("TTT"), wq("tmpm")], wr=[oq("TTT")])

            sso = [0]

            def ser(z, c, sl):
                cs_ = slice(c * 64, (c + 1) * 64)
                o = lambda n: HO[(n, z, sl)][0]
                oq = lambda n: HO[(n, z, sl)][1]
                s_ = lambda n: SW[(n, z)][0]
                sq = lambda n: SW[(n, z)][1]
                if (z == 0 and c % 4 == 0 and c > 0) or (z == 1 and c % 4 == 3 and c < 15):
                    TS(Sd[:, z, :], Sd[:, z, :], cfs("carry"), None, ALU.mult, rd=[Sd_r[z], r_c], wr=[Sd_r[z]])
                    CP(Sdb[:, z, :], Sd[:, z, :], rd=[Sd_r[z]], wr=[Sdb_r[z]], eng="act")
                    yield
                psKS, prKS, bKS = yield from getps()
                for h in range(4):
                    MM(psKS[0:64, h * 128:(h + 1) * 128], qkvT[:, 4 + h, cs_], Sdb[:, z, h * 128:(h + 1) * 128],
                       rd=[qkv_r[4 + h], Sdb_r[z]], wr=[prKS])
                yield
                TT(hd(s_("tmpr")[0:64, :]), hd(psKS[0:64, :]), o("sm")[0:64, 8:12].unsqueeze(2).to_broadcast([64, 4, 128]),
                   ALU.mult, rd=[prKS, oq("sm")], wr=[sq("tmpr")])
                relps(bKS)
                yield
                TT(s_("r")[0:64, :], s_("tmpr")[0:64, :], o("bv")[0:64, :], ALU.add, rd=[sq("tmpr"), oq("bv")], wr=[sq("r")])
                yield
                psV, prV, bV = yield from getps()
                for h in range(4):
                    MM(psV[0:64, h * 128:(h + 1) * 128], o("TTT")[0:64, 256 + h * 64:256 + (h + 1) * 64],
                       s_("r")[0:64, h * 128:(h + 1) * 128], rd=[oq("TTT"), sq("r")], wr=[prV])
                yield
                CP(s_("vn")[0:64, :], psV[0:64, :], rd=[prV], wr=[sq("vn")], eng="act")
                relps(bV)
                yield
                psO, prO, bO = yield from getps()
                for h in range(4):
                    MM(psO[:, h * 64:(h + 1) * 64], Sdb[:, z, h * 128:(h + 1) * 128], o("qte")[:, h * 64:(h + 1) * 64],
                       start=True, stop=False, rd=[Sdb_r[z], oq("qte")], wr=[prO])
                    MM(psO[:, h * 64:(h + 1) * 64], s_("vn")[:, h * 128:(h + 1) * 128], o("attT")[:, h * 64:(h + 1) * 64],
                       start=False, stop=True, rd=[sq("vn"), oq("attT")], wr=[prO])
                psS, prS, bS = yield from getps()
                for h in range(4):
                    MM(psS[:, h * 128:(h + 1) * 128], o("kd")[0:64, h * 128:(h + 1) * 128], s_("vn")[0:64, h * 128:(h + 1) * 128],
                       rd=[oq("kd"), sq("vn")], wr=[prS])
                yield
                TT(hd(Sd[:, z, :]), hd(Sd[:, z, :]), o("eGl").unsqueeze(2).to_broadcast([128, 4, 128]), ALU.mult,
                   rd=[Sd_r[z], oq("eGl")], wr=[Sd_r[z]])
                TT(Sd[:, z, :], psS[:], Sd[:, z, :], ALU.add, rd=[prS, Sd_r[z]], wr=[Sd_r[z]])
                relps(bS)
                yield
                CP(Sdb[:, z, :], Sd[:, z, :], rd=[Sd_r[z]], wr=[Sdb_r[z]], eng="act")
                ov = oacc[:, :, cs_]
                pv = h4(psO[:, 0:256])
                if not owritten[c]:
                    CP(ov, pv, rd=[prO], wr=[oacc_r[c]])
                    owritten[c] = True
                else:
                    TT(ov, pv, ov, ALU.add, rd=[prO, oacc_r[c]], wr=[oacc_r[c]])
                relps(bO)
                if (z == 0 and c % 4 == 3) or (z == 1 and c % 4 == 0):
                    CP(_s1, Sd[:, z, :], rd=[Sd_r[z]], wr=[_r1])
                    k = sso[0] % 2
                    sso[0] += 1
                    DMA("sp", sd_out[l, z, c // 4], _s1, rd=[_r1], sem=f"sso{k}")

            order = [list(range(16)), list(range(15, -1, -1))]
            pre_i = [0, 0]
            pre_done = [0, 0]
            ser_i = [0, 0]
            ser_done = [0, 0]
            active = []
            modg = [mod_gen(48, lag=4), mod_gen(48, lag=4)] if l == 0 else []
            while ser_done[0] < 16 or ser_done[1] < 16:
                for z in range(2):
                    n_pre = sum(1 for a_ in active if a_[1] == "pre" and a_[2] == z)
                    while (n_pre < NLN and pre_i[z] < 16 and pre_i[z] - ser_done[z] < NSL):
                        i_ = pre_i[z]
                        lanes_busy = [a_[4] for a_ in active if a_[1] == "pre" and a_[2] == z]
                        ln = 0 if 0 not in lanes_busy else 1
                        active.append([pre(z, order[z][i_], ln, i_ % NSL), "pre", z, i_, ln])
                        pre_i[z] += 1
                        n_pre += 1
                    if not any(a_[1] == "ser" and a_[2] == z for a_ in active) and ser_i[z] < 16 and pre_done[z] > ser_i[z]:
                        i_ = ser_i[z]
                        active.append([ser(z, order[z][i_], i_ % NSL), "ser", z, i_, -1])
                        ser_i[z] += 1
                for a_ in list(active):
                    try:
                        next(a_[0])
                    except StopIteration:
                        active.remove(a_)
                        if a_[1] == "pre":
                            pre_done[a_[2]] += 1
                        else:
                            ser_done[a_[2]] += 1
                for g_ in list(modg):
                    try:
                        next(g_)
                    except StopIteration:
                        modg.remove(g_)
            for g_ in modg:
                for _ in g_:
                    pass
            S.barrier()
            A2 = Arena(mark)
            mix_epilogue(A2, oacc, oacc_r, mixT, mix_r, gsm[:, 6 + l:7 + l])
            if l == 0:
                dump("mdn", mixT[:, 0, :], [mix_r[0][0], mix_r[0][1]], BF16)
            wout_apply(l, "wo_dn", 4, mixT, mix_r)

        for l in range(NL):
            S.barrier()
            A = Arena(HTW)
            rmsnorm_to_hT(l, 0, A)
            if l == 0:
                dump("h1", hT[:, 0, :], hT_r, BF16)
            S.barrier()
            if KSTOP == 'n1':
                break
            A = Arena(HTW)
            mixT = A.alloc(8 * T, BF16, "p (k t) -> p k t", k=8)
            mix_r = [[Res(), Res()] for _ in range(8)]
            qT = A.alloc(8 * T, BF16, "p (h t) -> p h t", h=8)
            qT_r = [[Res(), Res()] for _ in range(8)]
            kTa = A.alloc(2 * 1280, BF16, "p (h t) -> p h t", h=2)
            kTa_r = [Res(), Res()]
            vall = A.alloc(10 * 256, BF16, "p (t c) -> p t c", t=10)
            vall_r = Res()
            cs = A.alloc(2 * T, F32, "p (a t) -> p a t", a=2)
            r_cs = Res()
            atab = A.alloc(1280, BF16)
            btab = A.alloc(T, BF16)
            r_ab = Res()
            sq_ = A.alloc(512, BF16)
            rstd_ = A.alloc(512, F32)
            nslots = (sq_, Res(), rstd_, Res())
            qn = [A.alloc(512, F32) for _ in range(2)]
            qn_r = [Res(), Res()]
            qnb = [A.alloc(512, BF16) for _ in range(2)]
            qnb_r = [Res(), Res()]
            t1 = [A.alloc(512, F32) for _ in range(2)]
            t1_r = [Res(), Res()]
            t2 = [A.alloc(512, F32) for _ in range(2)]
            t2_r = [Res(), Res()]
            vst = [A.alloc(256, F32) for _ in range(2)]
            vst_r = [Res(), Res()]
            PT = [A.alloc(512, BF16) for _ in range(3)]
            PT_r = [Res() for _ in range(3)]
            rec = [A.alloc(512, F32) for _ in range(2)]
            rec_r = [Res(), Res()]
            if 'c' not in KSK:
                DMA("sp", cs, cs_in, wr=[r_cs], sem="cs")
            if 'a' not in KSK:
                DMA("pool", atab, atab_in, wr=[r_ab], sem="ab")
                DMA("pool", btab, btab_in, wr=[r_ab], sem="ab")
                DMA("pool", kTa[:, :, 1024:1280], ckT_in[l], wr=kTa_r, sem="ab")
                DMA("pool", vall[:, 8:10, :], cv_in[l], wr=[vall_r], sem="ab")
            sq2_ = A.alloc(512, BF16)
            rstd2_ = A.alloc(512, F32)
            nsl2 = [nslots, (sq2_, Res(), rstd2_, Res())]

            qk_cnt = [0]

            def qk_proj(kind, h, wt, wr_, par):
                for half in range(2):
                    b_ = par * 2 + half
                    proj_fm(wt, wr_, half, ps_t[b_], ps_r[b_])

            def tmp_ps():
                b_ = 4 + qk_cnt[0] % 3
                qk_cnt[0] += 1
                return ps_t[b_], ps_r[b_]

            def qk_iter(kind, h, half, par):
                hs = slice(half * 512, (half + 1) * 512)
                k = half
                ps, pr = ps_t[par * 2 + half], ps_r[par * 2 + half]
                sq, sq_r, rstd, rstd_r = nsl2[k]
                ACT(sq, ps[:], AF.Square, rd=[pr], wr=[sq_r])
                yield
                ps2, pr2 = tmp_ps()
                MM(ps2[:], ones_bf, sq, rd=[sq_r, r_cb], wr=[pr2])
                yield
                ACT(rstd, ps2[:], AF.Ln, rd=[pr2, r_eps], wr=[rstd_r], bias=epsb[:, 1:2])
                yield
                ACT(rstd, rstd, AF.Exp, rd=[rstd_r], wr=[rstd_r], scale=-0.5)
                yield
                gcol = gsm[:, l:l + 1] if kind == "aq" else gsm[:, 2 + l:3 + l]
                STT(qn[k], ps[:], gcol, rstd, ALU.mult, ALU.mult, rd=[pr, rstd_r, r_g], wr=[qn_r[k]])
                yield
                if kind == "ak" and 'k' not in KSK:
                    DMA("sp", kT_out[l, h, half], qn[k], rd=[qn_r[k]], sem=f"ko{k}")
                CP(qnb[k], qn[k], rd=[qn_r[k]], wr=[qnb_r[k]], eng="act")
                TT(t1[k], qn[k], cs[:, 0, hs], ALU.mult, rd=[qn_r[k], r_cs], wr=[t1_r[k]])
                yield
                ps3, pr3 = tmp_ps()
                MM(ps3[:], RmT_bf, qnb[k], rd=[qnb_r[k], r_cb], wr=[pr3])
                yield
                TT(t2[k], ps3[:], cs[:, 1, hs], ALU.mult, rd=[pr3, r_cs], wr=[t2_r[k]])
                yield
                if kind == "aq":
                    TT(qT[:, h, hs], t1[k], t2[k], ALU.add, rd=[t1_r[k], t2_r[k]], wr=[qT_r[h][half]])
                else:
                    TT(kTa[:, h, hs], t1[k], t2[k], ALU.add, rd=[t1_r[k], t2_r[k]], wr=[kTa_r[h]])

            heads = [("aq", h) for h in range(8)] + [("ak", h) for h in range(2)]
            wt0 = next_w((heads[0][0], l, heads[0][1]))
            qk_proj(heads[0][0], heads[0][1], wt0[0], wt0[1], 0)
            for hi, (kind, h) in enumerate(heads):
                if hi + 1 < len(heads):
                    kn, hn = heads[hi + 1]
                    wtn = next_w((kn, l, hn))
                    qk_proj(kn, hn, wtn[0], wtn[1], (hi + 1) % 2)
                run_il([qk_iter(kind, h, 0, hi % 2), qk_iter(kind, h, 1, hi % 2)])
            for h in range(2):
                wt, wr_ = next_w(("av", l, h))
                for tt in range(0 if 'v' in KSK else 8):
                    ps, pr = nextps()
                    for kc in range(KC):
                        MM(ps[:, 0:128], hT[:, kc, tt * 128:(tt + 1) * 128], wt[:, kc, :], start=(kc == 0),
                           stop=(kc == KC - 1), rd=[wr_, hT_r[tt // 4]], wr=[pr])
                    k = tt % 2
                    CP(vst[k][:, 0:128], ps[:, 0:128], rd=[pr], wr=[vst_r[k]], eng="act")
                    CP(vall[:, tt, h * 128:(h + 1) * 128], vst[k][:, 0:128], rd=[vst_r[k]], wr=[vall_r])
                    DMA("sp", v_out[l, h, tt], vst[k][:, 0:128], rd=[vst_r[k]], sem=f"vo{k}")
            if l == 0:
                dump("qT0", qT[:, 0, :], [qT_r[0][0], qT_r[0][1]], BF16)
                dump("kT0", kTa[:, 0, :], kTa_r, BF16)
            if KSTOP == 'ap':
                break
            items = [(hq, half, kt) for hq in range(8) for half in range(2) for kt in range(10)]
            sbank = [0, 1, 2]
            obank = [(3, 4), (5, 6)]

            def s_stage(i):
                hq, half, kt = items[i]
                kv = hq // 4
                hs = slice(half * 512, (half + 1) * 512)
                psS, prS = ps_t[sbank[i % 3]], ps_r[sbank[i % 3]]
                MM(psS[:], kTa[:, kv, kt * 128:(kt + 1) * 128], qT[:, hq, hs], start=True, stop=False,
                   rd=[kTa_r[kv], qT_r[hq][half]], wr=[prS])
                MM(psS[:], atab[:, kt * 128:(kt + 1) * 128], btab[:, hs], start=False, stop=True,
                   rd=[r_ab], wr=[prS])
                ACT(PT[i % 3], psS[:], AF.Exp, rd=[prS], wr=[PT_r[i % 3]])

            def pv_stage(i):
                hq, half, kt = items[i]
                kv = hq // 4
                hs = slice(half * 512, (half + 1) * 512)
                g_ = (i // 10) % 2
                psO, prO = ps_t[obank[g_][0]], ps_r[obank[g_][0]]
                psD, prD = ps_t[obank[g_][1]], ps_r[obank[g_][1]]
                p = i % 3
                MM(psO[:], vall[:, kt, kv * 128:(kv + 1) * 128], PT[p], start=(kt == 0), stop=(kt == 9),
                   rd=[vall_r, PT_r[p]], wr=[prO])
                MM(psD[:], ones_bf, PT[p], start=(kt == 0), stop=(kt == 9), rd=[PT_r[p], r_cb], wr=[prD])
                if kt == 9:
                    S.op("dve", (lambda o_, i_: (lambda e: e.reciprocal(out=o_, in_=i_)))(rec[g_], psD[:]),
                         [prD], [rec_r[g_]])
                    TT(mixT[:, hq, hs], psO[:], rec[g_], ALU.mult, rd=[prO, rec_r[g_]], wr=[mix_r[hq][half]])

            for i in range(len(items) + 1):
                if i < len(items):
                    s_stage(i)
                if i >= 1:
                    pv_stage(i - 1)
                pass
            if l == 0:
                dump("matt", mixT[:, 0, :], [mix_r[0][0], mix_r[0][1]], BF16)
            wout_apply(l, "wo_att", 8, mixT, mix_r)
            S.barrier()
            if KSTOP == "att":
                break
            gla_phase(l)
            S.barrier()
            if KSTOP in ("gla", "gp"):
                break
            dn_phase(l)
            S.barrier()
            if KSTOP == "dn":
                break
            A = Arena(HTW)
            if l == 0:
                pump_mod(0, 96)
            rmsnorm_to_hT(l, 1, A)
            S.barrier()
            A = Arena(HTW)
            actT = A.alloc(16 * T, BF16, "p (k t) -> p k t", k=16)
            act_r = [[Res(), Res()] for _ in range(16)]
            rl = [A.alloc(512, F32) for _ in range(2)]
            rl_r = [Res(), Res()]
            ri = 0
            for g in range(4):
                for j in range(16):
                    if l == 0:
                        pump_mod(1, 1)
                    wt, wr_ = next_w(("ff1", l, g, j))
                    for half in range(2):
                        hs = slice(half * 512, (half + 1) * 512)
                        ps, pr = nextps()
                        proj_fm(wt, wr_, half, ps, pr)
                        k = ri % 2
                        ri += 1
                        ACT(rl[k], ps[:], AF.Relu, rd=[pr], wr=[rl_r[k]])
                        TT(actT[:, j, hs], rl[k], rl[k], ALU.mult, rd=[rl_r[k]], wr=[act_r[j][half]])
                for dc in range(16):
                    if l == 0:
                        pump_mod(1, 1)
                    wt, wr_ = next_w(("ff2", l, g, dc))
                    for half in range(2):
                        hs = slice(half * 512, (half + 1) * 512)
                        ps, pr = nextps()
                        for j in range(16):
                            MM(ps[:], wt[:, j, :], actT[:, j, hs], start=(j == 0), stop=(j == 15),
                               rd=[wr_, act_r[j][half]], wr=[pr])
                        STT(xT[:, dc, hs], ps[:], mods[:, l, 5 * 16 + dc: 5 * 16 + dc + 1], xT[:, dc, hs],
                            ALU.mult, ALU.add, rd=[pr, mod_r[l][5], xT_r[dc][half]], wr=[xT_r[dc][half]])

        S.barrier()
        A = Arena(0)
        yst = [A.alloc(D, F32) for _ in range(2)]
        yst_r = [Res(), Res()]
        for tt in range(8):
            sl = tt % 2
            half = tt // 4
            for c4 in range(4):
                ps, pr = nextps()
                for j in range(4):
                    c = c4 * 4 + j
                    TR(ps[:, j * 128:(j + 1) * 128], xT[:, c, tt * 128:(tt + 1) * 128], ident_f,
                       rd=[xT_r[c][half], r_c], wr=[pr])
                eng = "act" if c4 % 2 else "dve"
                CP(yst[sl][:, c4 * 512:(c4 + 1) * 512], ps[:], rd=[pr], wr=[yst_r[sl]], eng=eng)
            DMA("sp", y_out[tt * 128:(tt + 1) * 128, :], yst[sl], rd=[yst_r[sl]], sem=f"yo{sl}")
        S.wait_all_dma("sp")
        stats = S.emit(nc, st)
        print("ops", stats, "weights", wi_[0], "/", len(plan))
    assert len(plan_rec) == len(plan) and offs_rec == woffs, (len(plan_rec), len(plan))
    return nc, list(dbg_outs), plan_rec


_CACHE = {}
LAST_DBG = {}


def _host_consts():
    t = np.arange(64)
    f32 = np.float32
    TriL = (t[:, None] <= t[None, :]).astype(f32)
    TriU = (t[:, None] >= t[None, :]).astype(f32)
    TriCL = (t[:, None] > t[None, :]).astype(f32)
    TriCU = (t[:, None] < t[None, :]).astype(f32)
    NEG = f32(-1e4)
    i_, j_ = t[:, None], t[None, :]
    d = dict(TriL=TriL, TriU=TriU, ones=np.ones((64, 128), f32), TriCL=TriCL, TriCU=TriCU,
             TriS0=-TriL / 16.0, TriS1=-TriU / 16.0,
             mbS0=np.where(j_ < i_, 0, NEG).astype(f32), mbS1=np.where(j_ > i_, 0, NEG).astype(f32),
             mbIT0=np.where(i_ <= j_, 0, NEG).astype(f32), mbIT1=np.where(i_ >= j_, 0, NEG).astype(f32),
             mT0=(i_ <= j_).astype(f32), mT1=(i_ >= j_).astype(f32),
             negones=-np.ones((64, 64), f32), nTriL=-TriL, nTriU=-TriU, ident=np.eye(64, dtype=f32))
    return d


def _mz_const():
    ii = np.arange(64)
    ML = np.zeros((64, 6, 64), np.float32)
    for m in range(6):
        b = 1 << m
        same = (ii[:, None] // (2 * b)) == (ii[None, :] // (2 * b))
        ML[:, m, :] = (same & ((ii[:, None] % (2 * b)) >= b) & ((ii[None, :] % (2 * b)) < b)).astype(np.float32)
    MU = ML.transpose(2, 1, 0)
    mz = np.zeros((64, 2, 6, 2, 64), np.float32)
    mz[:, 0, :, 0], mz[:, 0, :, 1] = ML, MU
    mz[:, 1, :, 0], mz[:, 1, :, 1] = MU, ML
    return np.ascontiguousarray(mz.reshape(64, 2 * 6 * 128))


def kernel(**inp):
    f32 = np.float32
    inp = {k: np.asarray(v) for k, v in inp.items()}
    if "nc" not in _CACHE:
        _CACHE["nc"] = build_program()
    nc, dbg_names, plan = _CACHE["nc"]
    _p0, woffs = weight_plan()
    warr = {a: np.zeros(n, f32) for a, n in woffs.items()}
    for (a, key, src, l, row0, nk, col0, ncols, off) in plan:
        W = inp[src][l]
        blk = W[row0:row0 + nk * 128, col0:col0 + ncols].reshape(nk, 128, ncols).transpose(1, 0, 2)
        dst = warr[a][off: off + 128 * nk * 128].reshape(128, nk, 128)
        dst[:, :, :ncols] = blk
    hc = _host_consts()
    tpos = np.arange(T)
    inv = (np.float32(10000.0) ** (-np.arange(0, 64, 2, dtype=f32) / np.float32(64))).astype(f32)
    ang = np.zeros((128, T), f32)
    for dd in range(128):
        pos = (tpos // 64) if dd < 64 else (tpos % 64)
        ang[dd] = pos.astype(f32) * inv[dd % 32]
    cos_s, sin_s = np.cos(ang).astype(f32), np.sin(ang).astype(f32)
    RmT = np.zeros((128, 128), f32)
    for dd in range(128):
        if dd % 64 < 32:
            RmT[dd + 32, dd] = -1.0
        else:
            RmT[dd - 32, dd] = 1.0
    w2aug = np.zeros((2, 33, 512), f32)
    for l in range(2):
        for z in range(2):
            w2aug[l, z * 16:(z + 1) * 16, z * 256:(z + 1) * 256] = inp["gla_w2"][l, z]
            w2aug[l, 32, z * 256:(z + 1) * 256] = inp["gla_b"][l, z]
    in_maps = []
    for core in range(8):
        ctx = core < 4
        m = dict(warr)
        if ctx:
            m["x"] = np.ascontiguousarray(inp["x_prompt"][4 * core:4 * core + 4].reshape(T, D))
            cond = inp["c_ctx"]
        else:
            b = core - 4
            m["x"] = np.ascontiguousarray(inp["x_sample"][b])
            cond = inp["c"][b]
        cfa = np.zeros((128, NCF), f32)

        def put(name, arr):
            o, w = CF[name]
            cfa[:, o:o + w] = arr
        put("ident", np.eye(128, dtype=f32))
        put("cond", cond.reshape(16, 128).T)
        for l in range(2):
            put(f"bmod{l}", inp["b_mod"][l].reshape(96, 128).T)
            put(f"n1g{l}", inp["norm1_g"][l].reshape(16, 128).T)
            put(f"n2g{l}", inp["norm2_g"][l].reshape(16, 128).T)
            put(f"glag{l}", inp["gla_norm_g"][l][:, None])
            put(f"qg{l}", inp["q_norm_g"][l][:, None])
            put(f"kg{l}", inp["k_norm_g"][l][:, None])
            put(f"dng{l}", inp["dn_norm_g"][l][:, None])
            put(f"conv{l}", inp["dn_conv"][l].reshape(3, 12, 128).transpose(2, 1, 0).reshape(128, 36))
        put("carry", 0.0 if ctx else 1.0)
        put("cflag", 1.0 if ctx else 0.0)
        put("onesf", 1.0)
        put("identb", np.eye(128, dtype=f32))
        put("Rm", RmT)
        m["cf"] = cfa
        c64a = np.zeros((64, NC64), f32)
        for name, arr in hc.items():
            o, w = C64[name]
            c64a[:, o:o + w] = arr
        for l in range(2):
            o, w = C64[f"alog{l}"]
            c64a[:, o:o + w] = inp["dn_a_log"][l].reshape(8)[None, :]
            o, w = C64[f"dtb{l}"]
            c64a[:, o:o + w] = inp["dn_dt_bias"][l].reshape(8)[None, :]
        m["c64"] = c64a
        m["mz"] = _mz_const()
        m["w2aug"] = w2aug
        cs = np.zeros((128, 2, T), f32)
        at = np.zeros((128, 1280), f32)
        bt = np.zeros((128, T), f32)
        at[4, 1024:] = 1.0
        if ctx:
            cs[:, 0, :] = 1.0
            for s in range(4):
                at[s, s * 256:(s + 1) * 256] = 1.0
                bt[s, :] = -30000.0
                bt[s, s * 256:(s + 1) * 256] = 0.0
            bt[4, :] = -30000.0
            m["ckT"] = np.zeros((2, 128, 2, 256), f32)
            m["cv"] = np.zeros((2, 128, 2, 256), f32)
            m["sgla"] = np.zeros((2, 64, 2, 512), f32)
            m["sdn"] = np.zeros((2, 128, 2, 512), f32)
        else:
            b = core - 4
            cs[:, 0, :] = cos_s
            cs[:, 1, :] = sin_s
            at[0, :1024] = 1.0
            m["ckT"] = np.ascontiguousarray(inp["cache_k"][b].transpose(0, 3, 2, 1))
            m["cv"] = np.ascontiguousarray(
                inp["cache_v"][b].reshape(2, 2, 128, 256).transpose(0, 2, 1, 3))
            m["sgla"] = np.ascontiguousarray(inp["state_gla"][b].transpose(0, 3, 1, 2, 4)).reshape(2, 64, 2, 512)
            m["sdn"] = np.ascontiguousarray(inp["state_dn"][b].transpose(0, 3, 1, 2, 4)).reshape(2, 128, 2, 512)
        m["cossin"] = cs
        m["atab"] = at
        m["btab"] = bt
        in_maps.append(m)
    kc_ = os.environ.get("KCORES", "")
    if kc_:
        sel = [int(s) for s in kc_.split(",")]
        res = run_bass_kernel_spmd(nc, [in_maps[c] for c in sel], core_ids=list(range(len(sel))))
        R = [res.results[sel.index(c)] if c in sel else res.results[0] for c in range(8)]
    else:
        res = run_bass_kernel_spmd(nc, in_maps, core_ids=list(range(8)))
        R = res.results
    for n in dbg_names:
        LAST_DBG[n] = [np.asarray(R[c]["dbg_" + n]) for c in range(8)]
    y_prompt = np.stack([np.asarray(R[c]["y"]) for c in range(4)]).reshape(16, 256, D).astype(f32)
    y_sample = np.stack([np.asarray(R[c]["y"]) for c in range(4, 8)]).astype(f32)
    kT = np.stack([np.asarray(R[c]["kTo"]) for c in range(4)])
    kT = kT.transpose(0, 1, 2, 4, 3, 5).reshape(4, 2, 2, 128, 4, 256)
    nk = kT.transpose(0, 4, 1, 5, 2, 3).reshape(16, 2, 256, 2, 128)
    vo = np.stack([np.asarray(R[c]["vo"]) for c in range(4)])
    vo = vo.reshape(4, 2, 2, 4, 256, 128)
    nv = vo.transpose(0, 3, 1, 4, 2, 5).reshape(16, 2, 256, 2, 128)
    sgo = np.stack([np.asarray(R[c]["sgo"]) for c in range(4)])
    nsg = sgo.reshape(4, 2, 2, 4, 64, 4, 128).transpose(0, 3, 1, 2, 5, 4, 6).reshape(16, 2, 2, 4, 64, 128)
    sdo = np.stack([np.asarray(R[c]["sdo"]) for c in range(4)])
    nsd = sdo.reshape(4, 2, 2, 4, 128, 4, 128).transpose(0, 3, 1, 2, 5, 4, 6).reshape(16, 2, 2, 4, 128, 128)
    return (y_prompt, y_sample, np.ascontiguousarray(nk, dtype=f32), np.ascontiguousarray(nv, dtype=f32),
            np.ascontiguousarray(nsg, dtype=f32), np.ascontiguousarray(nsd, dtype=f32))
```

```python
import os
import numpy as np
from contextlib import ExitStack
import concourse.bass as bass
import concourse.mybir as mybir
from concourse.bass_utils import run_bass_kernel_spmd

F32 = mybir.dt.float32
BF16 = mybir.dt.bfloat16
AF = mybir.ActivationFunctionType
ALU = mybir.AluOpType

T = 1024
D = 2048
KC = 16
NCH = 16
EPS = 1e-6
NSLOT = 3
ENGS = ("pe", "act", "dve", "pool", "sp")
SEM_CAP = 30000
KDBG = os.environ.get("KDBG", "")
KSTOP = os.environ.get("KSTOP", "")
KSK = os.environ.get("KSK", "")
KGS = os.environ.get("KGS", "")
LAZYMOD = True
CHD = F32 if os.environ.get("KCHAIN", "bf16") == "f32" else BF16


class Res:
    __slots__ = ("w", "rs")

    def __init__(self):
        self.w = None
        self.rs = []


class Op:
    __slots__ = ("eng", "idx", "fn", "waits", "signal", "clock", "dma", "sig_no")

    def __init__(self, eng, idx, fn):
        self.eng, self.idx, self.fn = eng, idx, fn
        self.waits = []
        self.signal = False
        self.clock = None
        self.dma = None
        self.sig_no = None


class Sched:
    def __init__(self):
        self.ops = {e: [] for e in ENGS}
        self.clock = {e: {} for e in ENGS}
        self.dma_val = {}
        self.dma_clock = {}

    def _need(self, eng, dep, same_ok):
        ck = self.clock[eng]
        if dep[0] == "e":
            if dep[1] == eng and same_ok:
                return False
            return ck.get(dep[1], -1) < dep[2]
        return ck.get(("d", dep[1]), 0) < dep[2]

    def _merge(self, eng, dep):
        ck = self.clock[eng]
        if dep[0] == "e":
            op2 = self.ops[dep[1]][dep[2]]
            op2.signal = True
            src = op2.clock
            if ck.get(dep[1], -1) < dep[2]:
                ck[dep[1]] = dep[2]
        else:
            src = self.dma_clock[(dep[1], dep[2])]
            ck[("d", dep[1])] = dep[2]
        for k, v in src.items():
            if ck.get(k, -1) < v:
                ck[k] = v

    def op(self, eng, fn, reads=(), writes=(), dma_sem=None):
        lst = self.ops[eng]
        o = Op(eng, len(lst), fn)
        deps = []
        for r in reads:
            if r.w is not None:
                deps.append((r.w, False))
        for w in writes:
            if w.w is not None:
                deps.append((w.w, True))
            for d in w.rs:
                deps.append((d, True))
        agg = {}
        for dep, same_ok in deps:
            if dep[0] == "e":
                if dep[1] == eng and (same_ok or eng == "pe"):
                    continue
                k = ("e", dep[1])
            else:
                k = ("d", dep[1])
            if k not in agg or agg[k][2] < dep[2]:
                agg[k] = dep
        for dep in agg.values():
            if self._need(eng, dep, False):
                o.waits.append(dep)
                self._merge(eng, dep)
        o.clock = dict(self.clock[eng])
        lst.append(o)
        if dma_sem is not None:
            v = self.dma_val.get(dma_sem, 0) + 16
            self.dma_val[dma_sem] = v
            o.dma = (dma_sem, v)
            self.dma_clock[(dma_sem, v)] = dict(o.clock)
            me = ("d", dma_sem, v)
        else:
            me = ("e", eng, o.idx)
        for r in reads:
            r.rs.append(me)
        for w in writes:
            w.w = me
            w.rs = []
        return o

    def barrier(self, engs=("pe", "act", "dve", "sp", "pool")):
        last = {}
        for e in engs:
            for o in reversed(self.ops[e]):
                if o.fn is not None and o.dma is None:
                    last[e] = o.idx
                    break
        dmas = dict(self.dma_val)
        for e in engs:
            lst = self.ops[e]
            o = Op(e, len(lst), None)
            for e2, i2 in last.items():
                dep = ("e", e2, i2)
                if self._need(e, dep, False):
                    o.waits.append(dep)
                    self._merge(e, dep)
            for sk, v in dmas.items():
                if sk.startswith("w"):
                    continue
                dep = ("d", sk, v)
                if self._need(e, dep, False):
                    o.waits.append(dep)
                    self._merge(e, dep)
            o.clock = dict(self.clock[e])
            lst.append(o)

    def wait_all_dma(self, eng="sp"):
        lst = self.ops[eng]
        o = Op(eng, len(lst), None)
        for sk, v in self.dma_val.items():
            o.waits.append(("d", sk, v))
        o.clock = dict(self.clock[eng])
        lst.append(o)

    def emit(self, nc, stack):
        nsig = {}
        for e in ENGS:
            n = 0
            for o in self.ops[e]:
                if o.signal:
                    n += 1
                    o.sig_no = n
            nsig[e] = n
        esems = {}
        for e in ENGS:
            k = max(1, (nsig[e] + SEM_CAP - 1) // SEM_CAP)
            esems[e] = [stack.enter_context(nc.semaphore(f"s_{e}{i}")) for i in range(k)]
        dsems = {sk: stack.enter_context(nc.semaphore(f"d_{sk}")) for sk in self.dma_val}
        ops = self.ops

        def sem_of(e, signo):
            return esems[e][(signo - 1) // SEM_CAP], (signo - 1) % SEM_CAP + 1

        def run(e, engobj):
            for o in ops[e]:
                for dep in o.waits:
                    if dep[0] == "e":
                        s, v = sem_of(dep[1], ops[dep[1]][dep[2]].sig_no)
                        engobj.wait_ge(s, v)
                    else:
                        engobj.wait_ge(dsems[dep[1]], dep[2])
                if o.fn is None:
                    continue
                ins = o.fn(engobj)
                if o.dma is not None:
                    ins.then_inc(dsems[o.dma[0]], 16)
                elif o.signal:
                    s, _ = sem_of(e, o.sig_no)
                    ins.then_inc(s, 1)

        block = stack.enter_context(nc.Block())

        @block.tensor
        def _(eng):
            run("pe", eng)

        @block.scalar
        def _(eng):
            run("act", eng)

        @block.vector
        def _(eng):
            run("dve", eng)

        @block.gpsimd
        def _(eng):
            run("pool", eng)

        @block.sync
        def _(eng):
            run("sp", eng)
        return {e: len(ops[e]) for e in ENGS}, nsig


W_IN_OFF = dict(gq=0, gk=256, gv=512, gr=1024, glr=1536, aq=1568, ak=2592, av=2848,
                dqkv=3104, dz=4640, dab=5152)


def weight_plan():
    plan = []
    for l in range(2):
        for j in range(96):
            plan.append(("wm", ("mod", l, j), "w_mod", l, 0, 16, j * 128, 128))
    for l in range(2):
        a = f"w{l}"

        def wi(key, col0, ncols=128):
            plan.append((a, key, "w_in", l, 0, 16, col0, ncols))

        def wo(key, row0, nk, dc):
            plan.append((a, key, "w_out", l, row0, nk, dc * 128, 128))
        for h in range(8):
            wi(("aq", l, h), W_IN_OFF["aq"] + h * 128)
        for h in range(2):
            wi(("ak", l, h), W_IN_OFF["ak"] + h * 128)
        for h in range(2):
            wi(("av", l, h), W_IN_OFF["av"] + h * 128)
        for dc in range(16):
            wo(("wo_att", l, dc), 512, 8, dc)
        for p in range(2):
            wi(("gq", l, p), W_IN_OFF["gq"] + p * 128)
        for p in range(2):
            wi(("gk", l, p), W_IN_OFF["gk"] + p * 128)
        for h in range(4):
            wi(("gr", l, h), W_IN_OFF["gr"] + h * 128)
        wi(("glr", l), W_IN_OFF["glr"], 32)
        for h in range(4):
            wi(("gv", l, h), W_IN_OFF["gv"] + h * 128)
        for dc in range(16):
            wo(("wo_gla", l, dc), 0, 4, dc)
        for cc in range(12):
            wi(("dqkv", l, cc), W_IN_OFF["dqkv"] + cc * 128)
        wi(("dab", l), W_IN_OFF["dab"], 16)
        for h in range(4):
            wi(("dz", l, h), W_IN_OFF["dz"] + h * 128)
        for dc in range(16):
            wo(("wo_dn", l, dc), 1536, 4, dc)
        for g in range(4):
            for j in range(16):
                plan.append((a, ("ff1", l, g, j), "w_ff1", l, 0, 16, (g * 16 + j) * 128, 128))
            for dc in range(16):
                plan.append((a, ("ff2", l, g, dc), "w_ff2", l, g * 2048, 16, dc * 128, 128))
    offs = {}
    out = []
    for (a, key, src, l, row0, nk, col0, ncols) in plan:
        off = offs.get(a, 0)
        out.append((a, key, src, l, row0, nk, col0, ncols, off))
        offs[a] = off + 128 * nk * 128
    return out, offs


CF = {}
_o = 0
for _n, _w in [("ident", 128), ("cond", 16), ("bmod0", 96), ("bmod1", 96), ("n1g0", 16), ("n1g1", 16),
               ("n2g0", 16), ("n2g1", 16), ("glag0", 1), ("glag1", 1), ("qg0", 1), ("qg1", 1),
               ("kg0", 1), ("kg1", 1), ("dng0", 1), ("dng1", 1), ("conv0", 36), ("conv1", 36),
               ("carry", 1), ("cflag", 1), ("onesf", 128), ("identb", 128), ("Rm", 128)]:
    CF[_n] = (_o, _w)
    _o += _w
NCF = _o
C64 = {}
_o = 0
for _n, _w in [("TriL", 64), ("TriU", 64), ("ones", 128), ("TriCL", 64), ("TriCU", 64), ("TriS0", 64),
               ("TriS1", 64), ("mbS0", 64), ("mbS1", 64), ("mbIT0", 64), ("mbIT1", 64), ("mT0", 64),
               ("mT1", 64), ("alog0", 8), ("alog1", 8), ("dtb0", 8), ("dtb1", 8), ("negones", 64),
               ("nTriL", 64), ("nTriU", 64), ("ident", 64)]:
    C64[_n] = (_o, _w)
    _o += _w
NC64 = _o


def build_program():
    nc = bass.Bass("TRN2", target_bir_lowering=False)
    plan, woffs = weight_plan()
    S = Sched()
    dbg_outs = {}

    def din(name, shape, dt=F32):
        return nc.dram_tensor(name, list(shape), dt, kind="ExternalInput").ap()

    def dout(name, shape, dt=F32):
        return nc.dram_tensor(name, list(shape), dt, kind="ExternalOutput").ap()

    wdram = {a: din(a, [n]) for a, n in woffs.items()}
    x_in = din("x", [T, D])
    cf_in = din("cf", [128, NCF])
    c64_in = din("c64", [64, NC64])
    mz_in = din("mz", [64, 2 * 6 * 128])
    w2aug_in = din("w2aug", [2, 33, 512])
    cs_in = din("cossin", [128, 2, T])
    atab_in = din("atab", [128, 1280])
    btab_in = din("btab", [128, T])
    ckT_in = din("ckT", [2, 128, 2, 256])
    cv_in = din("cv", [2, 128, 2, 256])
    sg_in = din("sgla", [2, 64, 2, 512])
    sd_in = din("sdn", [2, 128, 2, 512])
    y_out = dout("y", [T, D])
    kT_out = dout("kTo", [2, 2, 2, 128, 512])
    v_out = dout("vo", [2, 2, 8, 128, 128])
    sg_out = dout("sgo", [2, 2, 4, 64, 512])
    sd_out = dout("sdo", [2, 2, 4, 128, 512])

    with ExitStack() as st:
        def sbt(name, shape, dt):
            return st.enter_context(nc.sbuf_tensor(name, list(shape), dt))

        xT = sbt("xT", [128, KC, T], F32)
        wring = sbt("wring", [128, NSLOT, 16, 128], BF16)
        cf = sbt("cf_sb", [128, NCF], F32)
        c64 = sbt("c64_sb", [64, NC64], F32)
        cb = sbt("cb", [128, 3, 128], BF16)
        c64b = sbt("c64b", [64, 128], BF16)
        mzb_t = sbt("mzb", [64, 2 * 6 * 128], BF16)
        mzb = mzb_t[:].rearrange("p (z m v j) -> p z m v j", z=2, m=6, v=2)
        mods = sbt("mods", [128, 2, 96], F32)
        geff = sbt("geff", [128, 2, 2, 16], F32)
        gsm = sbt("gsm", [128, 16], F32)
        s_bf = sbt("s_bf", [128, 16], BF16)
        ARW = (nc.sbuf_bytes_remaining - 2048) // 4
        big = sbt("big", [128, ARW], F32)
        ps_t = [st.enter_context(nc.psum_tensor(f"ps{i}", [128, 512], F32)) for i in range(8)]
        ps_r = [Res() for _ in range(8)]
        ps_i = [0]

        def nextps():
            i = ps_i[0] % 7
            ps_i[0] += 1
            return ps_t[i], ps_r[i]

        class Arena:
            def __init__(self, base_words):
                self.p = base_words

            def alloc(self, nelem, dt, pat=None, **kw):
                sz = 4 if dt == F32 else 2
                nw = (nelem * sz + 3) // 4
                assert self.p + nw <= ARW, ("arena overflow", self.p + nw, ARW)
                ap = big[:, self.p:self.p + nw]
                self.p += nw
                if dt != F32:
                    ap = ap.bitcast(dt)
                    if ap.shape[1] != nelem:
                        ap = ap[:, 0:nelem]
                if pat:
                    ap = ap.rearrange(pat, **kw)
                return ap
        HTW = KC * T // 2
        hT = big[:, 0:HTW].bitcast(BF16).rearrange("p (k t) -> p k t", k=KC)

        def MM(out, lhsT, rhs, start=True, stop=True, rd=(), wr=()):
            S.op("pe", lambda e: e.matmul(out, lhsT=lhsT, rhs=rhs, start=start, stop=stop), rd, wr)

        def TR(out, in_, ident, rd=(), wr=()):
            S.op("pe", lambda e: e.transpose(out, in_, ident), rd, wr)

        def ACT(out, in_, func, rd=(), wr=(), scale=1.0, bias=0.0):
            S.op("act", lambda e: e.activation(out=out, in_=in_, func=func, bias=bias, scale=scale), rd, wr)

        def TT(out, in0, in1, op, rd=(), wr=(), eng="dve"):
            S.op(eng, lambda e: e.tensor_tensor(out=out, in0=in0, in1=in1, op=op), rd, wr)

        def TS(out, in0, s1, s2, op0, op1=None, rd=(), wr=(), eng="dve"):
            if op1 is None:
                S.op(eng, lambda e: e.tensor_scalar(out=out, in0=in0, scalar1=s1, scalar2=None, op0=op0), rd, wr)
            else:
                S.op(eng, lambda e: e.tensor_scalar(out=out, in0=in0, scalar1=s1, scalar2=s2, op0=op0, op1=op1), rd, wr)

        def STT(out, in0, scalar, in1, op0, op1, rd=(), wr=(), eng="dve"):
            S.op(eng, lambda e: e.scalar_tensor_tensor(out=out, in0=in0, scalar=scalar, in1=in1, op0=op0, op1=op1), rd, wr)

        def CP(out, in_, rd=(), wr=(), eng="dve"):
            if eng == "act":
                S.op("act", lambda e: e.copy(out=out, in_=in_), rd, wr)
            else:
                S.op(eng, lambda e: e.tensor_copy(out=out, in_=in_), rd, wr)

        def DMA(q, out, in_, rd=(), wr=(), sem="init"):
            S.op(q, lambda e: e.dma_start(out=out, in_=in_), rd, wr, dma_sem=sem)

        def dump(name, ap, rd, dt=F32):
            if name not in KDBG.split(","):
                return
            o = dout("dbg_" + name, list(ap.shape), dt)
            dbg_outs[name] = True
            DMA("sp", o, ap, rd=rd, sem="dbg")

        def cfs(name, a=None, b=None):
            o, w = CF[name]
            return cf[:, o + (a or 0): o + (w if b is None else b)]

        def c6(name, a=None, b=None, rows=64):
            o, w = C64[name]
            return c64[0:rows, o + (a or 0): o + (w if b is None else b)]

        r_c = Res()
        epsb = sbt("epsb", [128, 4], F32)
        r_eps = Res()
        for _i, _v in enumerate((D * EPS, 128.0 * EPS, EPS, 1.0)):
            S.op("dve", (lambda t_, v_: (lambda e: e.memset(t_, v_)))(epsb[:, _i:_i + 1], float(_v)), [], [r_eps])
        EPSCOL = {float(D * EPS): 0, float(128.0 * EPS): 1, float(EPS): 2}

        def RSTD(out, in_, eps_tot, rd, wr):
            c_ = EPSCOL[float(eps_tot)]
            ACT(out, in_, AF.Ln, rd=list(rd) + [r_eps], wr=wr, bias=epsb[:, c_:c_ + 1])
            ACT(out, out, AF.Exp, rd=wr, wr=wr, scale=-0.5)
        wres = [Res() for _ in range(NSLOT)]
        wi_ = [0]

        plan_by_key = {e[1]: e for e in plan}
        plan_rec = []
        offs_rec = {}

        def next_w(key):
            i = wi_[0]
            wi_[0] += 1
            a, k2, src, l, row0, nk, col0, ncols, _off = plan_by_key[key]
            off = offs_rec.get(a, 0)
            offs_rec[a] = off + 128 * nk * 128
            plan_rec.append((a, key, src, l, row0, nk, col0, ncols, off))
            slot = i % NSLOT
            srcap = wdram[a][off: off + 128 * nk * 128].rearrange("(p k n) -> p k n", p=128, k=nk)
            DMA("pool", wring[:, slot, 0:nk, :], srcap, wr=[wres[slot]], sem=f"w{slot}")
            return wring[:, slot], wres[slot]

        DMA("sp", cf[:], cf_in, wr=[r_c])
        DMA("sp", c64[:], c64_in, wr=[r_c])
        r_cb = Res()
        CP(cb[:, 0, :], cfs("onesf"), rd=[r_c], wr=[r_cb])
        CP(cb[:, 1, :], cfs("identb"), rd=[r_c], wr=[r_cb])
        CP(cb[:, 2, :], cfs("Rm"), rd=[r_c], wr=[r_cb])
        CP(c64b[:, 0:64], c6("mT0"), rd=[r_c], wr=[r_cb])
        CP(c64b[:, 64:128], c6("mT1"), rd=[r_c], wr=[r_cb])
        DMA("pool", mzb_t[:], mz_in, wr=[r_cb], sem="mz")
        ones_bf = cb[:, 0, :]
        ident_bf = cb[:, 1, :]
        RmT_bf = cb[:, 2, :]
        ident_f = cfs("ident")

        xT_r = [[Res(), Res()] for _ in range(KC)]
        A = Arena(HTW)
        xin = [A.alloc(D, F32) for _ in range(2)]
        xin_r = [Res(), Res()]
        for tt in range(8):
            sl = tt % 2
            DMA("sp", xin[sl], x_in[tt * 128:(tt + 1) * 128, :], wr=[xin_r[sl]], sem=f"xin{sl}")
            for c4 in range(4):
                ps, pr = nextps()
                for j in range(4):
                    c = c4 * 4 + j
                    TR(ps[:, j * 128:(j + 1) * 128], xin[sl][:, c * 128:(c + 1) * 128], ident_f,
                       rd=[xin_r[sl], r_c], wr=[pr])
                half = tt // 4
                eng = "act" if c4 % 2 else "dve"
                CP(xT[:, c4 * 4:(c4 + 1) * 4, tt * 128:(tt + 1) * 128],
                   ps[:].rearrange("p (j t) -> p j t", j=4), rd=[pr],
                   wr=[xT_r[c4 * 4 + j][half] for j in range(4)], eng=eng)

        NL = 0 if KSTOP in ('x', 'mod') else 2
        NMOD = 0 if KSTOP == 'x' else 2
        r_s = Res()
        sg_t = A.alloc(16, F32)
        ACT(sg_t, cfs("cond"), AF.Silu, rd=[r_c], wr=[r_s])
        CP(s_bf[:], sg_t, rd=[r_s], wr=[r_s])
        mod_r = [[Res() for _ in range(6)] for _ in range(2)]
        geff_r = [[Res(), Res()] for _ in range(2)]
        psM, prM = ps_t[7], ps_r[7]
        mod_next = [0, 0]

        def mod_finalize(l, m):
            TT(mods[:, l, m * 16:(m + 1) * 16], psM[:, l * 96 + m * 16: l * 96 + (m + 1) * 16],
               cfs(f"bmod{l}", m * 16, (m + 1) * 16), ALU.add, rd=[prM, r_c], wr=[mod_r[l][m]])
            if m in (1, 4):
                ni = 0 if m == 1 else 1
                gn = f"n1g{l}" if m == 1 else f"n2g{l}"
                STT(geff[:, l, ni, :], mods[:, l, m * 16:(m + 1) * 16], 1.0, cfs(gn), ALU.add, ALU.mult,
                    rd=[mod_r[l][m], r_c], wr=[geff_r[l][ni]])
                TS(geff[:, l, ni, :], geff[:, l, ni, :], float(np.sqrt(D)), None, ALU.mult,
                   rd=[geff_r[l][ni]], wr=[geff_r[l][ni]])

        def pump_mod(l, n):
            if NMOD == 0:
                return
            for _ in range(n):
                j = mod_next[l]
                if j >= 96:
                    return
                mod_next[l] = j + 1
                wt, wr_ = next_w(("mod", l, j))
                col = l * 96 + j
                for kc in range(KC):
                    MM(psM[:, col:col + 1], wt[:, kc, :], s_bf[:, kc:kc + 1], start=(kc == 0), stop=(kc == KC - 1),
                       rd=[wr_, r_s], wr=[prM])
                if j % 16 == 15:
                    mod_finalize(l, j // 16)

        pump_mod(0, 32 if LAZYMOD else 96)
        if not LAZYMOD:
            pump_mod(0, 96)
            pump_mod(1, 96)

        def mod_gen(n, lag=5):
            for _ in range(n):
                l_ = 0 if mod_next[0] < 96 else 1
                j = mod_next[l_]
                if NMOD == 0 or j >= 96:
                    return
                mod_next[l_] = j + 1
                wt, wr_ = next_w(("mod", l_, j))
                for _k in range(lag):
                    yield
                col = l_ * 96 + j
                for kc in range(KC):
                    MM(psM[:, col:col + 1], wt[:, kc, :], s_bf[:, kc:kc + 1], start=(kc == 0), stop=(kc == KC - 1),
                       rd=[wr_, r_s], wr=[prM])
                if j % 16 == 15:
                    mod_finalize(l_, j // 16)
                yield
        r_g = Res()
        for l in range(2):
            CP(gsm[:, l:l + 1], cfs(f"qg{l}"), rd=[r_c], wr=[r_g])
            TS(gsm[:, 2 + l:3 + l], cfs(f"kg{l}"), float(np.sqrt(128.0)), None, ALU.mult, rd=[r_c], wr=[r_g])
            TS(gsm[:, 4 + l:5 + l], cfs(f"glag{l}"), float(np.sqrt(128.0)), None, ALU.mult, rd=[r_c], wr=[r_g])
            TS(gsm[:, 6 + l:7 + l], cfs(f"dng{l}"), float(np.sqrt(128.0)), None, ALU.mult, rd=[r_c], wr=[r_g])
        if KSTOP in ('mod', 'n1') or 'mods' in KDBG:
            pump_mod(0, 96)
        dump("mods", mods[:].rearrange("p l m -> p (l m)"), mod_r[0])

        hT_r = [Res(), Res()]

        def rmsnorm_to_hT(l, ni, A):
            sh = 0 if ni == 0 else 3
            sq = [A.alloc(512, BF16) for _ in range(2)]
            sq_r = [Res(), Res()]
            rstd = [A.alloc(512, F32) for _ in range(2)]
            rstd_r = [Res(), Res()]
            tmp = [A.alloc(512, F32) for _ in range(2)]
            tmp_r = [Res(), Res()]
            for half in range(2):
                hs = slice(half * 512, (half + 1) * 512)
                ps, pr = nextps()
                for c in range(KC):
                    k = c % 2
                    if c % 2:
                        TT(sq[k], xT[:, c, hs], xT[:, c, hs], ALU.mult, rd=[xT_r[c][half]], wr=[sq_r[k]])
                    else:
                        ACT(sq[k], xT[:, c, hs], AF.Square, rd=[xT_r[c][half]], wr=[sq_r[k]])
                    MM(ps[:], ones_bf, sq[k], start=(c == 0), stop=(c == KC - 1), rd=[sq_r[k], r_cb], wr=[pr])
                RSTD(rstd[half], ps[:], D * EPS, [pr], [rstd_r[half]])
                for c in range(KC):
                    k = c % 2
                    if c % 4 == 3:
                        STT(tmp[k], xT[:, c, hs], geff[:, l, ni, c:c + 1], rstd[half], ALU.mult, ALU.mult,
                            rd=[xT_r[c][half], rstd_r[half], geff_r[l][ni]], wr=[tmp_r[k]])
                        TS(hT[:, c, hs], tmp[k], mods[:, l, sh * 16 + c: sh * 16 + c + 1], None, ALU.add,
                           rd=[tmp_r[k], mod_r[l][sh]], wr=[hT_r[half]])
                    else:
                        TT(tmp[k], xT[:, c, hs], rstd[half], ALU.mult, rd=[xT_r[c][half], rstd_r[half]], wr=[tmp_r[k]])
                        ACT(hT[:, c, hs], tmp[k], AF.Identity, rd=[tmp_r[k], geff_r[l][ni], mod_r[l][sh]], wr=[hT_r[half]],
                            scale=geff[:, l, ni, c:c + 1], bias=mods[:, l, sh * 16 + c: sh * 16 + c + 1])

        def proj_fm(wt, wr_, half, ps, pr, mcols=128):
            hs = slice(half * 512, (half + 1) * 512)
            for kc in range(KC):
                MM(ps[0:mcols, :], wt[:, kc, 0:mcols], hT[:, kc, hs], start=(kc == 0), stop=(kc == KC - 1),
                   rd=[wr_, hT_r[half]], wr=[pr])

        def wout_apply(l, key, nk, mixT, mix_r):
            for dc in range(16):
                wt, wr_ = next_w((key, l, dc))
                for half in range(2):
                    hs = slice(half * 512, (half + 1) * 512)
                    ps, pr = nextps()
                    for k in range(nk):
                        MM(ps[:], wt[:, k, :], mixT[:, k, hs], start=(k == 0), stop=(k == nk - 1),
                           rd=[wr_, mix_r[k][half]], wr=[pr])
                    STT(xT[:, dc, hs], ps[:], mods[:, l, 2 * 16 + dc: 2 * 16 + dc + 1], xT[:, dc, hs], ALU.mult, ALU.add,
                        rd=[pr, mod_r[l][2], xT_r[dc][half]], wr=[xT_r[dc][half]])

        def headnorm_fm(src_ap, src_r, A_slots, eps_tot):
            sq, sq_r, rstd, rstd_r = A_slots
            ACT(sq, src_ap, AF.Square, rd=src_r, wr=[sq_r])
            ps2, pr2 = nextps()
            MM(ps2[:], ones_bf, sq, rd=[sq_r, r_cb], wr=[pr2])
            RSTD(rstd, ps2[:], eps_tot, [pr2], [rstd_r])
            return rstd, rstd_r


        def run_il(gens):
            act_ = list(gens)
            while act_:
                for g_ in list(act_):
                    try:
                        next(g_)
                    except StopIteration:
                        act_.remove(g_)

        def mix_epilogue(A2, oacc, oacc_r, mixT, mix_r, gcol):
            NS = 3
            slots = [(A2.alloc(512, BF16), Res(), A2.alloc(512, F32), Res(), A2.alloc(512, F32), Res()) for _ in range(NS)]

            def it(h, half, k):
                hs = slice(half * 512, (half + 1) * 512)
                orr = oacc_r[half * 8:(half + 1) * 8]
                sq, sq_r, rstd, rstd_r, tmp, tmp_r = slots[k]
                ACT(sq, oacc[:, h, hs], AF.Square, rd=orr, wr=[sq_r])
                yield
                ps2, pr2 = nextps()
                MM(ps2[:], ones_bf, sq, rd=[sq_r, r_cb], wr=[pr2])
                yield
                ACT(rstd, ps2[:], AF.Ln, rd=[pr2, r_eps], wr=[rstd_r], bias=epsb[:, 1:2])
                yield
                ACT(rstd, rstd, AF.Exp, rd=[rstd_r], wr=[rstd_r], scale=-0.5)
                yield
                TT(tmp, oacc[:, h, hs], rstd, ALU.mult, rd=orr + [rstd_r], wr=[tmp_r])
                yield
                STT(mixT[:, h, hs], tmp, gcol, mixT[:, h, hs], ALU.mult, ALU.mult,
                    rd=[tmp_r, r_g, mix_r[h][half]], wr=[mix_r[h][half]])
            items = [(h, half) for h in range(4) for half in range(2)]
            for g0 in range(0, 8, NS):
                run_il([it(h, half, k) for k, (h, half) in enumerate(items[g0:g0 + NS])])

        def gla_phase(l):
            A = Arena(HTW)
            mixT = A.alloc(4 * T, BF16, "p (k t) -> p k t", k=4)
            mix_r = [[Res(), Res()] for _ in range(4)]
            qraw = A.alloc(4 * T, BF16, "p (h t) -> p h t", h=4)
            kraw = A.alloc(4 * T, BF16, "p (h t) -> p h t", h=4)
            qk_r = Res()
            vtok = A.alloc(16 * 512, BF16, "p (c n) -> p c n", c=16)
            vt_r = [Res() for _ in range(16)]
            oacc = A.alloc(4 * T, F32, "p (h t) -> p h t", h=4)
            oacc_r = [Res() for _ in range(16)]
            Sg = A.alloc(2 * 512, F32, "p (z n) -> p z n", z=2)
            Sg_r = [Res(), Res()]
            Sgb = A.alloc(2 * 512, BF16, "p (z n) -> p z n", z=2)
            Sgb_r = [Res(), Res()]
            lrT = A.alloc(T, F32)
            lr_r = Res()
            w2a = A.alloc(512, F32)
            r_w2 = Res()
            e1 = [A.alloc(256, F32) for _ in range(2)]
            sp = [A.alloc(256, F32) for _ in range(2)]
            eb = [A.alloc(256, F32) for _ in range(2)]
            enb = [A.alloc(256, F32) for _ in range(2)]
            qt = [A.alloc(256, BF16) for _ in range(2)]
            kt = [A.alloc(256, BF16) for _ in range(2)]
            ATb = [A.alloc(256, BF16) for _ in range(2)]
            ktok = [A.alloc(256, BF16) for _ in range(2)]
            sst = [A.alloc(512, F32) for _ in range(2)]
            sst_r = [Res(), Res()]
            rr = {n: [Res(), Res()] for n in ("e1", "sp", "eb", "enb", "qt", "kt", "AT", "ktok")}
            DMA("sp", Sg[0:64], sg_in[l], wr=Sg_r, sem="sgi")
            for z in range(2):
                CP(Sgb[0:64, z, :], Sg[0:64, z, :], rd=[Sg_r[z]], wr=[Sgb_r[z]], eng="act")
            DMA("sp", w2a[0:33, :], w2aug_in[l], wr=[r_w2], sem="sgi")
            S.op("dve", lambda e: e.memset(lrT[32:33, :], 1.0), [], [lr_r])
            ei = 0
            for kind, dst in (("gq", qraw), ("gk", kraw)):
                for p in range(2):
                    wt, wr_ = next_w((kind, l, p))
                    for hp in range(2):
                        h = 2 * p + hp
                        for half in range(2):
                            hs = slice(half * 512, (half + 1) * 512)
                            ps, pr = nextps()
                            for kc in range(KC):
                                MM(ps[0:64, :], wt[:, kc, hp * 64:(hp + 1) * 64], hT[:, kc, hs], start=(kc == 0),
                                   stop=(kc == KC - 1), rd=[wr_, hT_r[half]], wr=[pr])
                            CP(dst[0:64, h, hs], ps[0:64, :], rd=[pr], wr=[qk_r], eng=("act" if ei % 2 else "dve"))
                            ei += 1
            for h in range(4):
                wt, wr_ = next_w(("gr", l, h))
                for half in range(2):
                    hs = slice(half * 512, (half + 1) * 512)
                    ps, pr = nextps()
                    proj_fm(wt, wr_, half, ps, pr)
                    ACT(mixT[:, h, hs], ps[:], AF.Silu, rd=[pr], wr=[mix_r[h][half]])
            wt, wr_ = next_w(("glr", l))
            for half in range(2):
                hs = slice(half * 512, (half + 1) * 512)
                ps, pr = nextps()
                proj_fm(wt, wr_, half, ps, pr, mcols=32)
                CP(lrT[0:32, hs], ps[0:32, :], rd=[pr], wr=[lr_r])
            for h in range(4):
                wt, wr_ = next_w(("gv", l, h))
                for c in range(16):
                    ps, pr = nextps()
                    for kc in range(KC):
                        MM(ps[0:64, 0:128], hT[:, kc, c * 64:(c + 1) * 64], wt[:, kc, :], start=(kc == 0),
                           stop=(kc == KC - 1), rd=[wr_, hT_r[c // 8]], wr=[pr])
                    CP(vtok[0:64, c, h * 128:(h + 1) * 128], ps[0:64, 0:128], rd=[pr], wr=[vt_r[c]],
                       eng=("act" if c % 2 else "dve"))
            if KSTOP == 'gp':
                return
            owritten = [False] * 16
            sso = [0]
            h4 = lambda ap: ap.rearrange("p (h i) -> p h i", h=4)
            id64 = ident_bf[0:64, 0:64]

            def cut(n):
                return KGS != "" and n >= int(KGS)

            def step(z, c):
                cs_ = slice(c * 64, (c + 1) * 64)
                ps1, pr1 = nextps()
                MM(ps1[0:64, 0:256], lrT[0:33, cs_], w2a[0:33, z * 256:(z + 1) * 256], rd=[lr_r, r_w2], wr=[pr1])
                ACT(e1[z][0:64, :], ps1[0:64, 0:256], AF.Exp, rd=[pr1], wr=[rr["e1"][z]], scale=-1.0)
                ACT(sp[z][0:64, :], e1[z][0:64, :], AF.Ln, rd=[rr["e1"][z], r_eps], wr=[rr["sp"][z]], bias=epsb[0:64, 3:4])
                if cut(1):
                    return
                yield
                psb, prb = nextps()
                for h in range(4):
                    MM(psb[0:64, h * 64:(h + 1) * 64], sp[z][0:64, h * 64:(h + 1) * 64], c6(f"TriS{z}"),
                       rd=[rr["sp"][z], r_c], wr=[prb])
                ACT(eb[z][0:64, :], psb[0:64, 0:256], AF.Exp, rd=[prb], wr=[rr["eb"][z]])
                ACT(enb[z][0:64, :], psb[0:64, 0:256], AF.Exp, rd=[prb], wr=[rr["enb"][z]], scale=-1.0)
                if cut(2):
                    return
                yield
                STT(h4(qt[z][0:64, :]), h4(eb[z][0:64, :]), 0.125, qraw[0:64, :, cs_], ALU.mult, ALU.mult,
                    rd=[rr["eb"][z], qk_r], wr=[rr["qt"][z]])
                TT(h4(kt[z][0:64, :]), h4(enb[z][0:64, :]), kraw[0:64, :, cs_], ALU.mult,
                   rd=[rr["enb"][z], qk_r], wr=[rr["kt"][z]])
                if cut(3):
                    return
                yield
                psA, prA = nextps()
                for h in range(4):
                    hc = slice(h * 64, (h + 1) * 64)
                    MM(psA[0:64, hc], kt[z][0:64, hc], qt[z][0:64, hc], rd=[rr["kt"][z], rr["qt"][z]], wr=[prA])
                yield
                TT(h4(ATb[z][0:64, :]), h4(psA[0:64, 0:256]),
                   c64b[:, z * 64:(z + 1) * 64].unsqueeze(1).to_broadcast([64, 4, 64]), ALU.mult,
                   rd=[prA, r_cb], wr=[rr["AT"][z]])
                if cut(4):
                    return
                yield
                pst, prt = nextps()
                pstb = pst[:].bitcast(BF16)
                for h in range(4):
                    hc = slice(h * 64, (h + 1) * 64)
                    TR(pstb[0:64, hc], kt[z][0:64, hc], id64, rd=[rr["kt"][z], r_cb], wr=[prt])
                yield
                CP(ktok[z][0:64, :], pstb[0:64, 0:256], rd=[prt], wr=[rr["ktok"][z]], eng="act")
                if cut(5):
                    return
                if (z == 0 and c % 4 == 0 and c > 0) or (z == 1 and c % 4 == 3 and c < 15):
                    TS(Sg[0:64, z, :], Sg[0:64, z, :], cfs("carry")[0:64, :], None, ALU.mult, rd=[Sg_r[z], r_c], wr=[Sg_r[z]])
                    CP(Sgb[0:64, z, :], Sg[0:64, z, :], rd=[Sg_r[z]], wr=[Sgb_r[z]], eng="act")
                yield
                psO, prO = nextps()
                for h in range(4):
                    hc = slice(h * 64, (h + 1) * 64)
                    MM(psO[:, hc], vtok[0:64, c, h * 128:(h + 1) * 128], ATb[z][0:64, hc], start=True, stop=False,
                       rd=[vt_r[c], rr["AT"][z]], wr=[prO])
                    MM(psO[:, hc], Sgb[0:64, z, h * 128:(h + 1) * 128], qt[z][0:64, hc], start=False, stop=True,
                       rd=[Sgb_r[z], rr["qt"][z]], wr=[prO])
                yield
                ov = oacc[:, :, cs_]
                pv = h4(psO[:, 0:256])
                if not owritten[c]:
                    CP(ov, pv, rd=[prO], wr=[oacc_r[c]])
                    owritten[c] = True
                else:
                    TT(ov, pv, ov, ALU.add, rd=[prO, oacc_r[c]], wr=[oacc_r[c]])
                if cut(6):
                    return
                yield
                psS, prS = nextps()
                for h in range(4):
                    MM(psS[0:64, h * 128:(h + 1) * 128], ktok[z][0:64, h * 64:(h + 1) * 64],
                       vtok[0:64, c, h * 128:(h + 1) * 128], rd=[rr["ktok"][z], vt_r[c]], wr=[prS])
                yield
                TT(Sg[0:64, z, :], psS[0:64, :], Sg[0:64, z, :], ALU.add, rd=[prS, Sg_r[z]], wr=[Sg_r[z]])
                col = 63 if z == 0 else 0
                hd = lambda ap: ap.rearrange("p (h d) -> p h d", h=4)
                TT(hd(Sg[0:64, z, :]), hd(Sg[0:64, z, :]),
                   h4(eb[z][0:64, :])[:, :, col:col + 1].to_broadcast([64, 4, 128]), ALU.mult,
                   rd=[Sg_r[z], rr["eb"][z]], wr=[Sg_r[z]])
                CP(Sgb[0:64, z, :], Sg[0:64, z, :], rd=[Sg_r[z]], wr=[Sgb_r[z]], eng="act")
                if (z == 0 and c % 4 == 3) or (z == 1 and c % 4 == 0):
                    k = sso[0] % 2
                    sso[0] += 1
                    CP(sst[k][0:64, :], Sg[0:64, z, :], rd=[Sg_r[z]], wr=[sst_r[k]])
                    DMA("sp", sg_out[l, z, c // 4], sst[k][0:64, :], rd=[sst_r[k]], sem=f"sso{k}")

            for s in range(16):
                run_il([step(0, s), step(1, 15 - s)])
            S.barrier()
            if KGS != "":
                return
            A2 = Arena(HTW + 4 * T // 2)
            mix_epilogue(A2, oacc, oacc_r, mixT, mix_r, gsm[:, 4 + l:5 + l])
            if l == 0:
                dump("mgla", mixT[:, 0, :], [mix_r[0][0], mix_r[0][1]], BF16)
            wout_apply(l, "wo_gla", 4, mixT, mix_r)

        def dn_phase(l):
            A = Arena(HTW)
            mixT = A.alloc(4 * T, BF16, "p (k t) -> p k t", k=4)
            mix_r = [[Res(), Res()] for _ in range(4)]
            qkvT = A.alloc(8 * T, BF16, "p (c t) -> p c t", c=8)
            qkv_r = [Res() for _ in range(12)]
            oacc = A.alloc(4 * T, F32, "p (h t) -> p h t", h=4)
            oacc_r = [Res() for _ in range(16)]
            Sd = A.alloc(2 * 512, F32, "p (z n) -> p z n", z=2)
            Sd_r = [Res(), Res()]
            Sdb = A.alloc(2 * 512, BF16, "p (z n) -> p z n", z=2)
            Sdb_r = [Res(), Res()]
            gab = A.alloc(256, F32, "p (c n) -> p c n", c=16)
            gtmp = A.alloc(128, F32, "p (c n) -> p c n", c=16)
            gall = A.alloc(128, F32, "p (c n) -> p c n", c=16)
            beta = A.alloc(128, F32, "p (c n) -> p c n", c=16)
            nbeta = A.alloc(128, F32, "p (c n) -> p c n", c=16)
            nexpA = A.alloc(8, F32)
            nb = A.alloc(36, F32)
            r_gb = Res()
            vT = A.alloc(4 * T, BF16, "p (c t) -> p c t", c=4)
            mark = A.p
            raw = [A.alloc(1026, F32) for _ in range(2)]
            raw_r = [Res(), Res()]
            yv = [A.alloc(1024, F32) for _ in range(2)]
            yv_r = [Res(), Res()]
            sq_ = A.alloc(512, BF16)
            rstd_ = A.alloc(512, F32)
            nsl = (sq_, Res(), rstd_, Res())
            DMA("sp", Sd, sd_in[l], wr=Sd_r, sem="sgi")
            for z in range(2):
                CP(Sdb[:, z, :], Sd[:, z, :], rd=[Sd_r[z]], wr=[Sdb_r[z]], eng="act")
            TS(nb, cfs(f"conv{l}"), cfs("cflag"), -1.0, ALU.mult, ALU.mult, rd=[r_c], wr=[r_gb])
            for k in range(2):
                S.op("dve", (lambda t_: (lambda e: e.memset(t_, 0.0)))(raw[k][:, 0:1]), [], [raw_r[k]])
                S.op("dve", (lambda t_: (lambda e: e.memset(t_, 0.0)))(raw[k][:, 1025:1026]), [], [raw_r[k]])
            cw = cfs(f"conv{l}")
            def dn_proj(cc, wt, wr_):
                k = cc % 2
                for half in range(2):
                    ps, pr = nextps()
                    proj_fm(wt, wr_, half, ps, pr)
                    CP(raw[k][:, 1 + half * 512: 1 + (half + 1) * 512], ps[:], rd=[pr], wr=[raw_r[k]],
                       eng=("act" if half else "dve"))

            nsl_b = (A.alloc(512, BF16), Res(), A.alloc(512, F32), Res())
            nsl_k = [nsl, nsl_b]
            _sg = A.alloc(1024, F32)
            _sgr = Res()
            sg1 = [_sg, _sg]
            sg1_r = [_sgr, _sgr]
            assert A.p <= ARW, ("dn proj scratch", A.p, ARW)

            def dn_tail(cc):
                k = cc % 2
                TS(yv[k], raw[k][:, 1:1025], cw[:, cc * 3 + 1: cc * 3 + 2], None, ALU.mult, rd=[raw_r[k], r_c], wr=[yv_r[k]])
                STT(yv[k], raw[k][:, 0:1024], cw[:, cc * 3: cc * 3 + 1], yv[k], ALU.mult, ALU.add,
                    rd=[raw_r[k], r_c, yv_r[k]], wr=[yv_r[k]])
                STT(yv[k], raw[k][:, 2:1026], cw[:, cc * 3 + 2: cc * 3 + 3], yv[k], ALU.mult, ALU.add,
                    rd=[raw_r[k], r_c, yv_r[k]], wr=[yv_r[k]])
                STT(yv[k][:, 256:1024:256], raw[k][:, 256:1024:256], nb[:, cc * 3: cc * 3 + 1], yv[k][:, 256:1024:256],
                    ALU.mult, ALU.add, rd=[raw_r[k], r_gb, yv_r[k]], wr=[yv_r[k]])
                STT(yv[k][:, 255:1023:256], raw[k][:, 257:1025:256], nb[:, cc * 3 + 2: cc * 3 + 3],
                    yv[k][:, 255:1023:256], ALU.mult, ALU.add, rd=[raw_r[k], r_gb, yv_r[k]], wr=[yv_r[k]])
                yield
                ACT(sg1[k], yv[k], AF.Exp, rd=[yv_r[k]], wr=[sg1_r[k]], scale=-1.0)
                TS(sg1[k], sg1[k], 1.0, None, ALU.add, rd=[sg1_r[k]], wr=[sg1_r[k]])
                S.op("dve", (lambda t_: (lambda e: e.reciprocal(out=t_, in_=t_)))(sg1[k]), [sg1_r[k]], [sg1_r[k]])
                if cc >= 8:
                    TT(vT[:, cc - 8, :], yv[k], sg1[k], ALU.mult, rd=[yv_r[k], sg1_r[k]], wr=[qkv_r[cc]])
                    return
                TT(yv[k], yv[k], sg1[k], ALU.mult, rd=[yv_r[k], sg1_r[k]], wr=[yv_r[k]])
                yield
                sq, sq_r, rstd, rstd_r = nsl_k[k]
                for half in range(2):
                    hs = slice(half * 512, (half + 1) * 512)
                    ACT(sq, yv[k][:, hs], AF.Square, rd=[yv_r[k]], wr=[sq_r])
                    yield
                    ps2, pr2 = nextps()
                    MM(ps2[:], ones_bf, sq, rd=[sq_r, r_cb], wr=[pr2])
                    yield
                    ACT(rstd, ps2[:], AF.Ln, rd=[pr2, r_eps], wr=[rstd_r], bias=epsb[:, 2:3])
                    yield
                    ACT(rstd, rstd, AF.Exp, rd=[rstd_r], wr=[rstd_r], scale=-0.5)
                    yield
                    if cc < 4:
                        STT(qkvT[:, cc, hs], yv[k][:, hs], float(128.0 ** -0.5), rstd, ALU.mult, ALU.mult,
                            rd=[yv_r[k], rstd_r], wr=[qkv_r[cc]])
                    else:
                        TT(qkvT[:, cc, hs], yv[k][:, hs], rstd, ALU.mult, rd=[yv_r[k], rstd_r], wr=[qkv_r[cc]])
                    yield

            def proj_gen(ccs):
                yield
                for c_ in ccs:
                    w_ = next_w(("dqkv", l, c_))
                    dn_proj(c_, *w_)
                    yield

            for c_ in (0, 1):
                w_ = next_w(("dqkv", l, c_))
                dn_proj(c_, *w_)
            for cc0 in range(0, 12, 2):
                nxt = [c_ for c_ in (cc0 + 2, cc0 + 3) if c_ < 12]
                run_il([dn_tail(cc0), dn_tail(cc0 + 1), proj_gen(nxt)])
            wt, wr_ = next_w(("dab", l))
            psg, prg = nextps()
            for c in range(16):
                for kc in range(KC):
                    MM(psg[0:64, c * 16:(c + 1) * 16], hT[:, kc, c * 64:(c + 1) * 64], wt[:, kc, 0:16],
                       start=(kc == 0), stop=(kc == KC - 1), rd=[wr_, hT_r[c // 8]], wr=[prg])
            CP(gab[0:64], psg[0:64, 0:256].rearrange("p (c n) -> p c n", c=16), rd=[prg], wr=[r_gb])
            bc8 = lambda ap: ap.unsqueeze(1).to_broadcast([64, 16, 8])
            TT(gtmp[0:64], gab[0:64, :, 0:8], bc8(c6(f"dtb{l}")), ALU.add, rd=[r_gb, r_c], wr=[r_gb])
            ACT(gtmp[0:64], gtmp[0:64], AF.Exp, rd=[r_gb], wr=[r_gb])
            ACT(gtmp[0:64].rearrange("p c n -> p (c n)"), gtmp[0:64].rearrange("p c n -> p (c n)"), AF.Ln, rd=[r_gb, r_eps], wr=[r_gb], bias=epsb[0:64, 3:4])
            ACT(nexpA[0:64, :], c6(f"alog{l}"), AF.Exp, rd=[r_c], wr=[r_gb])
            TS(nexpA[0:64, :], nexpA[0:64, :], -1.0, None, ALU.mult, rd=[r_gb], wr=[r_gb])
            TT(gall[0:64], gtmp[0:64], bc8(nexpA[0:64, :]), ALU.mult, rd=[r_gb], wr=[r_gb])
            ACT(beta[0:64], gab[0:64, :, 8:16], AF.Exp, rd=[r_gb], wr=[r_gb], scale=-1.0)
            TS(beta[0:64], beta[0:64], 1.0, None, ALU.add, rd=[r_gb], wr=[r_gb])
            S.op("dve", lambda e: e.reciprocal(out=beta[0:64], in_=beta[0:64]), [r_gb], [r_gb])
            TS(nbeta[0:64], beta[0:64], -1.0, None, ALU.mult, rd=[r_gb], wr=[r_gb])
            for h in range(4):
                wt, wr_ = next_w(("dz", l, h))
                for half in range(2):
                    hs = slice(half * 512, (half + 1) * 512)
                    ps, pr = nextps()
                    proj_fm(wt, wr_, half, ps, pr)
                    ACT(mixT[:, h, hs], ps[:], AF.Silu, rd=[pr], wr=[mix_r[h][half]])
            if l == 0:
                dump("dq", qkvT[:, 0, :], [qkv_r[0]], BF16)
                dump("dk", qkvT[:, 4, :], [qkv_r[4]], BF16)
                dump("dv", vT[:, 0, :], [qkv_r[8]], BF16)
                dump("dg", gall[0:64].rearrange("p c n -> p (c n)"), [r_gb])
                dump("dbeta", beta[0:64].rearrange("p c n -> p (c n)"), [r_gb])
            S.barrier()
            AH = Arena(0)
            AR = Arena(mark)
            NSL = 3
            NLN = 2

            def alloc2(n, dt):
                nw = (n * (4 if dt == F32 else 2) + 3) // 4
                if AH.p + nw <= HTW:
                    return AH.alloc(n, dt)
                return AR.alloc(n, dt)
            WK = {}
            for z in range(2):
                for ln in range(NLN):
                    for name, n, dt in (("gTri", 256, F32), ("gbc", 256, F32), ("eGbc", 256, F32), ("AA", 512, BF16),
                                        ("Xs", 512, BF16)):
                        WK[(name, z, ln)] = (alloc2(n, dt), Res())
                    for al, sname in (("d1", "gTri"), ("a1", "gTri"), ("d2", "gbc"), ("tmpm", "Xs")):
                        WK[(al, z, ln)] = WK[(sname, z, ln)]
            HO = {}
            for z in range(2):
                for sl in range(NSL):
                    for name, n, dt in (("TTT", 512, BF16), ("attT", 256, BF16), ("qte", 256, BF16), ("kd", 512, BF16),
                                        ("bv", 512, BF16), ("sm", 12, F32), ("eGl", 4, F32)):
                        HO[(name, z, sl)] = (alloc2(n, dt), Res())
                    S.op("dve", (lambda t_: (lambda e: e.memset(t_, 0.0)))(HO[("attT", z, sl)][0][64:128, :]), [],
                         [HO[("attT", z, sl)][1]])
            SW = {}
            for z in range(2):
                for name, n, dt in (("tmpr", 512, F32), ("r", 512, BF16), ("vn", 512, BF16)):
                    SW[(name, z)] = (alloc2(n, dt), Res())
                S.op("dve", (lambda t_: (lambda e: e.memset(t_, 0.0)))(SW[("vn", z)][0][64:128, :]), [], [SW[("vn", z)][1]])
            _s1 = alloc2(512, F32)
            _r1 = Res()
            print("DN ring end", AH.p, HTW, AR.p, ARW)
            owritten = [False] * 16
            h4 = lambda ap: ap.rearrange("p (h i) -> p h i", h=4)
            hd = lambda ap: ap.rearrange("p (h d) -> p h d", h=4)
            v4 = lambda ap: ap.rearrange("p (v h j) -> p v h j", v=2, h=4)
            id64 = ident_bf[0:64, 0:64]

            ps_busy = [False] * 7

            def getps():
                while True:
                    for k_ in range(7):
                        i_ = (ps_i[0] + k_) % 7
                        if not ps_busy[i_]:
                            ps_busy[i_] = True
                            ps_i[0] = i_ + 1
                            return ps_t[i_], ps_r[i_], i_
                    yield

            def relps(i_):
                ps_busy[i_] = False

            def pre(z, c, ln, sl):
                cs_ = slice(c * 64, (c + 1) * 64)
                g = gall[0:64, c, z * 4:(z + 1) * 4]
                bz = beta[0:64, c, z * 4:(z + 1) * 4]
                nbz = nbeta[0:64, c, z * 4:(z + 1) * 4]
                Tri = c6("TriL") if z == 0 else c6("TriU")
                nTri = c6("nTriL") if z == 0 else c6("nTriU")
                TriC = c6("TriCL") if z == 0 else c6("TriCU")
                w = lambda n: WK[(n, z, ln)][0]
                wq = lambda n: WK[(n, z, ln)][1]
                o = lambda n: HO[(n, z, sl)][0]
                oq = lambda n: HO[(n, z, sl)][1]
                mz = lambda m: mzb[:, z, m].unsqueeze(2).to_broadcast([64, 2, 4, 64])
                gb64 = g.unsqueeze(2).to_broadcast([64, 4, 64])
                TT(h4(w("gTri")[0:64, :]), Tri.unsqueeze(1).to_broadcast([64, 4, 64]), gb64, ALU.mult,
                   rd=[r_c, r_gb], wr=[wq("gTri")])
                CP(h4(w("gbc")[0:64, :]), gb64, rd=[r_gb], wr=[wq("gbc")])
                yield
                psD, prD, bD = yield from getps()
                MM(psD[0:64, 0:256], Tri, w("gbc")[0:64, :], start=True, stop=False, rd=[r_c, wq("gbc")], wr=[prD])
                MM(psD[0:64, 0:256], c6("negones"), w("gTri")[0:64, :], start=False, stop=True, rd=[r_c, wq("gTri")], wr=[prD])
                MM(psD[0:64, 256:512], c6("ones", 0, 64), w("gTri")[0:64, :], start=True, stop=False, rd=[r_c, wq("gTri")], wr=[prD])
                MM(psD[0:64, 256:512], nTri, w("gbc")[0:64, :], start=False, stop=True, rd=[r_c, wq("gbc")], wr=[prD])
                psX, prX, bX = yield from getps()
                MM(psX[:, 0:256], c6("ones"), w("gTri")[0:64, :], rd=[r_c, wq("gTri")], wr=[prX])
                MM(psX[0:64, 256:260], Tri, g, rd=[r_c, r_gb], wr=[prX])
                MM(psX[:, 260:264], c6("ones"), g, rd=[r_c, r_gb], wr=[prX])
                MM(psX[0:64, 264:268], TriC, g, rd=[r_c, r_gb], wr=[prX])
                yield
                TT(h4(w("d1")[0:64, :]), h4(psD[0:64, 0:256]), c6(f"mbS{z}").unsqueeze(1).to_broadcast([64, 4, 64]),
                   ALU.add, rd=[prD, r_c], wr=[wq("d1")])
                TT(h4(w("d2")[0:64, :]), h4(psD[0:64, 256:512]), c6(f"mbIT{z}").unsqueeze(1).to_broadcast([64, 4, 64]),
                   ALU.add, rd=[prD, r_c], wr=[wq("d2")])
                ACT(w("eGbc"), psX[:, 0:256], AF.Exp, rd=[prX], wr=[wq("eGbc")])
                ACT(o("sm")[0:64, 0:4], psX[0:64, 256:260], AF.Exp, rd=[prX], wr=[oq("sm")])
                ACT(o("eGl"), psX[:, 260:264], AF.Exp, rd=[prX], wr=[oq("eGl")])
                ACT(o("sm")[0:64, 4:8], psX[0:64, 264:268], AF.Exp, rd=[prX], wr=[oq("sm")])
                relps(bD)
                relps(bX)
                yield
                ACT(w("d1")[0:64, :], w("d1")[0:64, :], AF.Exp, rd=[wq("d1")], wr=[wq("d1")])
                ACT(w("d2")[0:64, :], w("d2")[0:64, :], AF.Exp, rd=[wq("d2")], wr=[wq("d2")])
                TT(o("sm")[0:64, 8:12], o("sm")[0:64, 0:4], nbz, ALU.mult, rd=[oq("sm"), r_gb], wr=[oq("sm")])
                psK, prK, bK = yield from getps()
                for h in range(4):
                    MM(psK[0:64, h * 64:(h + 1) * 64], qkvT[:, 4 + h, cs_], qkvT[:, 4 + h, cs_], rd=[qkv_r[4 + h]], wr=[prK])
                for h in range(4):
                    MM(psK[0:64, 256 + h * 64:256 + (h + 1) * 64], qkvT[:, 4 + h, cs_], qkvT[:, h, cs_],
                       rd=[qkv_r[4 + h], qkv_r[h]], wr=[prK])
                yield
                TT(w("a1")[0:64, :], psK[0:64, 0:256], w("d1")[0:64, :], ALU.mult, rd=[prK, wq("d1")], wr=[wq("a1")])
                TT(h4(w("AA")[0:64, 0:256]), h4(w("a1")[0:64, :]), bz.unsqueeze(2).to_broadcast([64, 4, 64]), ALU.mult,
                   rd=[wq("a1"), r_gb], wr=[wq("AA")])
                TT(o("attT")[0:64, :], psK[0:64, 256:512], w("d2")[0:64, :], ALU.mult, rd=[prK, wq("d2")], wr=[oq("attT")])
                relps(bK)
                yield
                pst, prt, bt = yield from getps()
                pst_v = pst[:].bitcast(BF16)
                for h in range(4):
                    TR(pst_v[0:64, h * 64:(h + 1) * 64], w("AA")[0:64, h * 64:(h + 1) * 64], id64,
                       rd=[wq("AA"), r_cb], wr=[prt])
                psk, prk, bk = yield from getps()
                pskb = psk[:].bitcast(BF16)
                for h in range(4):
                    TR(pskb[0:64, h * 128:(h + 1) * 128], qkvT[:, 4 + h, cs_], ident_bf, rd=[qkv_r[4 + h], r_cb], wr=[prk])
                prv = prk
                for h in range(4):
                    TR(pskb[0:64, 512 + h * 128:512 + (h + 1) * 128], vT[:, h, cs_], ident_bf, rd=[qkv_r[8 + h], r_cb], wr=[prv])
                yield
                CP(w("AA")[0:64, 256:512], pst_v[0:64, 0:256], rd=[prt], wr=[wq("AA")], eng="act")
                TT(hd(o("kd")[0:64, :]), hd(pskb[0:64, 0:512]), o("sm")[0:64, 4:8].unsqueeze(2).to_broadcast([64, 4, 128]),
                   ALU.mult, rd=[prk, oq("sm")], wr=[oq("kd")])
                TT(hd(o("bv")[0:64, :]), hd(pskb[0:64, 512:1024]), bz.unsqueeze(2).to_broadcast([64, 4, 128]), ALU.mult,
                   rd=[prv, r_gb], wr=[oq("bv")])
                TT(h4(o("qte")), qkvT[:, 0:4, cs_], h4(w("eGbc")), ALU.mult, rd=qkv_r[0:4] + [wq("eGbc")], wr=[oq("qte")])
                relps(bt)
                relps(bk)
                yield
                TT(v4(w("tmpm")[0:64, :]), v4(w("AA")[0:64, :]), mz(0), ALU.mult, rd=[wq("AA"), r_cb], wr=[wq("tmpm")])
                TT(v4(o("TTT")[0:64, :]), c6("ident").unsqueeze(1).unsqueeze(1).to_broadcast([64, 2, 4, 64]),
                   v4(w("tmpm")[0:64, :]), ALU.subtract, rd=[wq("tmpm"), r_c], wr=[oq("TTT")])
                for m in range(1, 6):
                    yield
                    psXX, prXX, bXX = yield from getps()
                    for h in range(4):
                        hc = slice(h * 64, (h + 1) * 64)
                        hc2 = slice(256 + h * 64, 256 + (h + 1) * 64)
                        MM(psXX[0:64, hc], w("AA")[0:64, hc2], o("TTT")[0:64, hc], rd=[wq("AA"), oq("TTT")], wr=[prXX])
                        MM(psXX[0:64, hc2], w("AA")[0:64, hc], o("TTT")[0:64, hc2], rd=[wq("AA"), oq("TTT")], wr=[prXX])
                    yield
                    CP(w("Xs")[0:64, :], psXX[0:64, :], rd=[prXX], wr=[wq("Xs")], eng="act")
                    relps(bXX)
                    yield
                    psY, prY, bY = yield from getps()
                    for h in range(4):
                        hc = slice(h * 64, (h + 1) * 64)
                        hc2 = slice(256 + h * 64, 256 + (h + 1) * 64)
                        MM(psY[0:64, hc], o("TTT")[0:64, hc2], w("Xs")[0:64, hc], rd=[wq("Xs"), oq("TTT")], wr=[prY])
                        MM(psY[0:64, hc2], o("TTT")[0:64, hc], w("Xs")[0:64, hc2], rd=[wq("Xs"), oq("TTT")], wr=[prY])
                    yield
                    TT(v4(w("tmpm")[0:64, :]), v4(psY[0:64, :]), mz(m), ALU.mult, rd=[prY, r_cb], wr=[wq("tmpm")])
                    relps(bY)
                    yield
                    TT(o("TTT")[0:64, :], o("TTT")[0:64, :], w("tmpm")[0:64, :], ALU.subtract,
                       rd=[oq("TTT"), wq("tmpm")], wr=[oq("TTT")])

            sso = [0]

            def ser(z, c, sl):
                cs_ = slice(c * 64, (c + 1) * 64)
                o = lambda n: HO[(n, z, sl)][0]
                oq = lambda n: HO[(n, z, sl)][1]
                s_ = lambda n: SW[(n, z)][0]
                sq = lambda n: SW[(n, z)][1]
                if (z == 0 and c % 4 == 0 and c > 0) or (z == 1 and c % 4 == 3 and c < 15):
                    TS(Sd[:, z, :], Sd[:, z, :], cfs("carry"), None, ALU.mult, rd=[Sd_r[z], r_c], wr=[Sd_r[z]])
                    CP(Sdb[:, z, :], Sd[:, z, :], rd=[Sd_r[z]], wr=[Sdb_r[z]], eng="act")
                    yield
                psKS, prKS, bKS = yield from getps()
                for h in range(4):
                    MM(psKS[0:64, h * 128:(h + 1) * 128], qkvT[:, 4 + h, cs_], Sdb[:, z, h * 128:(h + 1) * 128],
                       rd=[qkv_r[4 + h], Sdb_r[z]], wr=[prKS])
                yield
                TT(hd(s_("tmpr")[0:64, :]), hd(psKS[0:64, :]), o("sm")[0:64, 8:12].unsqueeze(2).to_broadcast([64, 4, 128]),
                   ALU.mult, rd=[prKS, oq("sm")], wr=[sq("tmpr")])
                relps(bKS)
                yield
                TT(s_("r")[0:64, :], s_("tmpr")[0:64, :], o("bv")[0:64, :], ALU.add, rd=[sq("tmpr"), oq("bv")], wr=[sq("r")])
                yield
                psV, prV, bV = yield from getps()
                for h in range(4):
                    MM(psV[0:64, h * 128:(h + 1) * 128], o("TTT")[0:64, 256 + h * 64:256 + (h + 1) * 64],
                       s_("r")[0:64, h * 128:(h + 1) * 128], rd=[oq("TTT"), sq("r")], wr=[prV])
                yield
                CP(s_("vn")[0:64, :], psV[0:64, :], rd=[prV], wr=[sq("vn")], eng="act")
                relps(bV)
                yield
                psO, prO, bO = yield from getps()
                for h in range(4):
                    MM(psO[:, h * 64:(h + 1) * 64], Sdb[:, z, h * 128:(h + 1) * 128], o("qte")[:, h * 64:(h + 1) * 64],
                       start=True, stop=False, rd=[Sdb_r[z], oq("qte")], wr=[prO])
                    MM(psO[:, h * 64:(h + 1) * 64], s_("vn")[:, h * 128:(h + 1) * 128], o("attT")[:, h * 64:(h + 1) * 64],
                       start=False, stop=True, rd=[sq("vn"), oq("attT")], wr=[prO])
                psS, prS, bS = yield from getps()
                for h in range(4):
                    MM(psS[:, h * 128:(h + 1) * 128], o("kd")[0:64, h * 128:(h + 1) * 128], s_("vn")[0:64, h * 128:(h + 1) * 128],
                       rd=[oq("kd"), sq("vn")], wr=[prS])
                yield
                TT(hd(Sd[:, z, :]), hd(Sd[:, z, :]), o("eGl").unsqueeze(2).to_broadcast([128, 4, 128]), ALU.mult,
                   rd=[Sd_r[z], oq("eGl")], wr=[Sd_r[z]])
                TT(Sd[:, z, :], psS[:], Sd[:, z, :], ALU.add, rd=[prS, Sd_r[z]], wr=[Sd_r[z]])
                relps(bS)
                yield
                CP(Sdb[:, z, :], Sd[:, z, :], rd=[Sd_r[z]], wr=[Sdb_r[z]], eng="act")
                ov = oacc[:, :, cs_]
                pv = h4(psO[:, 0:256])
                if not owritten[c]:
                    CP(ov, pv, rd=[prO], wr=[oacc_r[c]])
                    owritten[c] = True
                else:
                    TT(ov, pv, ov, ALU.add, rd=[prO, oacc_r[c]], wr=[oacc_r[c]])
                relps(bO)
                if (z == 0 and c % 4 == 3) or (z == 1 and c % 4 == 0):
                    CP(_s1, Sd[:, z, :], rd=[Sd_r[z]], wr=[_r1])
                    k = sso[0] % 2
                    sso[0] += 1
                    DMA("sp", sd_out[l, z, c // 4], _s1, rd=[_r1], sem=f"sso{k}")

            order = [list(range(16)), list(range(15, -1, -1))]
            pre_i = [0, 0]
            pre_done = [0, 0]
            ser_i = [0, 0]
            ser_done = [0, 0]
            active = []
            modg = [mod_gen(48, lag=4), mod_gen(48, lag=4)] if l == 0 else []
            while ser_done[0] < 16 or ser_done[1] < 16:
                for z in range(2):
                    n_pre = sum(1 for a_ in active if a_[1] == "pre" and a_[2] == z)
                    while (n_pre < NLN and pre_i[z] < 16 and pre_i[z] - ser_done[z] < NSL):
                        i_ = pre_i[z]
                        lanes_busy = [a_[4] for a_ in active if a_[1] == "pre" and a_[2] == z]
                        ln = 0 if 0 not in lanes_busy else 1
                        active.append([pre(z, order[z][i_], ln, i_ % NSL), "pre", z, i_, ln])
                        pre_i[z] += 1
                        n_pre += 1
                    if not any(a_[1] == "ser" and a_[2] == z for a_ in active) and ser_i[z] < 16 and pre_done[z] > ser_i[z]:
                        i_ = ser_i[z]
                        active.append([ser(z, order[z][i_], i_ % NSL), "ser", z, i_, -1])
                        ser_i[z] += 1
                for a_ in list(active):
                    try:
                        next(a_[0])
                    except StopIteration:
                        active.remove(a_)
                        if a_[1] == "pre":
                            pre_done[a_[2]] += 1
                        else:
                            ser_done[a_[2]] += 1
                for g_ in list(modg):
                    try:
                        next(g_)
                    except StopIteration:
                        modg.remove(g_)
            for g_ in modg:
                for _ in g_:
                    pass
            S.barrier()
            A2 = Arena(mark)
            mix_epilogue(A2, oacc, oacc_r, mixT, mix_r, gsm[:, 6 + l:7 + l])
            if l == 0:
                dump("mdn", mixT[:, 0, :], [mix_r[0][0], mix_r[0][1]], BF16)
            wout_apply(l, "wo_dn", 4, mixT, mix_r)

        for l in range(NL):
            S.barrier()
            A = Arena(HTW)
            rmsnorm_to_hT(l, 0, A)
            if l == 0:
                dump("h1", hT[:, 0, :], hT_r, BF16)
            S.barrier()
            if KSTOP == 'n1':
                break
            A = Arena(HTW)
            mixT = A.alloc(8 * T, BF16, "p (k t) -> p k t", k=8)
            mix_r = [[Res(), Res()] for _ in range(8)]
            qT = A.alloc(8 * T, BF16, "p (h t) -> p h t", h=8)
            qT_r = [[Res(), Res()] for _ in range(8)]
            kTa = A.alloc(2 * 1280, BF16, "p (h t) -> p h t", h=2)
            kTa_r = [Res(), Res()]
            vall = A.alloc(10 * 256, BF16, "p (t c) -> p t c", t=10)
            vall_r = Res()
            cs = A.alloc(2 * T, F32, "p (a t) -> p a t", a=2)
            r_cs = Res()
            atab = A.alloc(1280, BF16)
            btab = A.alloc(T, BF16)
            r_ab = Res()
            sq_ = A.alloc(512, BF16)
            rstd_ = A.alloc(512, F32)
            nslots = (sq_, Res(), rstd_, Res())
            qn = [A.alloc(512, F32) for _ in range(2)]
            qn_r = [Res(), Res()]
            qnb = [A.alloc(512, BF16) for _ in range(2)]
            qnb_r = [Res(), Res()]
            t1 = [A.alloc(512, F32) for _ in range(2)]
            t1_r = [Res(), Res()]
            t2 = [A.alloc(512, F32) for _ in range(2)]
            t2_r = [Res(), Res()]
            vst = [A.alloc(256, F32) for _ in range(2)]
            vst_r = [Res(), Res()]
            PT = [A.alloc(512, BF16) for _ in range(3)]
            PT_r = [Res() for _ in range(3)]
            rec = [A.alloc(512, F32) for _ in range(2)]
            rec_r = [Res(), Res()]
            if 'c' not in KSK:
                DMA("sp", cs, cs_in, wr=[r_cs], sem="cs")
            if 'a' not in KSK:
                DMA("pool", atab, atab_in, wr=[r_ab], sem="ab")
                DMA("pool", btab, btab_in, wr=[r_ab], sem="ab")
                DMA("pool", kTa[:, :, 1024:1280], ckT_in[l], wr=kTa_r, sem="ab")
                DMA("pool", vall[:, 8:10, :], cv_in[l], wr=[vall_r], sem="ab")
            sq2_ = A.alloc(512, BF16)
            rstd2_ = A.alloc(512, F32)
            nsl2 = [nslots, (sq2_, Res(), rstd2_, Res())]

            qk_cnt = [0]

            def qk_proj(kind, h, wt, wr_, par):
                for half in range(2):
                    b_ = par * 2 + half
                    proj_fm(wt, wr_, half, ps_t[b_], ps_r[b_])

            def tmp_ps():
                b_ = 4 + qk_cnt[0] % 3
                qk_cnt[0] += 1
                return ps_t[b_], ps_r[b_]

            def qk_iter(kind, h, half, par):
                hs = slice(half * 512, (half + 1) * 512)
                k = half
                ps, pr = ps_t[par * 2 + half], ps_r[par * 2 + half]
                sq, sq_r, rstd, rstd_r = nsl2[k]
                ACT(sq, ps[:], AF.Square, rd=[pr], wr=[sq_r])
                yield
                ps2, pr2 = tmp_ps()
                MM(ps2[:], ones_bf, sq, rd=[sq_r, r_cb], wr=[pr2])
                yield
                ACT(rstd, ps2[:], AF.Ln, rd=[pr2, r_eps], wr=[rstd_r], bias=epsb[:, 1:2])
                yield
                ACT(rstd, rstd, AF.Exp, rd=[rstd_r], wr=[rstd_r], scale=-0.5)
                yield
                gcol = gsm[:, l:l + 1] if kind == "aq" else gsm[:, 2 + l:3 + l]
                STT(qn[k], ps[:], gcol, rstd, ALU.mult, ALU.mult, rd=[pr, rstd_r, r_g], wr=[qn_r[k]])
                yield
                if kind == "ak" and 'k' not in KSK:
                    DMA("sp", kT_out[l, h, half], qn[k], rd=[qn_r[k]], sem=f"ko{k}")
                CP(qnb[k], qn[k], rd=[qn_r[k]], wr=[qnb_r[k]], eng="act")
                TT(t1[k], qn[k], cs[:, 0, hs], ALU.mult, rd=[qn_r[k], r_cs], wr=[t1_r[k]])
                yield
                ps3, pr3 = tmp_ps()
                MM(ps3[:], RmT_bf, qnb[k], rd=[qnb_r[k], r_cb], wr=[pr3])
                yield
                TT(t2[k], ps3[:], cs[:, 1, hs], ALU.mult, rd=[pr3, r_cs], wr=[t2_r[k]])
                yield
                if kind == "aq":
                    TT(qT[:, h, hs], t1[k], t2[k], ALU.add, rd=[t1_r[k], t2_r[k]], wr=[qT_r[h][half]])
                else:
                    TT(kTa[:, h, hs], t1[k], t2[k], ALU.add, rd=[t1_r[k], t2_r[k]], wr=[kTa_r[h]])

            heads = [("aq", h) for h in range(8)] + [("ak", h) for h in range(2)]
            wt0 = next_w((heads[0][0], l, heads[0][1]))
            qk_proj(heads[0][0], heads[0][1], wt0[0], wt0[1], 0)
            for hi, (kind, h) in enumerate(heads):
                if hi + 1 < len(heads):
                    kn, hn = heads[hi + 1]
                    wtn = next_w((kn, l, hn))
                    qk_proj(kn, hn, wtn[0], wtn[1], (hi + 1) % 2)
                run_il([qk_iter(kind, h, 0, hi % 2), qk_iter(kind, h, 1, hi % 2)]
                       + ([mod_gen(2, lag=3)] if (l == 0 and mod_next[0] < 48) else []))
            for h in range(2):
                wt, wr_ = next_w(("av", l, h))
                for tt in range(0 if 'v' in KSK else 8):
                    ps, pr = nextps()
                    for kc in range(KC):
                        MM(ps[:, 0:128], hT[:, kc, tt * 128:(tt + 1) * 128], wt[:, kc, :], start=(kc == 0),
                           stop=(kc == KC - 1), rd=[wr_, hT_r[tt // 4]], wr=[pr])
                    k = tt % 2
                    CP(vst[k][:, 0:128], ps[:, 0:128], rd=[pr], wr=[vst_r[k]], eng="act")
                    CP(vall[:, tt, h * 128:(h + 1) * 128], vst[k][:, 0:128], rd=[vst_r[k]], wr=[vall_r])
                    DMA("sp", v_out[l, h, tt], vst[k][:, 0:128], rd=[vst_r[k]], sem=f"vo{k}")
            if l == 0:
                dump("qT0", qT[:, 0, :], [qT_r[0][0], qT_r[0][1]], BF16)
                dump("kT0", kTa[:, 0, :], kTa_r, BF16)
            if KSTOP == 'ap':
                break
            items = [(hq, half, kt) for hq in range(8) for half in range(2) for kt in range(10)]
            sbank = [0, 1, 2]
            obank = [(3, 4), (5, 6)]

            def s_stage(i):
                hq, half, kt = items[i]
                kv = hq // 4
                hs = slice(half * 512, (half + 1) * 512)
                psS, prS = ps_t[sbank[i % 3]], ps_r[sbank[i % 3]]
                MM(psS[:], kTa[:, kv, kt * 128:(kt + 1) * 128], qT[:, hq, hs], start=True, stop=False,
                   rd=[kTa_r[kv], qT_r[hq][half]], wr=[prS])
                MM(psS[:], atab[:, kt * 128:(kt + 1) * 128], btab[:, hs], start=False, stop=True,
                   rd=[r_ab], wr=[prS])
                ACT(PT[i % 3], psS[:], AF.Exp, rd=[prS], wr=[PT_r[i % 3]])

            def pv_stage(i):
                hq, half, kt = items[i]
                kv = hq // 4
                hs = slice(half * 512, (half + 1) * 512)
                g_ = (i // 10) % 2
                psO, prO = ps_t[obank[g_][0]], ps_r[obank[g_][0]]
                psD, prD = ps_t[obank[g_][1]], ps_r[obank[g_][1]]
                p = i % 3
                MM(psO[:], vall[:, kt, kv * 128:(kv + 1) * 128], PT[p], start=(kt == 0), stop=(kt == 9),
                   rd=[vall_r, PT_r[p]], wr=[prO])
                MM(psD[:], ones_bf, PT[p], start=(kt == 0), stop=(kt == 9), rd=[PT_r[p], r_cb], wr=[prD])
                if kt == 9:
                    S.op("dve", (lambda o_, i_: (lambda e: e.reciprocal(out=o_, in_=i_)))(rec[g_], psD[:]),
                         [prD], [rec_r[g_]])
                    TT(mixT[:, hq, hs], psO[:], rec[g_], ALU.mult, rd=[prO, rec_r[g_]], wr=[mix_r[hq][half]])

            for i in range(len(items) + 1):
                if i < len(items):
                    s_stage(i)
                if i >= 1:
                    pv_stage(i - 1)
                pass
            if l == 0:
                dump("matt", mixT[:, 0, :], [mix_r[0][0], mix_r[0][1]], BF16)
            if l == 0:
                pump_mod(0, 48 - mod_next[0])
            wout_apply(l, "wo_att", 8, mixT, mix_r)
            S.barrier()
            if KSTOP == "att":
                break
            gla_phase(l)
            S.barrier()
            if KSTOP in ("gla", "gp"):
                break
            dn_phase(l)
            S.barrier()
            if KSTOP == "dn":
                break
            A = Arena(HTW)
            if l == 0:
                pump_mod(0, 96)
            rmsnorm_to_hT(l, 1, A)
            S.barrier()
            A = Arena(HTW)
            actT = A.alloc(16 * T, BF16, "p (k t) -> p k t", k=16)
            act_r = [[Res(), Res()] for _ in range(16)]
            rl = [A.alloc(512, F32) for _ in range(2)]
            rl_r = [Res(), Res()]
            ri = 0
            for g in range(4):
                for j in range(16):
                    if l == 0:
                        pump_mod(1, 1)
                    wt, wr_ = next_w(("ff1", l, g, j))
                    for half in range(2):
                        hs = slice(half * 512, (half + 1) * 512)
                        ps, pr = nextps()
                        proj_fm(wt, wr_, half, ps, pr)
                        k = ri % 2
                        ri += 1
                        ACT(rl[k], ps[:], AF.Relu, rd=[pr], wr=[rl_r[k]])
                        TT(actT[:, j, hs], rl[k], rl[k], ALU.mult, rd=[rl_r[k]], wr=[act_r[j][half]])
                for dc in range(16):
                    if l == 0:
                        pump_mod(1, 1)
                    wt, wr_ = next_w(("ff2", l, g, dc))
                    for half in range(2):
                        hs = slice(half * 512, (half + 1) * 512)
                        ps, pr = nextps()
                        for j in range(16):
                            MM(ps[:], wt[:, j, :], actT[:, j, hs], start=(j == 0), stop=(j == 15),
                               rd=[wr_, act_r[j][half]], wr=[pr])
                        STT(xT[:, dc, hs], ps[:], mods[:, l, 5 * 16 + dc: 5 * 16 + dc + 1], xT[:, dc, hs],
                            ALU.mult, ALU.add, rd=[pr, mod_r[l][5], xT_r[dc][half]], wr=[xT_r[dc][half]])

        S.barrier()
        A = Arena(0)
        yst = [A.alloc(D, F32) for _ in range(2)]
        yst_r = [Res(), Res()]
        for tt in range(8):
            sl = tt % 2
            half = tt // 4
            for c4 in range(4):
                ps, pr = nextps()
                for j in range(4):
                    c = c4 * 4 + j
                    TR(ps[:, j * 128:(j + 1) * 128], xT[:, c, tt * 128:(tt + 1) * 128], ident_f,
                       rd=[xT_r[c][half], r_c], wr=[pr])
                eng = "act" if c4 % 2 else "dve"
                CP(yst[sl][:, c4 * 512:(c4 + 1) * 512], ps[:], rd=[pr], wr=[yst_r[sl]], eng=eng)
            DMA("sp", y_out[tt * 128:(tt + 1) * 128, :], yst[sl], rd=[yst_r[sl]], sem=f"yo{sl}")
        S.wait_all_dma("sp")
        stats = S.emit(nc, st)
        print("ops", stats, "weights", wi_[0], "/", len(plan))
    assert len(plan_rec) == len(plan) and offs_rec == woffs, (len(plan_rec), len(plan))
    return nc, list(dbg_outs), plan_rec


_CACHE = {}
LAST_DBG = {}


def _host_consts():
    t = np.arange(64)
    f32 = np.float32
    TriL = (t[:, None] <= t[None, :]).astype(f32)
    TriU = (t[:, None] >= t[None, :]).astype(f32)
    TriCL = (t[:, None] > t[None, :]).astype(f32)
    TriCU = (t[:, None] < t[None, :]).astype(f32)
    NEG = f32(-1e4)
    i_, j_ = t[:, None], t[None, :]
    d = dict(TriL=TriL, TriU=TriU, ones=np.ones((64, 128), f32), TriCL=TriCL, TriCU=TriCU,
             TriS0=-TriL / 16.0, TriS1=-TriU / 16.0,
             mbS0=np.where(j_ < i_, 0, NEG).astype(f32), mbS1=np.where(j_ > i_, 0, NEG).astype(f32),
             mbIT0=np.where(i_ <= j_, 0, NEG).astype(f32), mbIT1=np.where(i_ >= j_, 0, NEG).astype(f32),
             mT0=(i_ <= j_).astype(f32), mT1=(i_ >= j_).astype(f32),
             negones=-np.ones((64, 64), f32), nTriL=-TriL, nTriU=-TriU, ident=np.eye(64, dtype=f32))
    return d


def _mz_const():
    ii = np.arange(64)
    ML = np.zeros((64, 6, 64), np.float32)
    for m in range(6):
        b = 1 << m
        same = (ii[:, None] // (2 * b)) == (ii[None, :] // (2 * b))
        ML[:, m, :] = (same & ((ii[:, None] % (2 * b)) >= b) & ((ii[None, :] % (2 * b)) < b)).astype(np.float32)
    MU = ML.transpose(2, 1, 0)
    mz = np.zeros((64, 2, 6, 2, 64), np.float32)
    mz[:, 0, :, 0], mz[:, 0, :, 1] = ML, MU
    mz[:, 1, :, 0], mz[:, 1, :, 1] = MU, ML
    return np.ascontiguousarray(mz.reshape(64, 2 * 6 * 128))


def kernel(**inp):
    f32 = np.float32
    inp = {k: np.asarray(v) for k, v in inp.items()}
    if "nc" not in _CACHE:
        _CACHE["nc"] = build_program()
    nc, dbg_names, plan = _CACHE["nc"]
    _p0, woffs = weight_plan()
    warr = {a: np.zeros(n, f32) for a, n in woffs.items()}
    for (a, key, src, l, row0, nk, col0, ncols, off) in plan:
        W = inp[src][l]
        blk = W[row0:row0 + nk * 128, col0:col0 + ncols].reshape(nk, 128, ncols).transpose(1, 0, 2)
        dst = warr[a][off: off + 128 * nk * 128].reshape(128, nk, 128)
        dst[:, :, :ncols] = blk
    hc = _host_consts()
    tpos = np.arange(T)
    inv = (np.float32(10000.0) ** (-np.arange(0, 64, 2, dtype=f32) / np.float32(64))).astype(f32)
    ang = np.zeros((128, T), f32)
    for dd in range(128):
        pos = (tpos // 64) if dd < 64 else (tpos % 64)
        ang[dd] = pos.astype(f32) * inv[dd % 32]
    cos_s, sin_s = np.cos(ang).astype(f32), np.sin(ang).astype(f32)
    RmT = np.zeros((128, 128), f32)
    for dd in range(128):
        if dd % 64 < 32:
            RmT[dd + 32, dd] = -1.0
        else:
            RmT[dd - 32, dd] = 1.0
    w2aug = np.zeros((2, 33, 512), f32)
    for l in range(2):
        for z in range(2):
            w2aug[l, z * 16:(z + 1) * 16, z * 256:(z + 1) * 256] = inp["gla_w2"][l, z]
            w2aug[l, 32, z * 256:(z + 1) * 256] = inp["gla_b"][l, z]
    in_maps = []
    for core in range(8):
        ctx = core < 4
        m = dict(warr)
        if ctx:
            m["x"] = np.ascontiguousarray(inp["x_prompt"][4 * core:4 * core + 4].reshape(T, D))
            cond = inp["c_ctx"]
        else:
            b = core - 4
            m["x"] = np.ascontiguousarray(inp["x_sample"][b])
            cond = inp["c"][b]
        cfa = np.zeros((128, NCF), f32)

        def put(name, arr):
            o, w = CF[name]
            cfa[:, o:o + w] = arr
        put("ident", np.eye(128, dtype=f32))
        put("cond", cond.reshape(16, 128).T)
        for l in range(2):
            put(f"bmod{l}", inp["b_mod"][l].reshape(96, 128).T)
            put(f"n1g{l}", inp["norm1_g"][l].reshape(16, 128).T)
            put(f"n2g{l}", inp["norm2_g"][l].reshape(16, 128).T)
            put(f"glag{l}", inp["gla_norm_g"][l][:, None])
            put(f"qg{l}", inp["q_norm_g"][l][:, None])
            put(f"kg{l}", inp["k_norm_g"][l][:, None])
            put(f"dng{l}", inp["dn_norm_g"][l][:, None])
            put(f"conv{l}", inp["dn_conv"][l].reshape(3, 12, 128).transpose(2, 1, 0).reshape(128, 36))
        put("carry", 0.0 if ctx else 1.0)
        put("cflag", 1.0 if ctx else 0.0)
        put("onesf", 1.0)
        put("identb", np.eye(128, dtype=f32))
        put("Rm", RmT)
        m["cf"] = cfa
        c64a = np.zeros((64, NC64), f32)
        for name, arr in hc.items():
            o, w = C64[name]
            c64a[:, o:o + w] = arr
        for l in range(2):
            o, w = C64[f"alog{l}"]
            c64a[:, o:o + w] = inp["dn_a_log"][l].reshape(8)[None, :]
            o, w = C64[f"dtb{l}"]
            c64a[:, o:o + w] = inp["dn_dt_bias"][l].reshape(8)[None, :]
        m["c64"] = c64a
        m["mz"] = _mz_const()
        m["w2aug"] = w2aug
        cs = np.zeros((128, 2, T), f32)
        at = np.zeros((128, 1280), f32)
        bt = np.zeros((128, T), f32)
        at[4, 1024:] = 1.0
        if ctx:
            cs[:, 0, :] = 1.0
            for s in range(4):
                at[s, s * 256:(s + 1) * 256] = 1.0
                bt[s, :] = -30000.0
                bt[s, s * 256:(s + 1) * 256] = 0.0
            bt[4, :] = -30000.0
            m["ckT"] = np.zeros((2, 128, 2, 256), f32)
            m["cv"] = np.zeros((2, 128, 2, 256), f32)
            m["sgla"] = np.zeros((2, 64, 2, 512), f32)
            m["sdn"] = np.zeros((2, 128, 2, 512), f32)
        else:
            b = core - 4
            cs[:, 0, :] = cos_s
            cs[:, 1, :] = sin_s
            at[0, :1024] = 1.0
            m["ckT"] = np.ascontiguousarray(inp["cache_k"][b].transpose(0, 3, 2, 1))
            m["cv"] = np.ascontiguousarray(
                inp["cache_v"][b].reshape(2, 2, 128, 256).transpose(0, 2, 1, 3))
            m["sgla"] = np.ascontiguousarray(inp["state_gla"][b].transpose(0, 3, 1, 2, 4)).reshape(2, 64, 2, 512)
            m["sdn"] = np.ascontiguousarray(inp["state_dn"][b].transpose(0, 3, 1, 2, 4)).reshape(2, 128, 2, 512)
        m["cossin"] = cs
        m["atab"] = at
        m["btab"] = bt
        in_maps.append(m)
    kc_ = os.environ.get("KCORES", "")
    if kc_:
        sel = [int(s) for s in kc_.split(",")]
        res = run_bass_kernel_spmd(nc, [in_maps[c] for c in sel], core_ids=list(range(len(sel))))
        R = [res.results[sel.index(c)] if c in sel else res.results[0] for c in range(8)]
    else:
        res = run_bass_kernel_spmd(nc, in_maps, core_ids=list(range(8)))
        R = res.results
    for n in dbg_names:
        LAST_DBG[n] = [np.asarray(R[c]["dbg_" + n]) for c in range(8)]
    y_prompt = np.stack([np.asarray(R[c]["y"]) for c in range(4)]).reshape(16, 256, D).astype(f32)
    y_sample = np.stack([np.asarray(R[c]["y"]) for c in range(4, 8)]).astype(f32)
    kT = np.stack([np.asarray(R[c]["kTo"]) for c in range(4)])
    kT = kT.transpose(0, 1, 2, 4, 3, 5).reshape(4, 2, 2, 128, 4, 256)
    nk = kT.transpose(0, 4, 1, 5, 2, 3).reshape(16, 2, 256, 2, 128)
    vo = np.stack([np.asarray(R[c]["vo"]) for c in range(4)])
    vo = vo.reshape(4, 2, 2, 4, 256, 128)
    nv = vo.transpose(0, 3, 1, 4, 2, 5).reshape(16, 2, 256, 2, 128)
    sgo = np.stack([np.asarray(R[c]["sgo"]) for c in range(4)])
    nsg = sgo.reshape(4, 2, 2, 4, 64, 4, 128).transpose(0, 3, 1, 2, 5, 4, 6).reshape(16, 2, 2, 4, 64, 128)
    sdo = np.stack([np.asarray(R[c]["sdo"]) for c in range(4)])
    nsd = sdo.reshape(4, 2, 2, 4, 128, 4, 128).transpose(0, 3, 1, 2, 5, 4, 6).reshape(16, 2, 2, 4, 128, 128)
    return (y_prompt, y_sample, np.ascontiguousarray(nk, dtype=f32), np.ascontiguousarray(nv, dtype=f32),
            np.ascontiguousarray(nsg, dtype=f32), np.ascontiguousarray(nsd, dtype=f32))
```

```python
import os
import numpy as np
from contextlib import ExitStack
import concourse.bass as bass
import concourse.mybir as mybir
from concourse.bass_utils import run_bass_kernel_spmd

F32 = mybir.dt.float32
BF16 = mybir.dt.bfloat16
AF = mybir.ActivationFunctionType
ALU = mybir.AluOpType

T = 1024
D = 2048
KC = 16
NCH = 16
EPS = 1e-6
NSLOT = 3
ENGS = ("pe", "act", "dve", "pool", "sp")
SEM_CAP = 30000
KDBG = os.environ.get("KDBG", "")
KSTOP = os.environ.get("KSTOP", "")
KSK = os.environ.get("KSK", "")
KGS = os.environ.get("KGS", "")
LAZYMOD = True
CHD = F32 if os.environ.get("KCHAIN", "bf16") == "f32" else BF16


class Res:
    __slots__ = ("w", "rs")

    def __init__(self):
        self.w = None
        self.rs = []


class Op:
    __slots__ = ("eng", "idx", "fn", "waits", "signal", "clock", "dma", "sig_no")

    def __init__(self, eng, idx, fn):
        self.eng, self.idx, self.fn = eng, idx, fn
        self.waits = []
        self.signal = False
        self.clock = None
        self.dma = None
        self.sig_no = None


class Sched:
    def __init__(self):
        self.ops = {e: [] for e in ENGS}
        self.clock = {e: {} for e in ENGS}
        self.dma_val = {}
        self.dma_clock = {}

    def _need(self, eng, dep, same_ok):
        ck = self.clock[eng]
        if dep[0] == "e":
            if dep[1] == eng and same_ok:
                return False
            return ck.get(dep[1], -1) < dep[2]
        return ck.get(("d", dep[1]), 0) < dep[2]

    def _merge(self, eng, dep):
        ck = self.clock[eng]
        if dep[0] == "e":
            op2 = self.ops[dep[1]][dep[2]]
            op2.signal = True
            src = op2.clock
            if ck.get(dep[1], -1) < dep[2]:
                ck[dep[1]] = dep[2]
        else:
            src = self.dma_clock[(dep[1], dep[2])]
            ck[("d", dep[1])] = dep[2]
        for k, v in src.items():
            if ck.get(k, -1) < v:
                ck[k] = v

    def op(self, eng, fn, reads=(), writes=(), dma_sem=None):
        lst = self.ops[eng]
        o = Op(eng, len(lst), fn)
        deps = []
        for r in reads:
            if r.w is not None:
                deps.append((r.w, False))
        for w in writes:
            if w.w is not None:
                deps.append((w.w, True))
            for d in w.rs:
                deps.append((d, True))
        agg = {}
        for dep, same_ok in deps:
            if dep[0] == "e":
                if dep[1] == eng and eng == "pe":
                    continue
                k = ("e", dep[1])
            else:
                k = ("d", dep[1])
            if k not in agg or agg[k][2] < dep[2]:
                agg[k] = dep
        for dep in agg.values():
            if self._need(eng, dep, False):
                o.waits.append(dep)
                self._merge(eng, dep)
        o.clock = dict(self.clock[eng])
        lst.append(o)
        if dma_sem is not None:
            v = self.dma_val.get(dma_sem, 0) + 16
            self.dma_val[dma_sem] = v
            o.dma = (dma_sem, v)
            self.dma_clock[(dma_sem, v)] = dict(o.clock)
            me = ("d", dma_sem, v)
        else:
            me = ("e", eng, o.idx)
        for r in reads:
            r.rs.append(me)
        for w in writes:
            w.w = me
            w.rs = []
        return o

    def barrier(self, engs=("pe", "act", "dve", "sp", "pool")):
        last = {}
        for e in engs:
            for o in reversed(self.ops[e]):
                if o.fn is not None and o.dma is None:
                    last[e] = o.idx
                    break
        dmas = dict(self.dma_val)
        for e in engs:
            lst = self.ops[e]
            o = Op(e, len(lst), None)
            for e2, i2 in last.items():
                dep = ("e", e2, i2)
                if self._need(e, dep, False):
                    o.waits.append(dep)
                    self._merge(e, dep)
            for sk, v in dmas.items():
                if sk.startswith("w"):
                    continue
                dep = ("d", sk, v)
                if self._need(e, dep, False):
                    o.waits.append(dep)
                    self._merge(e, dep)
            o.clock = dict(self.clock[e])
            lst.append(o)

    def wait_all_dma(self, eng="sp"):
        lst = self.ops[eng]
        o = Op(eng, len(lst), None)
        for sk, v in self.dma_val.items():
            o.waits.append(("d", sk, v))
        o.clock = dict(self.clock[eng])
        lst.append(o)

    def emit(self, nc, stack):
        nsig = {}
        for e in ENGS:
            n = 0
            for o in self.ops[e]:
                if o.signal:
                    n += 1
                    o.sig_no = n
            nsig[e] = n
        esems = {}
        for e in ENGS:
            k = max(1, (nsig[e] + SEM_CAP - 1) // SEM_CAP)
            esems[e] = [stack.enter_context(nc.semaphore(f"s_{e}{i}")) for i in range(k)]
        dsems = {sk: stack.enter_context(nc.semaphore(f"d_{sk}")) for sk in self.dma_val}
        ops = self.ops

        def sem_of(e, signo):
            return esems[e][(signo - 1) // SEM_CAP], (signo - 1) % SEM_CAP + 1

        def run(e, engobj):
            for o in ops[e]:
                for dep in o.waits:
                    if dep[0] == "e":
                        s, v = sem_of(dep[1], ops[dep[1]][dep[2]].sig_no)
                        engobj.wait_ge(s, v)
                    else:
                        engobj.wait_ge(dsems[dep[1]], dep[2])
                if o.fn is None:
                    continue
                ins = o.fn(engobj)
                if o.dma is not None:
                    ins.then_inc(dsems[o.dma[0]], 16)
                elif o.signal:
                    s, _ = sem_of(e, o.sig_no)
                    ins.then_inc(s, 1)

        block = stack.enter_context(nc.Block())

        @block.tensor
        def _(eng):
            run("pe", eng)

        @block.scalar
        def _(eng):
            run("act", eng)

        @block.vector
        def _(eng):
            run("dve", eng)

        @block.gpsimd
        def _(eng):
            run("pool", eng)

        @block.sync
        def _(eng):
            run("sp", eng)
        return {e: len(ops[e]) for e in ENGS}, nsig


W_IN_OFF = dict(gq=0, gk=256, gv=512, gr=1024, glr=1536, aq=1568, ak=2592, av=2848,
                dqkv=3104, dz=4640, dab=5152)


def weight_plan():
    plan = []
    for l in range(2):
        for j in range(96):
            plan.append(("wm", ("mod", l, j), "w_mod", l, 0, 16, j * 128, 128))
    for l in range(2):
        a = f"w{l}"

        def wi(key, col0, ncols=128):
            plan.append((a, key, "w_in", l, 0, 16, col0, ncols))

        def wo(key, row0, nk, dc):
            plan.append((a, key, "w_out", l, row0, nk, dc * 128, 128))
        for h in range(8):
            wi(("aq", l, h), W_IN_OFF["aq"] + h * 128)
        for h in range(2):
            wi(("ak", l, h), W_IN_OFF["ak"] + h * 128)
        for h in range(2):
            wi(("av", l, h), W_IN_OFF["av"] + h * 128)
        for dc in range(16):
            wo(("wo_att", l, dc), 512, 8, dc)
        for p in range(2):
            wi(("gq", l, p), W_IN_OFF["gq"] + p * 128)
        for p in range(2):
            wi(("gk", l, p), W_IN_OFF["gk"] + p * 128)
        for h in range(4):
            wi(("gr", l, h), W_IN_OFF["gr"] + h * 128)
        wi(("glr", l), W_IN_OFF["glr"], 32)
        for h in range(4):
            wi(("gv", l, h), W_IN_OFF["gv"] + h * 128)
        for dc in range(16):
            wo(("wo_gla", l, dc), 0, 4, dc)
        for cc in range(12):
            wi(("dqkv", l, cc), W_IN_OFF["dqkv"] + cc * 128)
        wi(("dab", l), W_IN_OFF["dab"], 16)
        for h in range(4):
            wi(("dz", l, h), W_IN_OFF["dz"] + h * 128)
        for dc in range(16):
            wo(("wo_dn", l, dc), 1536, 4, dc)
        for g in range(4):
            for j in range(16):
                plan.append((a, ("ff1", l, g, j), "w_ff1", l, 0, 16, (g * 16 + j) * 128, 128))
            for dc in range(16):
                plan.append((a, ("ff2", l, g, dc), "w_ff2", l, g * 2048, 16, dc * 128, 128))
    offs = {}
    out = []
    for (a, key, src, l, row0, nk, col0, ncols) in plan:
        off = offs.get(a, 0)
        out.append((a, key, src, l, row0, nk, col0, ncols, off))
        offs[a] = off + 128 * nk * 128
    return out, offs


CF = {}
_o = 0
for _n, _w in [("ident", 128), ("cond", 16), ("bmod0", 96), ("bmod1", 96), ("n1g0", 16), ("n1g1", 16),
               ("n2g0", 16), ("n2g1", 16), ("glag0", 1), ("glag1", 1), ("qg0", 1), ("qg1", 1),
               ("kg0", 1), ("kg1", 1), ("dng0", 1), ("dng1", 1), ("conv0", 36), ("conv1", 36),
               ("carry", 1), ("cflag", 1), ("onesf", 128), ("identb", 128), ("Rm", 128)]:
    CF[_n] = (_o, _w)
    _o += _w
NCF = _o
C64 = {}
_o = 0
for _n, _w in [("TriL", 64), ("TriU", 64), ("ones", 128), ("TriCL", 64), ("TriCU", 64), ("TriS0", 64),
               ("TriS1", 64), ("mbS0", 64), ("mbS1", 64), ("mbIT0", 64), ("mbIT1", 64), ("mT0", 64),
               ("mT1", 64), ("alog0", 8), ("alog1", 8), ("dtb0", 8), ("dtb1", 8), ("negones", 64),
               ("nTriL", 64), ("nTriU", 64), ("ident", 64)]:
    C64[_n] = (_o, _w)
    _o += _w
NC64 = _o


def build_program():
    nc = bass.Bass("TRN2", target_bir_lowering=False)
    plan, woffs = weight_plan()
    S = Sched()
    dbg_outs = {}

    def din(name, shape, dt=F32):
        return nc.dram_tensor(name, list(shape), dt, kind="ExternalInput").ap()

    def dout(name, shape, dt=F32):
        return nc.dram_tensor(name, list(shape), dt, kind="ExternalOutput").ap()

    wdram = {a: din(a, [n]) for a, n in woffs.items()}
    x_in = din("x", [T, D])
    cf_in = din("cf", [128, NCF])
    c64_in = din("c64", [64, NC64])
    mz_in = din("mz", [64, 2 * 6 * 128])
    w2aug_in = din("w2aug", [2, 33, 512])
    cs_in = din("cossin", [128, 2, T])
    atab_in = din("atab", [128, 1280])
    btab_in = din("btab", [128, T])
    ckT_in = din("ckT", [2, 128, 2, 256])
    cv_in = din("cv", [2, 128, 2, 256])
    sg_in = din("sgla", [2, 64, 2, 512])
    sd_in = din("sdn", [2, 128, 2, 512])
    y_out = dout("y", [T, D])
    kT_out = dout("kTo", [2, 2, 2, 128, 512])
    v_out = dout("vo", [2, 2, 8, 128, 128])
    sg_out = dout("sgo", [2, 2, 4, 64, 512])
    sd_out = dout("sdo", [2, 2, 4, 128, 512])

    with ExitStack() as st:
        def sbt(name, shape, dt):
            return st.enter_context(nc.sbuf_tensor(name, list(shape), dt))

        xT = sbt("xT", [128, KC, T], F32)
        wring = sbt("wring", [128, NSLOT, 16, 128], BF16)
        cf = sbt("cf_sb", [128, NCF], F32)
        c64 = sbt("c64_sb", [64, NC64], F32)
        cb = sbt("cb", [128, 3, 128], BF16)
        c64b = sbt("c64b", [64, 128], BF16)
        mzb_t = sbt("mzb", [64, 2 * 6 * 128], BF16)
        mzb = mzb_t[:].rearrange("p (z m v j) -> p z m v j", z=2, m=6, v=2)
        mods = sbt("mods", [128, 2, 96], F32)
        geff = sbt("geff", [128, 2, 2, 16], F32)
        gsm = sbt("gsm", [128, 16], F32)
        s_bf = sbt("s_bf", [128, 16], BF16)
        ARW = (nc.sbuf_bytes_remaining - 2048) // 4
        big = sbt("big", [128, ARW], F32)
        ps_t = [st.enter_context(nc.psum_tensor(f"ps{i}", [128, 512], F32)) for i in range(8)]
        ps_r = [Res() for _ in range(8)]
        ps_i = [0]

        def nextps():
            i = ps_i[0] % 7
            ps_i[0] += 1
            return ps_t[i], ps_r[i]

        class Arena:
            def __init__(self, base_words):
                self.p = base_words

            def alloc(self, nelem, dt, pat=None, **kw):
                sz = 4 if dt == F32 else 2
                nw = (nelem * sz + 3) // 4
                assert self.p + nw <= ARW, ("arena overflow", self.p + nw, ARW)
                ap = big[:, self.p:self.p + nw]
                self.p += nw
                if dt != F32:
                    ap = ap.bitcast(dt)
                    if ap.shape[1] != nelem:
                        ap = ap[:, 0:nelem]
                if pat:
                    ap = ap.rearrange(pat, **kw)
                return ap
        HTW = KC * T // 2
        hT = big[:, 0:HTW].bitcast(BF16).rearrange("p (k t) -> p k t", k=KC)

        def MM(out, lhsT, rhs, start=True, stop=True, rd=(), wr=()):
            S.op("pe", lambda e: e.matmul(out, lhsT=lhsT, rhs=rhs, start=start, stop=stop), rd, wr)

        def TR(out, in_, ident, rd=(), wr=()):
            S.op("pe", lambda e: e.transpose(out, in_, ident), rd, wr)

        def ACT(out, in_, func, rd=(), wr=(), scale=1.0, bias=0.0):
            S.op("act", lambda e: e.activation(out=out, in_=in_, func=func, bias=bias, scale=scale), rd, wr)

        def TT(out, in0, in1, op, rd=(), wr=(), eng="dve"):
            S.op(eng, lambda e: e.tensor_tensor(out=out, in0=in0, in1=in1, op=op), rd, wr)

        def TS(out, in0, s1, s2, op0, op1=None, rd=(), wr=(), eng="dve"):
            if op1 is None:
                S.op(eng, lambda e: e.tensor_scalar(out=out, in0=in0, scalar1=s1, scalar2=None, op0=op0), rd, wr)
            else:
                S.op(eng, lambda e: e.tensor_scalar(out=out, in0=in0, scalar1=s1, scalar2=s2, op0=op0, op1=op1), rd, wr)

        def STT(out, in0, scalar, in1, op0, op1, rd=(), wr=(), eng="dve"):
            S.op(eng, lambda e: e.scalar_tensor_tensor(out=out, in0=in0, scalar=scalar, in1=in1, op0=op0, op1=op1), rd, wr)

        def CP(out, in_, rd=(), wr=(), eng="dve"):
            if eng == "act":
                S.op("act", lambda e: e.copy(out=out, in_=in_), rd, wr)
            else:
                S.op(eng, lambda e: e.tensor_copy(out=out, in_=in_), rd, wr)

        def DMA(q, out, in_, rd=(), wr=(), sem="init"):
            S.op(q, lambda e: e.dma_start(out=out, in_=in_), rd, wr, dma_sem=sem)

        def dump(name, ap, rd, dt=F32):
            if name not in KDBG.split(","):
                return
            o = dout("dbg_" + name, list(ap.shape), dt)
            dbg_outs[name] = True
            DMA("sp", o, ap, rd=rd, sem="dbg")

        def cfs(name, a=None, b=None):
            o, w = CF[name]
            return cf[:, o + (a or 0): o + (w if b is None else b)]

        def c6(name, a=None, b=None, rows=64):
            o, w = C64[name]
            return c64[0:rows, o + (a or 0): o + (w if b is None else b)]

        r_c = Res()
        epsb = sbt("epsb", [128, 4], F32)
        r_eps = Res()
        for _i, _v in enumerate((D * EPS, 128.0 * EPS, EPS, 1.0)):
            S.op("dve", (lambda t_, v_: (lambda e: e.memset(t_, v_)))(epsb[:, _i:_i + 1], float(_v)), [], [r_eps])
        EPSCOL = {float(D * EPS): 0, float(128.0 * EPS): 1, float(EPS): 2}

        def RSTD(out, in_, eps_tot, rd, wr):
            c_ = EPSCOL[float(eps_tot)]
            ACT(out, in_, AF.Ln, rd=list(rd) + [r_eps], wr=wr, bias=epsb[:, c_:c_ + 1])
            ACT(out, out, AF.Exp, rd=wr, wr=wr, scale=-0.5)
        wres = [Res() for _ in range(NSLOT)]
        wi_ = [0]

        plan_by_key = {e[1]: e for e in plan}
        plan_rec = []
        offs_rec = {}

        def next_w(key):
            i = wi_[0]
            wi_[0] += 1
            a, k2, src, l, row0, nk, col0, ncols, _off = plan_by_key[key]
            off = offs_rec.get(a, 0)
            offs_rec[a] = off + 128 * nk * 128
            plan_rec.append((a, key, src, l, row0, nk, col0, ncols, off))
            slot = i % NSLOT
            srcap = wdram[a][off: off + 128 * nk * 128].rearrange("(p k n) -> p k n", p=128, k=nk)
            DMA("pool", wring[:, slot, 0:nk, :], srcap, wr=[wres[slot]], sem=f"w{slot}")
            return wring[:, slot], wres[slot]

        DMA("sp", cf[:], cf_in, wr=[r_c])
        DMA("sp", c64[:], c64_in, wr=[r_c])
        r_cb = Res()
        CP(cb[:, 0, :], cfs("onesf"), rd=[r_c], wr=[r_cb])
        CP(cb[:, 1, :], cfs("identb"), rd=[r_c], wr=[r_cb])
        CP(cb[:, 2, :], cfs("Rm"), rd=[r_c], wr=[r_cb])
        CP(c64b[:, 0:64], c6("mT0"), rd=[r_c], wr=[r_cb])
        CP(c64b[:, 64:128], c6("mT1"), rd=[r_c], wr=[r_cb])
        DMA("pool", mzb_t[:], mz_in, wr=[r_cb], sem="mz")
        ones_bf = cb[:, 0, :]
        ident_bf = cb[:, 1, :]
        RmT_bf = cb[:, 2, :]
        ident_f = cfs("ident")

        xT_r = [[Res(), Res()] for _ in range(KC)]
        A = Arena(HTW)
        xin = [A.alloc(D, F32) for _ in range(2)]
        xin_r = [Res(), Res()]
        for tt in range(8):
            sl = tt % 2
            DMA("sp", xin[sl], x_in[tt * 128:(tt + 1) * 128, :], wr=[xin_r[sl]], sem=f"xin{sl}")
            for c4 in range(4):
                ps, pr = nextps()
                for j in range(4):
                    c = c4 * 4 + j
                    TR(ps[:, j * 128:(j + 1) * 128], xin[sl][:, c * 128:(c + 1) * 128], ident_f,
                       rd=[xin_r[sl], r_c], wr=[pr])
                half = tt // 4
                eng = "act" if c4 % 2 else "dve"
                CP(xT[:, c4 * 4:(c4 + 1) * 4, tt * 128:(tt + 1) * 128],
                   ps[:].rearrange("p (j t) -> p j t", j=4), rd=[pr],
                   wr=[xT_r[c4 * 4 + j][half] for j in range(4)], eng=eng)

        NL = 0 if KSTOP in ('x', 'mod') else 2
        NMOD = 0 if KSTOP == 'x' else 2
        r_s = Res()
        sg_t = A.alloc(16, F32)
        ACT(sg_t, cfs("cond"), AF.Silu, rd=[r_c], wr=[r_s])
        CP(s_bf[:], sg_t, rd=[r_s], wr=[r_s])
        mod_r = [[Res() for _ in range(6)] for _ in range(2)]
        geff_r = [[Res(), Res()] for _ in range(2)]
        psM, prM = ps_t[7], ps_r[7]
        mod_next = [0, 0]

        def mod_finalize(l, m):
            TT(mods[:, l, m * 16:(m + 1) * 16], psM[:, l * 96 + m * 16: l * 96 + (m + 1) * 16],
               cfs(f"bmod{l}", m * 16, (m + 1) * 16), ALU.add, rd=[prM, r_c], wr=[mod_r[l][m]])
            if m in (1, 4):
                ni = 0 if m == 1 else 1
                gn = f"n1g{l}" if m == 1 else f"n2g{l}"
                STT(geff[:, l, ni, :], mods[:, l, m * 16:(m + 1) * 16], 1.0, cfs(gn), ALU.add, ALU.mult,
                    rd=[mod_r[l][m], r_c], wr=[geff_r[l][ni]])
                TS(geff[:, l, ni, :], geff[:, l, ni, :], float(np.sqrt(D)), None, ALU.mult,
                   rd=[geff_r[l][ni]], wr=[geff_r[l][ni]])

        def pump_mod(l, n):
            if NMOD == 0:
                return
            for _ in range(n):
                j = mod_next[l]
                if j >= 96:
                    return
                mod_next[l] = j + 1
                wt, wr_ = next_w(("mod", l, j))
                col = l * 96 + j
                for kc in range(KC):
                    MM(psM[:, col:col + 1], wt[:, kc, :], s_bf[:, kc:kc + 1], start=(kc == 0), stop=(kc == KC - 1),
                       rd=[wr_, r_s], wr=[prM])
                if j % 16 == 15:
                    mod_finalize(l, j // 16)

        pump_mod(0, 32 if LAZYMOD else 96)
        if not LAZYMOD:
            pump_mod(0, 96)
            pump_mod(1, 96)

        def mod_gen(n, lag=5):
            for _ in range(n):
                l_ = 0 if mod_next[0] < 96 else 1
                j = mod_next[l_]
                if NMOD == 0 or j >= 96:
                    return
                mod_next[l_] = j + 1
                wt, wr_ = next_w(("mod", l_, j))
                for _k in range(lag):
                    yield
                col = l_ * 96 + j
                for kc in range(KC):
                    MM(psM[:, col:col + 1], wt[:, kc, :], s_bf[:, kc:kc + 1], start=(kc == 0), stop=(kc == KC - 1),
                       rd=[wr_, r_s], wr=[prM])
                if j % 16 == 15:
                    mod_finalize(l_, j // 16)
                yield
        r_g = Res()
        for l in range(2):
            CP(gsm[:, l:l + 1], cfs(f"qg{l}"), rd=[r_c], wr=[r_g])
            TS(gsm[:, 2 + l:3 + l], cfs(f"kg{l}"), float(np.sqrt(128.0)), None, ALU.mult, rd=[r_c], wr=[r_g])
            TS(gsm[:, 4 + l:5 + l], cfs(f"glag{l}"), float(np.sqrt(128.0)), None, ALU.mult, rd=[r_c], wr=[r_g])
            TS(gsm[:, 6 + l:7 + l], cfs(f"dng{l}"), float(np.sqrt(128.0)), None, ALU.mult, rd=[r_c], wr=[r_g])
        if KSTOP in ('mod', 'n1') or 'mods' in KDBG:
            pump_mod(0, 96)
        dump("mods", mods[:].rearrange("p l m -> p (l m)"), mod_r[0])

        hT_r = [Res(), Res()]

        def rmsnorm_to_hT(l, ni, A):
            sh = 0 if ni == 0 else 3
            sq = [A.alloc(512, BF16) for _ in range(2)]
            sq_r = [Res(), Res()]
            rstd = [A.alloc(512, F32) for _ in range(2)]
            rstd_r = [Res(), Res()]
            tmp = [A.alloc(512, F32) for _ in range(2)]
            tmp_r = [Res(), Res()]
            for half in range(2):
                hs = slice(half * 512, (half + 1) * 512)
                ps, pr = nextps()
                for c in range(KC):
                    k = c % 2
                    if c % 2:
                        TT(sq[k], xT[:, c, hs], xT[:, c, hs], ALU.mult, rd=[xT_r[c][half]], wr=[sq_r[k]])
                    else:
                        ACT(sq[k], xT[:, c, hs], AF.Square, rd=[xT_r[c][half]], wr=[sq_r[k]])
                    MM(ps[:], ones_bf, sq[k], start=(c == 0), stop=(c == KC - 1), rd=[sq_r[k], r_cb], wr=[pr])
                RSTD(rstd[half], ps[:], D * EPS, [pr], [rstd_r[half]])
                for c in range(KC):
                    k = c % 2
                    if c % 4 == 3:
                        STT(tmp[k], xT[:, c, hs], geff[:, l, ni, c:c + 1], rstd[half], ALU.mult, ALU.mult,
                            rd=[xT_r[c][half], rstd_r[half], geff_r[l][ni]], wr=[tmp_r[k]])
                        TS(hT[:, c, hs], tmp[k], mods[:, l, sh * 16 + c: sh * 16 + c + 1], None, ALU.add,
                           rd=[tmp_r[k], mod_r[l][sh]], wr=[hT_r[half]])
                    else:
                        TT(tmp[k], xT[:, c, hs], rstd[half], ALU.mult, rd=[xT_r[c][half], rstd_r[half]], wr=[tmp_r[k]])
                        ACT(hT[:, c, hs], tmp[k], AF.Identity, rd=[tmp_r[k], geff_r[l][ni], mod_r[l][sh]], wr=[hT_r[half]],
                            scale=geff[:, l, ni, c:c + 1], bias=mods[:, l, sh * 16 + c: sh * 16 + c + 1])

        def proj_fm(wt, wr_, half, ps, pr, mcols=128):
            hs = slice(half * 512, (half + 1) * 512)
            for kc in range(KC):
                MM(ps[0:mcols, :], wt[:, kc, 0:mcols], hT[:, kc, hs], start=(kc == 0), stop=(kc == KC - 1),
                   rd=[wr_, hT_r[half]], wr=[pr])

        def wout_apply(l, key, nk, mixT, mix_r):
            for dc in range(16):
                wt, wr_ = next_w((key, l, dc))
                for half in range(2):
                    hs = slice(half * 512, (half + 1) * 512)
                    ps, pr = nextps()
                    for k in range(nk):
                        MM(ps[:], wt[:, k, :], mixT[:, k, hs], start=(k == 0), stop=(k == nk - 1),
                           rd=[wr_, mix_r[k][half]], wr=[pr])
                    STT(xT[:, dc, hs], ps[:], mods[:, l, 2 * 16 + dc: 2 * 16 + dc + 1], xT[:, dc, hs], ALU.mult, ALU.add,
                        rd=[pr, mod_r[l][2], xT_r[dc][half]], wr=[xT_r[dc][half]])

        def headnorm_fm(src_ap, src_r, A_slots, eps_tot):
            sq, sq_r, rstd, rstd_r = A_slots
            ACT(sq, src_ap, AF.Square, rd=src_r, wr=[sq_r])
            ps2, pr2 = nextps()
            MM(ps2[:], ones_bf, sq, rd=[sq_r, r_cb], wr=[pr2])
            RSTD(rstd, ps2[:], eps_tot, [pr2], [rstd_r])
            return rstd, rstd_r


        def run_il(gens):
            act_ = list(gens)
            while act_:
                for g_ in list(act_):
                    try:
                        next(g_)
                    except StopIteration:
                        act_.remove(g_)

        def mix_epilogue(A2, oacc, oacc_r, mixT, mix_r, gcol):
            NS = 3
            slots = [(A2.alloc(512, BF16), Res(), A2.alloc(512, F32), Res(), A2.alloc(512, F32), Res()) for _ in range(NS)]

            def it(h, half, k):
                hs = slice(half * 512, (half + 1) * 512)
                orr = oacc_r[half * 8:(half + 1) * 8]
                sq, sq_r, rstd, rstd_r, tmp, tmp_r = slots[k]
                ACT(sq, oacc[:, h, hs], AF.Square, rd=orr, wr=[sq_r])
                yield
                ps2, pr2 = nextps()
                MM(ps2[:], ones_bf, sq, rd=[sq_r, r_cb], wr=[pr2])
                yield
                ACT(rstd, ps2[:], AF.Ln, rd=[pr2, r_eps], wr=[rstd_r], bias=epsb[:, 1:2])
                yield
                ACT(rstd, rstd, AF.Exp, rd=[rstd_r], wr=[rstd_r], scale=-0.5)
                yield
                TT(tmp, oacc[:, h, hs], rstd, ALU.mult, rd=orr + [rstd_r], wr=[tmp_r])
                yield
                STT(mixT[:, h, hs], tmp, gcol, mixT[:, h, hs], ALU.mult, ALU.mult,
                    rd=[tmp_r, r_g, mix_r[h][half]], wr=[mix_r[h][half]])
            items = [(h, half) for h in range(4) for half in range(2)]
            for g0 in range(0, 8, NS):
                run_il([it(h, half, k) for k, (h, half) in enumerate(items[g0:g0 + NS])])

        def gla_phase(l):
            A = Arena(HTW)
            mixT = A.alloc(4 * T, BF16, "p (k t) -> p k t", k=4)
            mix_r = [[Res(), Res()] for _ in range(4)]
            qraw = A.alloc(4 * T, BF16, "p (h t) -> p h t", h=4)
            kraw = A.alloc(4 * T, BF16, "p (h t) -> p h t", h=4)
            qk_r = Res()
            vtok = A.alloc(16 * 512, BF16, "p (c n) -> p c n", c=16)
            vt_r = [Res() for _ in range(16)]
            oacc = A.alloc(4 * T, F32, "p (h t) -> p h t", h=4)
            oacc_r = [Res() for _ in range(16)]
            Sg = A.alloc(2 * 512, F32, "p (z n) -> p z n", z=2)
            Sg_r = [Res(), Res()]
            Sgb = A.alloc(2 * 512, BF16, "p (z n) -> p z n", z=2)
            Sgb_r = [Res(), Res()]
            lrT = A.alloc(T, F32)
            lr_r = Res()
            w2a = A.alloc(512, F32)
            r_w2 = Res()
            e1 = [A.alloc(256, F32) for _ in range(2)]
            sp = [A.alloc(256, F32) for _ in range(2)]
            eb = [A.alloc(256, F32) for _ in range(2)]
            enb = [A.alloc(256, F32) for _ in range(2)]
            qt = [A.alloc(256, BF16) for _ in range(2)]
            kt = [A.alloc(256, BF16) for _ in range(2)]
            ATb = [A.alloc(256, BF16) for _ in range(2)]
            ktok = [A.alloc(256, BF16) for _ in range(2)]
            sst = [A.alloc(512, F32) for _ in range(2)]
            sst_r = [Res(), Res()]
            rr = {n: [Res(), Res()] for n in ("e1", "sp", "eb", "enb", "qt", "kt", "AT", "ktok")}
            DMA("sp", Sg[0:64], sg_in[l], wr=Sg_r, sem="sgi")
            for z in range(2):
                CP(Sgb[0:64, z, :], Sg[0:64, z, :], rd=[Sg_r[z]], wr=[Sgb_r[z]], eng="act")
            DMA("sp", w2a[0:33, :], w2aug_in[l], wr=[r_w2], sem="sgi")
            S.op("dve", lambda e: e.memset(lrT[32:33, :], 1.0), [], [lr_r])
            ei = 0
            for kind, dst in (("gq", qraw), ("gk", kraw)):
                for p in range(2):
                    wt, wr_ = next_w((kind, l, p))
                    for hp in range(2):
                        h = 2 * p + hp
                        for half in range(2):
                            hs = slice(half * 512, (half + 1) * 512)
                            ps, pr = nextps()
                            for kc in range(KC):
                                MM(ps[0:64, :], wt[:, kc, hp * 64:(hp + 1) * 64], hT[:, kc, hs], start=(kc == 0),
                                   stop=(kc == KC - 1), rd=[wr_, hT_r[half]], wr=[pr])
                            CP(dst[0:64, h, hs], ps[0:64, :], rd=[pr], wr=[qk_r], eng=("act" if ei % 2 else "dve"))
                            ei += 1
            for h in range(4):
                wt, wr_ = next_w(("gr", l, h))
                for half in range(2):
                    hs = slice(half * 512, (half + 1) * 512)
                    ps, pr = nextps()
                    proj_fm(wt, wr_, half, ps, pr)
                    ACT(mixT[:, h, hs], ps[:], AF.Silu, rd=[pr], wr=[mix_r[h][half]])
            wt, wr_ = next_w(("glr", l))
            for half in range(2):
                hs = slice(half * 512, (half + 1) * 512)
                ps, pr = nextps()
                proj_fm(wt, wr_, half, ps, pr, mcols=32)
                CP(lrT[0:32, hs], ps[0:32, :], rd=[pr], wr=[lr_r])
            for h in range(4):
                wt, wr_ = next_w(("gv", l, h))
                for c in range(16):
                    ps, pr = nextps()
                    for kc in range(KC):
                        MM(ps[0:64, 0:128], hT[:, kc, c * 64:(c + 1) * 64], wt[:, kc, :], start=(kc == 0),
                           stop=(kc == KC - 1), rd=[wr_, hT_r[c // 8]], wr=[pr])
                    CP(vtok[0:64, c, h * 128:(h + 1) * 128], ps[0:64, 0:128], rd=[pr], wr=[vt_r[c]],
                       eng=("act" if c % 2 else "dve"))
            if KSTOP == 'gp':
                return
            owritten = [False] * 16
            sso = [0]
            h4 = lambda ap: ap.rearrange("p (h i) -> p h i", h=4)
            id64 = ident_bf[0:64, 0:64]

            def cut(n):
                return KGS != "" and n >= int(KGS)

            def step(z, c):
                cs_ = slice(c * 64, (c + 1) * 64)
                ps1, pr1 = nextps()
                MM(ps1[0:64, 0:256], lrT[0:33, cs_], w2a[0:33, z * 256:(z + 1) * 256], rd=[lr_r, r_w2], wr=[pr1])
                ACT(e1[z][0:64, :], ps1[0:64, 0:256], AF.Exp, rd=[pr1], wr=[rr["e1"][z]], scale=-1.0)
                ACT(sp[z][0:64, :], e1[z][0:64, :], AF.Ln, rd=[rr["e1"][z], r_eps], wr=[rr["sp"][z]], bias=epsb[0:64, 3:4])
                if cut(1):
                    return
                yield
                psb, prb = nextps()
                for h in range(4):
                    MM(psb[0:64, h * 64:(h + 1) * 64], sp[z][0:64, h * 64:(h + 1) * 64], c6(f"TriS{z}"),
                       rd=[rr["sp"][z], r_c], wr=[prb])
                ACT(eb[z][0:64, :], psb[0:64, 0:256], AF.Exp, rd=[prb], wr=[rr["eb"][z]])
                ACT(enb[z][0:64, :], psb[0:64, 0:256], AF.Exp, rd=[prb], wr=[rr["enb"][z]], scale=-1.0)
                if cut(2):
                    return
                yield
                STT(h4(qt[z][0:64, :]), h4(eb[z][0:64, :]), 0.125, qraw[0:64, :, cs_], ALU.mult, ALU.mult,
                    rd=[rr["eb"][z], qk_r], wr=[rr["qt"][z]])
                TT(h4(kt[z][0:64, :]), h4(enb[z][0:64, :]), kraw[0:64, :, cs_], ALU.mult,
                   rd=[rr["enb"][z], qk_r], wr=[rr["kt"][z]])
                if cut(3):
                    return
                yield
                psA, prA = nextps()
                for h in range(4):
                    hc = slice(h * 64, (h + 1) * 64)
                    MM(psA[0:64, hc], kt[z][0:64, hc], qt[z][0:64, hc], rd=[rr["kt"][z], rr["qt"][z]], wr=[prA])
                yield
                TT(h4(ATb[z][0:64, :]), h4(psA[0:64, 0:256]),
                   c64b[:, z * 64:(z + 1) * 64].unsqueeze(1).to_broadcast([64, 4, 64]), ALU.mult,
                   rd=[prA, r_cb], wr=[rr["AT"][z]])
                if cut(4):
                    return
                yield
                pst, prt = nextps()
                pstb = pst[:].bitcast(BF16)
                for h in range(4):
                    hc = slice(h * 64, (h + 1) * 64)
                    TR(pstb[0:64, hc], kt[z][0:64, hc], id64, rd=[rr["kt"][z], r_cb], wr=[prt])
                yield
                CP(ktok[z][0:64, :], pstb[0:64, 0:256], rd=[prt], wr=[rr["ktok"][z]], eng="act")
                if cut(5):
                    return
                if (z == 0 and c % 4 == 0 and c > 0) or (z == 1 and c % 4 == 3 and c < 15):
                    TS(Sg[0:64, z, :], Sg[0:64, z, :], cfs("carry")[0:64, :], None, ALU.mult, rd=[Sg_r[z], r_c], wr=[Sg_r[z]])
                    CP(Sgb[0:64, z, :], Sg[0:64, z, :], rd=[Sg_r[z]], wr=[Sgb_r[z]], eng="act")
                yield
                psO, prO = nextps()
                for h in range(4):
                    hc = slice(h * 64, (h + 1) * 64)
                    MM(psO[:, hc], vtok[0:64, c, h * 128:(h + 1) * 128], ATb[z][0:64, hc], start=True, stop=False,
                       rd=[vt_r[c], rr["AT"][z]], wr=[prO])
                    MM(psO[:, hc], Sgb[0:64, z, h * 128:(h + 1) * 128], qt[z][0:64, hc], start=False, stop=True,
                       rd=[Sgb_r[z], rr["qt"][z]], wr=[prO])
                yield
                ov = oacc[:, :, cs_]
                pv = h4(psO[:, 0:256])
                if not owritten[c]:
                    CP(ov, pv, rd=[prO], wr=[oacc_r[c]])
                    owritten[c] = True
                else:
                    TT(ov, pv, ov, ALU.add, rd=[prO, oacc_r[c]], wr=[oacc_r[c]])
                if cut(6):
                    return
                yield
                psS, prS = nextps()
                for h in range(4):
                    MM(psS[0:64, h * 128:(h + 1) * 128], ktok[z][0:64, h * 64:(h + 1) * 64],
                       vtok[0:64, c, h * 128:(h + 1) * 128], rd=[rr["ktok"][z], vt_r[c]], wr=[prS])
                yield
                TT(Sg[0:64, z, :], psS[0:64, :], Sg[0:64, z, :], ALU.add, rd=[prS, Sg_r[z]], wr=[Sg_r[z]])
                col = 63 if z == 0 else 0
                hd = lambda ap: ap.rearrange("p (h d) -> p h d", h=4)
                TT(hd(Sg[0:64, z, :]), hd(Sg[0:64, z, :]),
                   h4(eb[z][0:64, :])[:, :, col:col + 1].to_broadcast([64, 4, 128]), ALU.mult,
                   rd=[Sg_r[z], rr["eb"][z]], wr=[Sg_r[z]])
                CP(Sgb[0:64, z, :], Sg[0:64, z, :], rd=[Sg_r[z]], wr=[Sgb_r[z]], eng="act")
                if (z == 0 and c % 4 == 3) or (z == 1 and c % 4 == 0):
                    k = sso[0] % 2
                    sso[0] += 1
                    CP(sst[k][0:64, :], Sg[0:64, z, :], rd=[Sg_r[z]], wr=[sst_r[k]])
                    DMA("sp", sg_out[l, z, c // 4], sst[k][0:64, :], rd=[sst_r[k]], sem=f"sso{k}")

            for s in range(16):
                run_il([step(0, s), step(1, 15 - s)])
            S.barrier()
            if KGS != "":
                return
            A2 = Arena(HTW + 4 * T // 2)
            mix_epilogue(A2, oacc, oacc_r, mixT, mix_r, gsm[:, 4 + l:5 + l])
            if l == 0:
                dump("mgla", mixT[:, 0, :], [mix_r[0][0], mix_r[0][1]], BF16)
            wout_apply(l, "wo_gla", 4, mixT, mix_r)

        def dn_phase(l):
            A = Arena(HTW)
            mixT = A.alloc(4 * T, BF16, "p (k t) -> p k t", k=4)
            mix_r = [[Res(), Res()] for _ in range(4)]
            qkvT = A.alloc(8 * T, BF16, "p (c t) -> p c t", c=8)
            qkv_r = [Res() for _ in range(12)]
            oacc = A.alloc(4 * T, F32, "p (h t) -> p h t", h=4)
            oacc_r = [Res() for _ in range(16)]
            Sd = A.alloc(2 * 512, F32, "p (z n) -> p z n", z=2)
            Sd_r = [Res(), Res()]
            Sdb = A.alloc(2 * 512, BF16, "p (z n) -> p z n", z=2)
            Sdb_r = [Res(), Res()]
            gab = A.alloc(256, F32, "p (c n) -> p c n", c=16)
            gtmp = A.alloc(128, F32, "p (c n) -> p c n", c=16)
            gall = A.alloc(128, F32, "p (c n) -> p c n", c=16)
            beta = A.alloc(128, F32, "p (c n) -> p c n", c=16)
            nbeta = A.alloc(128, F32, "p (c n) -> p c n", c=16)
            nexpA = A.alloc(8, F32)
            nb = A.alloc(36, F32)
            r_gb = Res()
            vT = A.alloc(4 * T, BF16, "p (c t) -> p c t", c=4)
            mark = A.p
            raw = [A.alloc(1026, F32) for _ in range(2)]
            raw_r = [Res(), Res()]
            yv = [A.alloc(1024, F32) for _ in range(2)]
            yv_r = [Res(), Res()]
            sq_ = A.alloc(512, BF16)
            rstd_ = A.alloc(512, F32)
            nsl = (sq_, Res(), rstd_, Res())
            DMA("sp", Sd, sd_in[l], wr=Sd_r, sem="sgi")
            for z in range(2):
                CP(Sdb[:, z, :], Sd[:, z, :], rd=[Sd_r[z]], wr=[Sdb_r[z]], eng="act")
            TS(nb, cfs(f"conv{l}"), cfs("cflag"), -1.0, ALU.mult, ALU.mult, rd=[r_c], wr=[r_gb])
            for k in range(2):
                S.op("dve", (lambda t_: (lambda e: e.memset(t_, 0.0)))(raw[k][:, 0:1]), [], [raw_r[k]])
                S.op("dve", (lambda t_: (lambda e: e.memset(t_, 0.0)))(raw[k][:, 1025:1026]), [], [raw_r[k]])
            cw = cfs(f"conv{l}")
            def dn_proj(cc, wt, wr_):
                k = cc % 2
                for half in range(2):
                    ps, pr = nextps()
                    proj_fm(wt, wr_, half, ps, pr)
                    CP(raw[k][:, 1 + half * 512: 1 + (half + 1) * 512], ps[:], rd=[pr], wr=[raw_r[k]],
                       eng=("act" if half else "dve"))

            def dn_tail(cc):
                k = cc % 2
                TS(yv[k], raw[k][:, 1:1025], cw[:, cc * 3 + 1: cc * 3 + 2], None, ALU.mult, rd=[raw_r[k], r_c], wr=[yv_r[k]])
                STT(yv[k], raw[k][:, 0:1024], cw[:, cc * 3: cc * 3 + 1], yv[k], ALU.mult, ALU.add,
                    rd=[raw_r[k], r_c, yv_r[k]], wr=[yv_r[k]])
                STT(yv[k], raw[k][:, 2:1026], cw[:, cc * 3 + 2: cc * 3 + 3], yv[k], ALU.mult, ALU.add,
                    rd=[raw_r[k], r_c, yv_r[k]], wr=[yv_r[k]])
                STT(yv[k][:, 256:1024:256], raw[k][:, 256:1024:256], nb[:, cc * 3: cc * 3 + 1], yv[k][:, 256:1024:256],
                    ALU.mult, ALU.add, rd=[raw_r[k], r_gb, yv_r[k]], wr=[yv_r[k]])
                STT(yv[k][:, 255:1023:256], raw[k][:, 257:1025:256], nb[:, cc * 3 + 2: cc * 3 + 3],
                    yv[k][:, 255:1023:256], ALU.mult, ALU.add, rd=[raw_r[k], r_gb, yv_r[k]], wr=[yv_r[k]])
                if cc >= 8:
                    ACT(vT[:, cc - 8, :], yv[k], AF.Silu, rd=[yv_r[k]], wr=[qkv_r[cc]])
                else:
                    ACT(yv[k], yv[k], AF.Silu, rd=[yv_r[k]], wr=[yv_r[k]])
                    for half in range(2):
                        hs = slice(half * 512, (half + 1) * 512)
                        rstd, rstd_r = headnorm_fm(yv[k][:, hs], [yv_r[k]], nsl, EPS)
                        if cc < 4:
                            STT(qkvT[:, cc, hs], yv[k][:, hs], float(128.0 ** -0.5), rstd, ALU.mult, ALU.mult,
                                rd=[yv_r[k], rstd_r], wr=[qkv_r[cc]])
                        else:
                            TT(qkvT[:, cc, hs], yv[k][:, hs], rstd, ALU.mult, rd=[yv_r[k], rstd_r], wr=[qkv_r[cc]])

            w_ = next_w(("dqkv", l, 0))
            dn_proj(0, *w_)
            for cc in range(12):
                if cc + 1 < 12:
                    w_ = next_w(("dqkv", l, cc + 1))
                    dn_proj(cc + 1, *w_)
                dn_tail(cc)
            wt, wr_ = next_w(("dab", l))
            psg, prg = nextps()
            for c in range(16):
                for kc in range(KC):
                    MM(psg[0:64, c * 16:(c + 1) * 16], hT[:, kc, c * 64:(c + 1) * 64], wt[:, kc, 0:16],
                       start=(kc == 0), stop=(kc == KC - 1), rd=[wr_, hT_r[c // 8]], wr=[prg])
            CP(gab[0:64], psg[0:64, 0:256].rearrange("p (c n) -> p c n", c=16), rd=[prg], wr=[r_gb])
            bc8 = lambda ap: ap.unsqueeze(1).to_broadcast([64, 16, 8])
            TT(gtmp[0:64], gab[0:64, :, 0:8], bc8(c6(f"dtb{l}")), ALU.add, rd=[r_gb, r_c], wr=[r_gb])
            ACT(gtmp[0:64], gtmp[0:64], AF.Exp, rd=[r_gb], wr=[r_gb])
            ACT(gtmp[0:64].rearrange("p c n -> p (c n)"), gtmp[0:64].rearrange("p c n -> p (c n)"), AF.Ln, rd=[r_gb, r_eps], wr=[r_gb], bias=epsb[0:64, 3:4])
            ACT(nexpA[0:64, :], c6(f"alog{l}"), AF.Exp, rd=[r_c], wr=[r_gb])
            TS(nexpA[0:64, :], nexpA[0:64, :], -1.0, None, ALU.mult, rd=[r_gb], wr=[r_gb])
            TT(gall[0:64], gtmp[0:64], bc8(nexpA[0:64, :]), ALU.mult, rd=[r_gb], wr=[r_gb])
            ACT(beta[0:64], gab[0:64, :, 8:16], AF.Exp, rd=[r_gb], wr=[r_gb], scale=-1.0)
            TS(beta[0:64], beta[0:64], 1.0, None, ALU.add, rd=[r_gb], wr=[r_gb])
            S.op("dve", lambda e: e.reciprocal(out=beta[0:64], in_=beta[0:64]), [r_gb], [r_gb])
            TS(nbeta[0:64], beta[0:64], -1.0, None, ALU.mult, rd=[r_gb], wr=[r_gb])
            for h in range(4):
                wt, wr_ = next_w(("dz", l, h))
                for half in range(2):
                    hs = slice(half * 512, (half + 1) * 512)
                    ps, pr = nextps()
                    proj_fm(wt, wr_, half, ps, pr)
                    ACT(mixT[:, h, hs], ps[:], AF.Silu, rd=[pr], wr=[mix_r[h][half]])
            if l == 0:
                dump("dq", qkvT[:, 0, :], [qkv_r[0]], BF16)
                dump("dk", qkvT[:, 4, :], [qkv_r[4]], BF16)
                dump("dv", vT[:, 0, :], [qkv_r[8]], BF16)
                dump("dg", gall[0:64].rearrange("p c n -> p (c n)"), [r_gb])
                dump("dbeta", beta[0:64].rearrange("p c n -> p (c n)"), [r_gb])
            S.barrier()
            AH = Arena(0)
            AR = Arena(mark)
            NSL = 3
            NLN = 2

            def alloc2(n, dt):
                nw = (n * (4 if dt == F32 else 2) + 3) // 4
                if AH.p + nw <= HTW:
                    return AH.alloc(n, dt)
                return AR.alloc(n, dt)
            WK = {}
            for z in range(2):
                for ln in range(NLN):
                    for name, n, dt in (("gTri", 256, F32), ("gbc", 256, F32), ("eGbc", 256, F32), ("AA", 512, BF16),
                                        ("Xs", 512, BF16)):
                        WK[(name, z, ln)] = (alloc2(n, dt), Res())
                    for al, sname in (("d1", "gTri"), ("a1", "gTri"), ("d2", "gbc"), ("tmpm", "Xs")):
                        WK[(al, z, ln)] = WK[(sname, z, ln)]
            HO = {}
            for z in range(2):
                for sl in range(NSL):
                    for name, n, dt in (("TTT", 512, BF16), ("attT", 256, BF16), ("qte", 256, BF16), ("kd", 512, BF16),
                                        ("bv", 512, BF16), ("sm", 12, F32), ("eGl", 4, F32)):
                        HO[(name, z, sl)] = (alloc2(n, dt), Res())
                    S.op("dve", (lambda t_: (lambda e: e.memset(t_, 0.0)))(HO[("attT", z, sl)][0][64:128, :]), [],
                         [HO[("attT", z, sl)][1]])
            SW = {}
            for z in range(2):
                for name, n, dt in (("tmpr", 512, F32), ("r", 512, BF16), ("vn", 512, BF16)):
                    SW[(name, z)] = (alloc2(n, dt), Res())
                S.op("dve", (lambda t_: (lambda e: e.memset(t_, 0.0)))(SW[("vn", z)][0][64:128, :]), [], [SW[("vn", z)][1]])
            _s1 = alloc2(512, F32)
            _r1 = Res()
            print("DN ring end", AH.p, HTW, AR.p, ARW)
            owritten = [False] * 16
            h4 = lambda ap: ap.rearrange("p (h i) -> p h i", h=4)
            hd = lambda ap: ap.rearrange("p (h d) -> p h d", h=4)
            v4 = lambda ap: ap.rearrange("p (v h j) -> p v h j", v=2, h=4)
            id64 = ident_bf[0:64, 0:64]

            ps_busy = [False] * 7

            def getps():
                while True:
                    for k_ in range(7):
                        i_ = (ps_i[0] + k_) % 7
                        if not ps_busy[i_]:
                            ps_busy[i_] = True
                            ps_i[0] = i_ + 1
                            return ps_t[i_], ps_r[i_], i_
                    yield

            def relps(i_):
                ps_busy[i_] = False

            def pre(z, c, ln, sl):
                cs_ = slice(c * 64, (c + 1) * 64)
                g = gall[0:64, c, z * 4:(z + 1) * 4]
                bz = beta[0:64, c, z * 4:(z + 1) * 4]
                nbz = nbeta[0:64, c, z * 4:(z + 1) * 4]
                Tri = c6("TriL") if z == 0 else c6("TriU")
                nTri = c6("nTriL") if z == 0 else c6("nTriU")
                TriC = c6("TriCL") if z == 0 else c6("TriCU")
                w = lambda n: WK[(n, z, ln)][0]
                wq = lambda n: WK[(n, z, ln)][1]
                o = lambda n: HO[(n, z, sl)][0]
                oq = lambda n: HO[(n, z, sl)][1]
                mz = lambda m: mzb[:, z, m].unsqueeze(2).to_broadcast([64, 2, 4, 64])
                gb64 = g.unsqueeze(2).to_broadcast([64, 4, 64])
                TT(h4(w("gTri")[0:64, :]), Tri.unsqueeze(1).to_broadcast([64, 4, 64]), gb64, ALU.mult,
                   rd=[r_c, r_gb], wr=[wq("gTri")])
                CP(h4(w("gbc")[0:64, :]), gb64, rd=[r_gb], wr=[wq("gbc")])
                yield
                psD, prD, bD = yield from getps()
                MM(psD[0:64, 0:256], Tri, w("gbc")[0:64, :], start=True, stop=False, rd=[r_c, wq("gbc")], wr=[prD])
                MM(psD[0:64, 0:256], c6("negones"), w("gTri")[0:64, :], start=False, stop=True, rd=[r_c, wq("gTri")], wr=[prD])
                MM(psD[0:64, 256:512], c6("ones", 0, 64), w("gTri")[0:64, :], start=True, stop=False, rd=[r_c, wq("gTri")], wr=[prD])
                MM(psD[0:64, 256:512], nTri, w("gbc")[0:64, :], start=False, stop=True, rd=[r_c, wq("gbc")], wr=[prD])
                psX, prX, bX = yield from getps()
                MM(psX[:, 0:256], c6("ones"), w("gTri")[0:64, :], rd=[r_c, wq("gTri")], wr=[prX])
                MM(psX[0:64, 256:260], Tri, g, rd=[r_c, r_gb], wr=[prX])
                MM(psX[:, 260:264], c6("ones"), g, rd=[r_c, r_gb], wr=[prX])
                MM(psX[0:64, 264:268], TriC, g, rd=[r_c, r_gb], wr=[prX])
                yield
                TT(h4(w("d1")[0:64, :]), h4(psD[0:64, 0:256]), c6(f"mbS{z}").unsqueeze(1).to_broadcast([64, 4, 64]),
                   ALU.add, rd=[prD, r_c], wr=[wq("d1")])
                TT(h4(w("d2")[0:64, :]), h4(psD[0:64, 256:512]), c6(f"mbIT{z}").unsqueeze(1).to_broadcast([64, 4, 64]),
                   ALU.add, rd=[prD, r_c], wr=[wq("d2")])
                ACT(w("eGbc"), psX[:, 0:256], AF.Exp, rd=[prX], wr=[wq("eGbc")])
                ACT(o("sm")[0:64, 0:4], psX[0:64, 256:260], AF.Exp, rd=[prX], wr=[oq("sm")])
                ACT(o("eGl"), psX[:, 260:264], AF.Exp, rd=[prX], wr=[oq("eGl")])
                ACT(o("sm")[0:64, 4:8], psX[0:64, 264:268], AF.Exp, rd=[prX], wr=[oq("sm")])
                relps(bD)
                relps(bX)
                yield
                ACT(w("d1")[0:64, :], w("d1")[0:64, :], AF.Exp, rd=[wq("d1")], wr=[wq("d1")])
                ACT(w("d2")[0:64, :], w("d2")[0:64, :], AF.Exp, rd=[wq("d2")], wr=[wq("d2")])
                TT(o("sm")[0:64, 8:12], o("sm")[0:64, 0:4], nbz, ALU.mult, rd=[oq("sm"), r_gb], wr=[oq("sm")])
                psK, prK, bK = yield from getps()
                for h in range(4):
                    MM(psK[0:64, h * 64:(h + 1) * 64], qkvT[:, 4 + h, cs_], qkvT[:, 4 + h, cs_], rd=[qkv_r[4 + h]], wr=[prK])
                for h in range(4):
                    MM(psK[0:64, 256 + h * 64:256 + (h + 1) * 64], qkvT[:, 4 + h, cs_], qkvT[:, h, cs_],
                       rd=[qkv_r[4 + h], qkv_r[h]], wr=[prK])
                yield
                TT(w("a1")[0:64, :], psK[0:64, 0:256], w("d1")[0:64, :], ALU.mult, rd=[prK, wq("d1")], wr=[wq("a1")])
                TT(h4(w("AA")[0:64, 0:256]), h4(w("a1")[0:64, :]), bz.unsqueeze(2).to_broadcast([64, 4, 64]), ALU.mult,
                   rd=[wq("a1"), r_gb], wr=[wq("AA")])
                TT(o("attT")[0:64, :], psK[0:64, 256:512], w("d2")[0:64, :], ALU.mult, rd=[prK, wq("d2")], wr=[oq("attT")])
                relps(bK)
                yield
                pst, prt, bt = yield from getps()
                pst_v = pst[:].bitcast(BF16)
                for h in range(4):
                    TR(pst_v[0:64, h * 64:(h + 1) * 64], w("AA")[0:64, h * 64:(h + 1) * 64], id64,
                       rd=[wq("AA"), r_cb], wr=[prt])
                psk, prk, bk = yield from getps()
                pskb = psk[:].bitcast(BF16)
                for h in range(4):
                    TR(pskb[0:64, h * 128:(h + 1) * 128], qkvT[:, 4 + h, cs_], ident_bf, rd=[qkv_r[4 + h], r_cb], wr=[prk])
                prv = prk
                for h in range(4):
                    TR(pskb[0:64, 512 + h * 128:512 + (h + 1) * 128], vT[:, h, cs_], ident_bf, rd=[qkv_r[8 + h], r_cb], wr=[prv])
                yield
                CP(w("AA")[0:64, 256:512], pst_v[0:64, 0:256], rd=[prt], wr=[wq("AA")], eng="act")
                TT(hd(o("kd")[0:64, :]), hd(pskb[0:64, 0:512]), o("sm")[0:64, 4:8].unsqueeze(2).to_broadcast([64, 4, 128]),
                   ALU.mult, rd=[prk, oq("sm")], wr=[oq("kd")])
                TT(hd(o("bv")[0:64, :]), hd(pskb[0:64, 512:1024]), bz.unsqueeze(2).to_broadcast([64, 4, 128]), ALU.mult,
                   rd=[prv, r_gb], wr=[oq("bv")])
                TT(h4(o("qte")), qkvT[:, 0:4, cs_], h4(w("eGbc")), ALU.mult, rd=qkv_r[0:4] + [wq("eGbc")], wr=[oq("qte")])
                relps(bt)
                relps(bk)
                yield
                TT(v4(w("tmpm")[0:64, :]), v4(w("AA")[0:64, :]), mz(0), ALU.mult, rd=[wq("AA"), r_cb], wr=[wq("tmpm")])
                TT(v4(o("TTT")[0:64, :]), c6("ident").unsqueeze(1).unsqueeze(1).to_broadcast([64, 2, 4, 64]),
                   v4(w("tmpm")[0:64, :]), ALU.subtract, rd=[wq("tmpm"), r_c], wr=[oq("TTT")])
                for m in range(1, 6):
                    yield
                    psXX, prXX, bXX = yield from getps()
                    for h in range(4):
                        hc = slice(h * 64, (h + 1) * 64)
                        hc2 = slice(256 + h * 64, 256 + (h + 1) * 64)
                        MM(psXX[0:64, hc], w("AA")[0:64, hc2], o("TTT")[0:64, hc], rd=[wq("AA"), oq("TTT")], wr=[prXX])
                        MM(psXX[0:64, hc2], w("AA")[0:64, hc], o("TTT")[0:64, hc2], rd=[wq("AA"), oq("TTT")], wr=[prXX])
                    yield
                    CP(w("Xs")[0:64, :], psXX[0:64, :], rd=[prXX], wr=[wq("Xs")], eng="act")
                    relps(bXX)
                    yield
                    psY, prY, bY = yield from getps()
                    for h in range(4):
                        hc = slice(h * 64, (h + 1) * 64)
                        hc2 = slice(256 + h * 64, 256 + (h + 1) * 64)
                        MM(psY[0:64, hc], o("TTT")[0:64, hc2], w("Xs")[0:64, hc], rd=[wq("Xs"), oq("TTT")], wr=[prY])
                        MM(psY[0:64, hc2], o("TTT")[0:64, hc], w("Xs")[0:64, hc2], rd=[wq("Xs"), oq("TTT")], wr=[prY])
                    yield
                    TT(v4(w("tmpm")[0:64, :]), v4(psY[0:64, :]), mz(m), ALU.mult, rd=[prY, r_cb], wr=[wq("tmpm")])
                    relps(bY)
                    yield
                    TT(o("TTT")[0:64, :], o("TTT")[0:64, :], w("tmpm")[0:64, :], ALU.subtract,
                       rd=[oq("TTT"), wq("tmpm")], wr=[oq("TTT")])

            sso = [0]

            def ser(z, c, sl):
                cs_ = slice(c * 64, (c + 1) * 64)
                o = lambda n: HO[(n, z, sl)][0]
                oq = lambda n: HO[(n, z, sl)][1]
                s_ = lambda n: SW[(n, z)][0]
                sq = lambda n: SW[(n, z)][1]
                if (z == 0 and c % 4 == 0 and c > 0) or (z == 1 and c % 4 == 3 and c < 15):
                    TS(Sd[:, z, :], Sd[:, z, :], cfs("carry"), None, ALU.mult, rd=[Sd_r[z], r_c], wr=[Sd_r[z]])
                    CP(Sdb[:, z, :], Sd[:, z, :], rd=[Sd_r[z]], wr=[Sdb_r[z]], eng="act")
                    yield
                psKS, prKS, bKS = yield from getps()
                for h in range(4):
                    MM(psKS[0:64, h * 128:(h + 1) * 128], qkvT[:, 4 + h, cs_], Sdb[:, z, h * 128:(h + 1) * 128],
                       rd=[qkv_r[4 + h], Sdb_r[z]], wr=[prKS])
                yield
                TT(hd(s_("tmpr")[0:64, :]), hd(psKS[0:64, :]), o("sm")[0:64, 8:12].unsqueeze(2).to_broadcast([64, 4, 128]),
                   ALU.mult, rd=[prKS, oq("sm")], wr=[sq("tmpr")])
                relps(bKS)
                yield
                TT(s_("r")[0:64, :], s_("tmpr")[0:64, :], o("bv")[0:64, :], ALU.add, rd=[sq("tmpr"), oq("bv")], wr=[sq("r")])
                yield
                psV, prV, bV = yield from getps()
                for h in range(4):
                    MM(psV[0:64, h * 128:(h + 1) * 128], o("TTT")[0:64, 256 + h * 64:256 + (h + 1) * 64],
                       s_("r")[0:64, h * 128:(h + 1) * 128], rd=[oq("TTT"), sq("r")], wr=[prV])
                yield
                CP(s_("vn")[0:64, :], psV[0:64, :], rd=[prV], wr=[sq("vn")], eng="act")
                relps(bV)
                yield
                psO, prO, bO = yield from getps()
                for h in range(4):
                    MM(psO[:, h * 64:(h + 1) * 64], Sdb[:, z, h * 128:(h + 1) * 128], o("qte")[:, h * 64:(h + 1) * 64],
                       start=True, stop=False, rd=[Sdb_r[z], oq("qte")], wr=[prO])
                    MM(psO[:, h * 64:(h + 1) * 64], s_("vn")[:, h * 128:(h + 1) * 128], o("attT")[:, h * 64:(h + 1) * 64],
                       start=False, stop=True, rd=[sq("vn"), oq("attT")], wr=[prO])
                psS, prS, bS = yield from getps()
                for h in range(4):
                    MM(psS[:, h * 128:(h + 1) * 128], o("kd")[0:64, h * 128:(h + 1) * 128], s_("vn")[0:64, h * 128:(h + 1) * 128],
                       rd=[oq("kd"), sq("vn")], wr=[prS])
                yield
                TT(hd(Sd[:, z, :]), hd(Sd[:, z, :]), o("eGl").unsqueeze(2).to_broadcast([128, 4, 128]), ALU.mult,
                   rd=[Sd_r[z], oq("eGl")], wr=[Sd_r[z]])
                TT(Sd[:, z, :], psS[:], Sd[:, z, :], ALU.add, rd=[prS, Sd_r[z]], wr=[Sd_r[z]])
                relps(bS)
                yield
                CP(Sdb[:, z, :], Sd[:, z, :], rd=[Sd_r[z]], wr=[Sdb_r[z]], eng="act")
                ov = oacc[:, :, cs_]
                pv = h4(psO[:, 0:256])
                if not owritten[c]:
                    CP(ov, pv, rd=[prO], wr=[oacc_r[c]])
                    owritten[c] = True
                else:
                    TT(ov, pv, ov, ALU.add, rd=[prO, oacc_r[c]], wr=[oacc_r[c]])
                relps(bO)
                if (z == 0 and c % 4 == 3) or (z == 1 and c % 4 == 0):
                    CP(_s1, Sd[:, z, :], rd=[Sd_r[z]], wr=[_r1])
                    k = sso[0] % 2
                    sso[0] += 1
                    DMA("sp", sd_out[l, z, c // 4], _s1, rd=[_r1], sem=f"sso{k}")

            order = [list(range(16)), list(range(15, -1, -1))]
            pre_i = [0, 0]
            pre_done = [0, 0]
            ser_i = [0, 0]
            ser_done = [0, 0]
            active = []
            modg = [mod_gen(48, lag=4), mod_gen(48, lag=4)] if l == 0 else []
            while ser_done[0] < 16 or ser_done[1] < 16:
                for z in range(2):
                    n_pre = sum(1 for a_ in active if a_[1] == "pre" and a_[2] == z)
                    while (n_pre < NLN and pre_i[z] < 16 and pre_i[z] - ser_done[z] < NSL):
                        i_ = pre_i[z]
                        lanes_busy = [a_[4] for a_ in active if a_[1] == "pre" and a_[2] == z]
                        ln = 0 if 0 not in lanes_busy else 1
                        active.append([pre(z, order[z][i_], ln, i_ % NSL), "pre", z, i_, ln])
                        pre_i[z] += 1
                        n_pre += 1
                    if not any(a_[1] == "ser" and a_[2] == z for a_ in active) and ser_i[z] < 16 and pre_done[z] > ser_i[z]:
                        i_ = ser_i[z]
                        active.append([ser(z, order[z][i_], i_ % NSL), "ser", z, i_, -1])
                        ser_i[z] += 1
                for a_ in list(active):
                    try:
                        next(a_[0])
                    except StopIteration:
                        active.remove(a_)
                        if a_[1] == "pre":
                            pre_done[a_[2]] += 1
                        else:
                            ser_done[a_[2]] += 1
                for g_ in list(modg):
                    try:
                        next(g_)
                    except StopIteration:
                        modg.remove(g_)
            for g_ in modg:
                for _ in g_:
                    pass
            S.barrier()
            A2 = Arena(mark)
            mix_epilogue(A2, oacc, oacc_r, mixT, mix_r, gsm[:, 6 + l:7 + l])
            if l == 0:
                dump("mdn", mixT[:, 0, :], [mix_r[0][0], mix_r[0][1]], BF16)
            wout_apply(l, "wo_dn", 4, mixT, mix_r)

        for l in range(NL):
            S.barrier()
            A = Arena(HTW)
            rmsnorm_to_hT(l, 0, A)
            if l == 0:
                dump("h1", hT[:, 0, :], hT_r, BF16)
            S.barrier()
            if KSTOP == 'n1':
                break
            A = Arena(HTW)
            mixT = A.alloc(8 * T, BF16, "p (k t) -> p k t", k=8)
            mix_r = [[Res(), Res()] for _ in range(8)]
            qT = A.alloc(8 * T, BF16, "p (h t) -> p h t", h=8)
            qT_r = [[Res(), Res()] for _ in range(8)]
            kTa = A.alloc(2 * 1280, BF16, "p (h t) -> p h t", h=2)
            kTa_r = [Res(), Res()]
            vall = A.alloc(10 * 256, BF16, "p (t c) -> p t c", t=10)
            vall_r = Res()
            cs = A.alloc(2 * T, F32, "p (a t) -> p a t", a=2)
            r_cs = Res()
            atab = A.alloc(1280, BF16)
            btab = A.alloc(T, BF16)
            r_ab = Res()
            sq_ = A.alloc(512, BF16)
            rstd_ = A.alloc(512, F32)
            nslots = (sq_, Res(), rstd_, Res())
            qn = [A.alloc(512, F32) for _ in range(2)]
            qn_r = [Res(), Res()]
            qnb = [A.alloc(512, BF16) for _ in range(2)]
            qnb_r = [Res(), Res()]
            t1 = [A.alloc(512, F32) for _ in range(2)]
            t1_r = [Res(), Res()]
            t2 = [A.alloc(512, F32) for _ in range(2)]
            t2_r = [Res(), Res()]
            vst = [A.alloc(256, F32) for _ in range(2)]
            vst_r = [Res(), Res()]
            PT = [A.alloc(512, BF16) for _ in range(3)]
            PT_r = [Res() for _ in range(3)]
            rec = [A.alloc(512, F32) for _ in range(2)]
            rec_r = [Res(), Res()]
            if 'c' not in KSK:
                DMA("sp", cs, cs_in, wr=[r_cs], sem="cs")
            if 'a' not in KSK:
                DMA("pool", atab, atab_in, wr=[r_ab], sem="ab")
                DMA("pool", btab, btab_in, wr=[r_ab], sem="ab")
                DMA("pool", kTa[:, :, 1024:1280], ckT_in[l], wr=kTa_r, sem="ab")
                DMA("pool", vall[:, 8:10, :], cv_in[l], wr=[vall_r], sem="ab")
            sq2_ = A.alloc(512, BF16)
            rstd2_ = A.alloc(512, F32)
            nsl2 = [nslots, (sq2_, Res(), rstd2_, Res())]

            qk_cnt = [0]

            def qk_proj(kind, h, wt, wr_, par):
                for half in range(2):
                    b_ = par * 2 + half
                    proj_fm(wt, wr_, half, ps_t[b_], ps_r[b_])

            def tmp_ps():
                b_ = 4 + qk_cnt[0] % 3
                qk_cnt[0] += 1
                return ps_t[b_], ps_r[b_]

            def qk_iter(kind, h, half, par):
                hs = slice(half * 512, (half + 1) * 512)
                k = half
                ps, pr = ps_t[par * 2 + half], ps_r[par * 2 + half]
                sq, sq_r, rstd, rstd_r = nsl2[k]
                ACT(sq, ps[:], AF.Square, rd=[pr], wr=[sq_r])
                yield
                ps2, pr2 = tmp_ps()
                MM(ps2[:], ones_bf, sq, rd=[sq_r, r_cb], wr=[pr2])
                yield
                ACT(rstd, ps2[:], AF.Ln, rd=[pr2, r_eps], wr=[rstd_r], bias=epsb[:, 1:2])
                yield
                ACT(rstd, rstd, AF.Exp, rd=[rstd_r], wr=[rstd_r], scale=-0.5)
                yield
                gcol = gsm[:, l:l + 1] if kind == "aq" else gsm[:, 2 + l:3 + l]
                STT(qn[k], ps[:], gcol, rstd, ALU.mult, ALU.mult, rd=[pr, rstd_r, r_g], wr=[qn_r[k]])
                yield
                if kind == "ak" and 'k' not in KSK:
                    DMA("sp", kT_out[l, h, half], qn[k], rd=[qn_r[k]], sem=f"ko{k}")
                CP(qnb[k], qn[k], rd=[qn_r[k]], wr=[qnb_r[k]], eng="act")
                TT(t1[k], qn[k], cs[:, 0, hs], ALU.mult, rd=[qn_r[k], r_cs], wr=[t1_r[k]])
                yield
                ps3, pr3 = tmp_ps()
                MM(ps3[:], RmT_bf, qnb[k], rd=[qnb_r[k], r_cb], wr=[pr3])
                yield
                TT(t2[k], ps3[:], cs[:, 1, hs], ALU.mult, rd=[pr3, r_cs], wr=[t2_r[k]])
                yield
                if kind == "aq":
                    TT(qT[:, h, hs], t1[k], t2[k], ALU.add, rd=[t1_r[k], t2_r[k]], wr=[qT_r[h][half]])
                else:
                    TT(kTa[:, h, hs], t1[k], t2[k], ALU.add, rd=[t1_r[k], t2_r[k]], wr=[kTa_r[h]])

            heads = [("aq", h) for h in range(8)] + [("ak", h) for h in range(2)]
            wt0 = next_w((heads[0][0], l, heads[0][1]))
            qk_proj(heads[0][0], heads[0][1], wt0[0], wt0[1], 0)
            for hi, (kind, h) in enumerate(heads):
                if hi + 1 < len(heads):
                    kn, hn = heads[hi + 1]
                    wtn = next_w((kn, l, hn))
                    qk_proj(kn, hn, wtn[0], wtn[1], (hi + 1) % 2)
                run_il([qk_iter(kind, h, 0, hi % 2), qk_iter(kind, h, 1, hi % 2)]
                       + ([mod_gen(2, lag=3)] if (l == 0 and mod_next[0] < 48) else []))
            for h in range(2):
                wt, wr_ = next_w(("av", l, h))
                for tt in range(0 if 'v' in KSK else 8):
                    ps, pr = nextps()
                    for kc in range(KC):
                        MM(ps[:, 0:128], hT[:, kc, tt * 128:(tt + 1) * 128], wt[:, kc, :], start=(kc == 0),
                           stop=(kc == KC - 1), rd=[wr_, hT_r[tt // 4]], wr=[pr])
                    k = tt % 2
                    CP(vst[k][:, 0:128], ps[:, 0:128], rd=[pr], wr=[vst_r[k]], eng="act")
                    CP(vall[:, tt, h * 128:(h + 1) * 128], vst[k][:, 0:128], rd=[vst_r[k]], wr=[vall_r])
                    DMA("sp", v_out[l, h, tt], vst[k][:, 0:128], rd=[vst_r[k]], sem=f"vo{k}")
            if l == 0:
                dump("qT0", qT[:, 0, :], [qT_r[0][0], qT_r[0][1]], BF16)
                dump("kT0", kTa[:, 0, :], kTa_r, BF16)
            if KSTOP == 'ap':
                break
            items = [(hq, half, kt) for hq in range(8) for half in range(2) for kt in range(10)]
            sbank = [0, 1, 2]
            obank = [(3, 4), (5, 6)]

            def s_stage(i):
                hq, half, kt = items[i]
                kv = hq // 4
                hs = slice(half * 512, (half + 1) * 512)
                psS, prS = ps_t[sbank[i % 3]], ps_r[sbank[i % 3]]
                MM(psS[:], kTa[:, kv, kt * 128:(kt + 1) * 128], qT[:, hq, hs], start=True, stop=False,
                   rd=[kTa_r[kv], qT_r[hq][half]], wr=[prS])
                MM(psS[:], atab[:, kt * 128:(kt + 1) * 128], btab[:, hs], start=False, stop=True,
                   rd=[r_ab], wr=[prS])
                ACT(PT[i % 3], psS[:], AF.Exp, rd=[prS], wr=[PT_r[i % 3]])

            def pv_stage(i):
                hq, half, kt = items[i]
                kv = hq // 4
                hs = slice(half * 512, (half + 1) * 512)
                g_ = (i // 10) % 2
                psO, prO = ps_t[obank[g_][0]], ps_r[obank[g_][0]]
                psD, prD = ps_t[obank[g_][1]], ps_r[obank[g_][1]]
                p = i % 3
                MM(psO[:], vall[:, kt, kv * 128:(kv + 1) * 128], PT[p], start=(kt == 0), stop=(kt == 9),
                   rd=[vall_r, PT_r[p]], wr=[prO])
                MM(psD[:], ones_bf, PT[p], start=(kt == 0), stop=(kt == 9), rd=[PT_r[p], r_cb], wr=[prD])
                if kt == 9:
                    S.op("dve", (lambda o_, i_: (lambda e: e.reciprocal(out=o_, in_=i_)))(rec[g_], psD[:]),
                         [prD], [rec_r[g_]])
                    TT(mixT[:, hq, hs], psO[:], rec[g_], ALU.mult, rd=[prO, rec_r[g_]], wr=[mix_r[hq][half]])

            for i in range(len(items) + 1):
                if i < len(items):
                    s_stage(i)
                if i >= 1:
                    pv_stage(i - 1)
                pass
            if l == 0:
                dump("matt", mixT[:, 0, :], [mix_r[0][0], mix_r[0][1]], BF16)
            if l == 0:
                pump_mod(0, 48 - mod_next[0])
            wout_apply(l, "wo_att", 8, mixT, mix_r)
            S.barrier()
            if KSTOP == "att":
                break
            gla_phase(l)
            S.barrier()
            if KSTOP in ("gla", "gp"):
                break
            dn_phase(l)
            S.barrier()
            if KSTOP == "dn":
                break
            A = Arena(HTW)
            if l == 0:
                pump_mod(0, 96)
            rmsnorm_to_hT(l, 1, A)
            S.barrier()
            A = Arena(HTW)
            actT = A.alloc(16 * T, BF16, "p (k t) -> p k t", k=16)
            act_r = [[Res(), Res()] for _ in range(16)]
            rl = [A.alloc(512, F32) for _ in range(2)]
            rl_r = [Res(), Res()]
            ri = 0
            for g in range(4):
                for j in range(16):
                    if l == 0:
                        pump_mod(1, 1)
                    wt, wr_ = next_w(("ff1", l, g, j))
                    for half in range(2):
                        hs = slice(half * 512, (half + 1) * 512)
                        ps, pr = nextps()
                        proj_fm(wt, wr_, half, ps, pr)
                        k = ri % 2
                        ri += 1
                        ACT(rl[k], ps[:], AF.Relu, rd=[pr], wr=[rl_r[k]])
                        TT(actT[:, j, hs], rl[k], rl[k], ALU.mult, rd=[rl_r[k]], wr=[act_r[j][half]])
                for dc in range(16):
                    if l == 0:
                        pump_mod(1, 1)
                    wt, wr_ = next_w(("ff2", l, g, dc))
                    for half in range(2):
                        hs = slice(half * 512, (half + 1) * 512)
                        ps, pr = nextps()
                        for j in range(16):
                            MM(ps[:], wt[:, j, :], actT[:, j, hs], start=(j == 0), stop=(j == 15),
                               rd=[wr_, act_r[j][half]], wr=[pr])
                        STT(xT[:, dc, hs], ps[:], mods[:, l, 5 * 16 + dc: 5 * 16 + dc + 1], xT[:, dc, hs],
                            ALU.mult, ALU.add, rd=[pr, mod_r[l][5], xT_r[dc][half]], wr=[xT_r[dc][half]])

        S.barrier()
        A = Arena(0)
        yst = [A.alloc(D, F32) for _ in range(2)]
        yst_r = [Res(), Res()]
        for tt in range(8):
            sl = tt % 2
            half = tt // 4
            for c4 in range(4):
                ps, pr = nextps()
                for j in range(4):
                    c = c4 * 4 + j
                    TR(ps[:, j * 128:(j + 1) * 128], xT[:, c, tt * 128:(tt + 1) * 128], ident_f,
                       rd=[xT_r[c][half], r_c], wr=[pr])
                eng = "act" if c4 % 2 else "dve"
                CP(yst[sl][:, c4 * 512:(c4 + 1) * 512], ps[:], rd=[pr], wr=[yst_r[sl]], eng=eng)
            DMA("sp", y_out[tt * 128:(tt + 1) * 128, :], yst[sl], rd=[yst_r[sl]], sem=f"yo{sl}")
        S.wait_all_dma("sp")
        stats = S.emit(nc, st)
        print("ops", stats, "weights", wi_[0], "/", len(plan))
    assert len(plan_rec) == len(plan) and offs_rec == woffs, (len(plan_rec), len(plan))
    return nc, list(dbg_outs), plan_rec


_CACHE = {}
LAST_DBG = {}


def _host_consts():
    t = np.arange(64)
    f32 = np.float32
    TriL = (t[:, None] <= t[None, :]).astype(f32)
    TriU = (t[:, None] >= t[None, :]).astype(f32)
    TriCL = (t[:, None] > t[None, :]).astype(f32)
    TriCU = (t[:, None] < t[None, :]).astype(f32)
    NEG = f32(-1e4)
    i_, j_ = t[:, None], t[None, :]
    d = dict(TriL=TriL, TriU=TriU, ones=np.ones((64, 128), f32), TriCL=TriCL, TriCU=TriCU,
             TriS0=-TriL / 16.0, TriS1=-TriU / 16.0,
             mbS0=np.where(j_ < i_, 0, NEG).astype(f32), mbS1=np.where(j_ > i_, 0, NEG).astype(f32),
             mbIT0=np.where(i_ <= j_, 0, NEG).astype(f32), mbIT1=np.where(i_ >= j_, 0, NEG).astype(f32),
             mT0=(i_ <= j_).astype(f32), mT1=(i_ >= j_).astype(f32),
             negones=-np.ones((64, 64), f32), nTriL=-TriL, nTriU=-TriU, ident=np.eye(64, dtype=f32))
    return d


def _mz_const():
    ii = np.arange(64)
    ML = np.zeros((64, 6, 64), np.float32)
    for m in range(6):
        b = 1 << m
        same = (ii[:, None] // (2 * b)) == (ii[None, :] // (2 * b))
        ML[:, m, :] = (same & ((ii[:, None] % (2 * b)) >= b) & ((ii[None, :] % (2 * b)) < b)).astype(np.float32)
    MU = ML.transpose(2, 1, 0)
    mz = np.zeros((64, 2, 6, 2, 64), np.float32)
    mz[:, 0, :, 0], mz[:, 0, :, 1] = ML, MU
    mz[:, 1, :, 0], mz[:, 1, :, 1] = MU, ML
    return np.ascontiguousarray(mz.reshape(64, 2 * 6 * 128))


def kernel(**inp):
    f32 = np.float32
    inp = {k: np.asarray(v) for k, v in inp.items()}
    if "nc" not in _CACHE:
        _CACHE["nc"] = build_program()
    nc, dbg_names, plan = _CACHE["nc"]
    _p0, woffs = weight_plan()
    warr = {a: np.zeros(n, f32) for a, n in woffs.items()}
    for (a, key, src, l, row0, nk, col0, ncols, off) in plan:
        W = inp[src][l]
        blk = W[row0:row0 + nk * 128, col0:col0 + ncols].reshape(nk, 128, ncols).transpose(1, 0, 2)
        dst = warr[a][off: off + 128 * nk * 128].reshape(128, nk, 128)
        dst[:, :, :ncols] = blk
    hc = _host_consts()
    tpos = np.arange(T)
    inv = (np.float32(10000.0) ** (-np.arange(0, 64, 2, dtype=f32) / np.float32(64))).astype(f32)
    ang = np.zeros((128, T), f32)
    for dd in range(128):
        pos = (tpos // 64) if dd < 64 else (tpos % 64)
        ang[dd] = pos.astype(f32) * inv[dd % 32]
    cos_s, sin_s = np.cos(ang).astype(f32), np.sin(ang).astype(f32)
    RmT = np.zeros((128, 128), f32)
    for dd in range(128):
        if dd % 64 < 32:
            RmT[dd + 32, dd] = -1.0
        else:
            RmT[dd - 32, dd] = 1.0
    w2aug = np.zeros((2, 33, 512), f32)
    for l in range(2):
        for z in range(2):
            w2aug[l, z * 16:(z + 1) * 16, z * 256:(z + 1) * 256] = inp["gla_w2"][l, z]
            w2aug[l, 32, z * 256:(z + 1) * 256] = inp["gla_b"][l, z]
    in_maps = []
    for core in range(8):
        ctx = core < 4
        m = dict(warr)
        if ctx:
            m["x"] = np.ascontiguousarray(inp["x_prompt"][4 * core:4 * core + 4].reshape(T, D))
            cond = inp["c_ctx"]
        else:
            b = core - 4
            m["x"] = np.ascontiguousarray(inp["x_sample"][b])
            cond = inp["c"][b]
        cfa = np.zeros((128, NCF), f32)

        def put(name, arr):
            o, w = CF[name]
            cfa[:, o:o + w] = arr
        put("ident", np.eye(128, dtype=f32))
        put("cond", cond.reshape(16, 128).T)
        for l in range(2):
            put(f"bmod{l}", inp["b_mod"][l].reshape(96, 128).T)
            put(f"n1g{l}", inp["norm1_g"][l].reshape(16, 128).T)
            put(f"n2g{l}", inp["norm2_g"][l].reshape(16, 128).T)
            put(f"glag{l}", inp["gla_norm_g"][l][:, None])
            put(f"qg{l}", inp["q_norm_g"][l][:, None])
            put(f"kg{l}", inp["k_norm_g"][l][:, None])
            put(f"dng{l}", inp["dn_norm_g"][l][:, None])
            put(f"conv{l}", inp["dn_conv"][l].reshape(3, 12, 128).transpose(2, 1, 0).reshape(128, 36))
        put("carry", 0.0 if ctx else 1.0)
        put("cflag", 1.0 if ctx else 0.0)
        put("onesf", 1.0)
        put("identb", np.eye(128, dtype=f32))
        put("Rm", RmT)
        m["cf"] = cfa
        c64a = np.zeros((64, NC64), f32)
        for name, arr in hc.items():
            o, w = C64[name]
            c64a[:, o:o + w] = arr
        for l in range(2):
            o, w = C64[f"alog{l}"]
            c64a[:, o:o + w] = inp["dn_a_log"][l].reshape(8)[None, :]
            o, w = C64[f"dtb{l}"]
            c64a[:, o:o + w] = inp["dn_dt_bias"][l].reshape(8)[None, :]
        m["c64"] = c64a
        m["mz"] = _mz_const()
        m["w2aug"] = w2aug
        cs = np.zeros((128, 2, T), f32)
        at = np.zeros((128, 1280), f32)
        bt = np.zeros((128, T), f32)
        at[4, 1024:] = 1.0
        if ctx:
            cs[:, 0, :] = 1.0
            for s in range(4):
                at[s, s * 256:(s + 1) * 256] = 1.0
                bt[s, :] = -30000.0
                bt[s, s * 256:(s + 1) * 256] = 0.0
            bt[4, :] = -30000.0
            m["ckT"] = np.zeros((2, 128, 2, 256), f32)
            m["cv"] = np.zeros((2, 128, 2, 256), f32)
            m["sgla"] = np.zeros((2, 64, 2, 512), f32)
            m["sdn"] = np.zeros((2, 128, 2, 512), f32)
        else:
            b = core - 4
            cs[:, 0, :] = cos_s
            cs[:, 1, :] = sin_s
            at[0, :1024] = 1.0
            m["ckT"] = np.ascontiguousarray(inp["cache_k"][b].transpose(0, 3, 2, 1))
            m["cv"] = np.ascontiguousarray(
                inp["cache_v"][b].reshape(2, 2, 128, 256).transpose(0, 2, 1, 3))
            m["sgla"] = np.ascontiguousarray(inp["state_gla"][b].transpose(0, 3, 1, 2, 4)).reshape(2, 64, 2, 512)
            m["sdn"] = np.ascontiguousarray(inp["state_dn"][b].transpose(0, 3, 1, 2, 4)).reshape(2, 128, 2, 512)
        m["cossin"] = cs
        m["atab"] = at
        m["btab"] = bt
        in_maps.append(m)
    kc_ = os.environ.get("KCORES", "")
    if kc_:
        sel = [int(s) for s in kc_.split(",")]
        res = run_bass_kernel_spmd(nc, [in_maps[c] for c in sel], core_ids=list(range(len(sel))))
        R = [res.results[sel.index(c)] if c in sel else res.results[0] for c in range(8)]
    else:
        res = run_bass_kernel_spmd(nc, in_maps, core_ids=list(range(8)))
        R = res.results
    for n in dbg_names:
        LAST_DBG[n] = [np.asarray(R[c]["dbg_" + n]) for c in range(8)]
    y_prompt = np.stack([np.asarray(R[c]["y"]) for c in range(4)]).reshape(16, 256, D).astype(f32)
    y_sample = np.stack([np.asarray(R[c]["y"]) for c in range(4, 8)]).astype(f32)
    kT = np.stack([np.asarray(R[c]["kTo"]) for c in range(4)])
    kT = kT.transpose(0, 1, 2, 4, 3, 5).reshape(4, 2, 2, 128, 4, 256)
    nk = kT.transpose(0, 4, 1, 5, 2, 3).reshape(16, 2, 256, 2, 128)
    vo = np.stack([np.asarray(R[c]["vo"]) for c in range(4)])
    vo = vo.reshape(4, 2, 2, 4, 256, 128)
    nv = vo.transpose(0, 3, 1, 4, 2, 5).reshape(16, 2, 256, 2, 128)
    sgo = np.stack([np.asarray(R[c]["sgo"]) for c in range(4)])
    nsg = sgo.reshape(4, 2, 2, 4, 64, 4, 128).transpose(0, 3, 1, 2, 5, 4, 6).reshape(16, 2, 2, 4, 64, 128)
    sdo = np.stack([np.asarray(R[c]["sdo"]) for c in range(4)])
    nsd = sdo.reshape(4, 2, 2, 4, 128, 4, 128).transpose(0, 3, 1, 2, 5, 4, 6).reshape(16, 2, 2, 4, 128, 128)
    return (y_prompt, y_sample, np.ascontiguousarray(nk, dtype=f32), np.ascontiguousarray(nv, dtype=f32),
            np.ascontiguousarray(nsg, dtype=f32), np.ascontiguousarray(nsd, dtype=f32))
```

```python
import os
import numpy as np
from contextlib import ExitStack
import concourse.bass as bass
import concourse.mybir as mybir
from concourse.bass_utils import run_bass_kernel_spmd

F32 = mybir.dt.float32
BF16 = mybir.dt.bfloat16
AF = mybir.ActivationFunctionType
ALU = mybir.AluOpType

T = 1024
D = 2048
KC = 16
NCH = 16
EPS = 1e-6
NSLOT = 3
ENGS = ("pe", "act", "dve", "pool", "sp")
SEM_CAP = 30000
KDBG = os.environ.get("KDBG", "")
KSTOP = os.environ.get("KSTOP", "")
KSK = os.environ.get("KSK", "")
KGS = os.environ.get("KGS", "")
LAZYMOD = True
CHD = F32 if os.environ.get("KCHAIN", "bf16") == "f32" else BF16


class Res:
    __slots__ = ("w", "rs")

    def __init__(self):
        self.w = None
        self.rs = []


class Op:
    __slots__ = ("eng", "idx", "fn", "waits", "signal", "clock", "dma", "sig_no")

    def __init__(self, eng, idx, fn):
        self.eng, self.idx, self.fn = eng, idx, fn
        self.waits = []
        self.signal = False
        self.clock = None
        self.dma = None
        self.sig_no = None


class Sched:
    def __init__(self):
        self.ops = {e: [] for e in ENGS}
        self.clock = {e: {} for e in ENGS}
        self.dma_val = {}
        self.dma_clock = {}

    def _need(self, eng, dep, same_ok):
        ck = self.clock[eng]
        if dep[0] == "e":
            if dep[1] == eng and same_ok:
                return False
            return ck.get(dep[1], -1) < dep[2]
        return ck.get(("d", dep[1]), 0) < dep[2]

    def _merge(self, eng, dep):
        ck = self.clock[eng]
        if dep[0] == "e":
            op2 = self.ops[dep[1]][dep[2]]
            op2.signal = True
            src = op2.clock
            if ck.get(dep[1], -1) < dep[2]:
                ck[dep[1]] = dep[2]
        else:
            src = self.dma_clock[(dep[1], dep[2])]
            ck[("d", dep[1])] = dep[2]
        for k, v in src.items():
            if ck.get(k, -1) < v:
                ck[k] = v

    def op(self, eng, fn, reads=(), writes=(), dma_sem=None):
        lst = self.ops[eng]
        o = Op(eng, len(lst), fn)
        deps = []
        for r in reads:
            if r.w is not None:
                deps.append((r.w, False))
        for w in writes:
            if w.w is not None:
                deps.append((w.w, True))
            for d in w.rs:
                deps.append((d, True))
        agg = {}
        for dep, same_ok in deps:
            if dep[0] == "e":
                if dep[1] == eng and eng == "pe":
                    continue
                k = ("e", dep[1])
            else:
                k = ("d", dep[1])
            if k not in agg or agg[k][2] < dep[2]:
                agg[k] = dep
        for dep in agg.values():
            if self._need(eng, dep, False):
                o.waits.append(dep)
                self._merge(eng, dep)
        o.clock = dict(self.clock[eng])
        lst.append(o)
        if dma_sem is not None:
            v = self.dma_val.get(dma_sem, 0) + 16
            self.dma_val[dma_sem] = v
            o.dma = (dma_sem, v)
            self.dma_clock[(dma_sem, v)] = dict(o.clock)
            me = ("d", dma_sem, v)
        else:
            me = ("e", eng, o.idx)
        for r in reads:
            r.rs.append(me)
        for w in writes:
            w.w = me
            w.rs = []
        return o

    def barrier(self, engs=("pe", "act", "dve", "sp", "pool")):
        last = {}
        for e in engs:
            for o in reversed(self.ops[e]):
                if o.fn is not None and o.dma is None:
                    last[e] = o.idx
                    break
        dmas = dict(self.dma_val)
        for e in engs:
            lst = self.ops[e]
            o = Op(e, len(lst), None)
            for e2, i2 in last.items():
                dep = ("e", e2, i2)
                if self._need(e, dep, False):
                    o.waits.append(dep)
                    self._merge(e, dep)
            for sk, v in dmas.items():
                if sk.startswith("w"):
                    continue
                dep = ("d", sk, v)
                if self._need(e, dep, False):
                    o.waits.append(dep)
                    self._merge(e, dep)
            o.clock = dict(self.clock[e])
            lst.append(o)

    def wait_all_dma(self, eng="sp"):
        lst = self.ops[eng]
        o = Op(eng, len(lst), None)
        for sk, v in self.dma_val.items():
            o.waits.append(("d", sk, v))
        o.clock = dict(self.clock[eng])
        lst.append(o)

    def emit(self, nc, stack):
        nsig = {}
        for e in ENGS:
            n = 0
            for o in self.ops[e]:
                if o.signal:
                    n += 1
                    o.sig_no = n
            nsig[e] = n
        esems = {}
        for e in ENGS:
            k = max(1, (nsig[e] + SEM_CAP - 1) // SEM_CAP)
            esems[e] = [stack.enter_context(nc.semaphore(f"s_{e}{i}")) for i in range(k)]
        dsems = {sk: stack.enter_context(nc.semaphore(f"d_{sk}")) for sk in self.dma_val}
        ops = self.ops

        def sem_of(e, signo):
            return esems[e][(signo - 1) // SEM_CAP], (signo - 1) % SEM_CAP + 1

        def run(e, engobj):
            for o in ops[e]:
                for dep in o.waits:
                    if dep[0] == "e":
                        s, v = sem_of(dep[1], ops[dep[1]][dep[2]].sig_no)
                        engobj.wait_ge(s, v)
                    else:
                        engobj.wait_ge(dsems[dep[1]], dep[2])
                if o.fn is None:
                    continue
                ins = o.fn(engobj)
                if o.dma is not None:
                    ins.then_inc(dsems[o.dma[0]], 16)
                elif o.signal:
                    s, _ = sem_of(e, o.sig_no)
                    ins.then_inc(s, 1)

        block = stack.enter_context(nc.Block())

        @block.tensor
        def _(eng):
            run("pe", eng)

        @block.scalar
        def _(eng):
            run("act", eng)

        @block.vector
        def _(eng):
            run("dve", eng)

        @block.gpsimd
        def _(eng):
            run("pool", eng)

        @block.sync
        def _(eng):
            run("sp", eng)
        return {e: len(ops[e]) for e in ENGS}, nsig


W_IN_OFF = dict(gq=0, gk=256, gv=512, gr=1024, glr=1536, aq=1568, ak=2592, av=2848,
                dqkv=3104, dz=4640, dab=5152)


def weight_plan():
    plan = []
    for l in range(2):
        for j in range(96):
            plan.append(("wm", ("mod", l, j), "w_mod", l, 0, 16, j * 128, 128))
    for l in range(2):
        a = f"w{l}"

        def wi(key, col0, ncols=128):
            plan.append((a, key, "w_in", l, 0, 16, col0, ncols))

        def wo(key, row0, nk, dc):
            plan.append((a, key, "w_out", l, row0, nk, dc * 128, 128))
        for h in range(8):
            wi(("aq", l, h), W_IN_OFF["aq"] + h * 128)
        for h in range(2):
            wi(("ak", l, h), W_IN_OFF["ak"] + h * 128)
        for h in range(2):
            wi(("av", l, h), W_IN_OFF["av"] + h * 128)
        for dc in range(16):
            wo(("wo_att", l, dc), 512, 8, dc)
        for p in range(2):
            wi(("gq", l, p), W_IN_OFF["gq"] + p * 128)
        for p in range(2):
            wi(("gk", l, p), W_IN_OFF["gk"] + p * 128)
        for h in range(4):
            wi(("gr", l, h), W_IN_OFF["gr"] + h * 128)
        wi(("glr", l), W_IN_OFF["glr"], 32)
        for h in range(4):
            wi(("gv", l, h), W_IN_OFF["gv"] + h * 128)
        for dc in range(16):
            wo(("wo_gla", l, dc), 0, 4, dc)
        for cc in range(12):
            wi(("dqkv", l, cc), W_IN_OFF["dqkv"] + cc * 128)
        wi(("dab", l), W_IN_OFF["dab"], 16)
        for h in range(4):
            wi(("dz", l, h), W_IN_OFF["dz"] + h * 128)
        for dc in range(16):
            wo(("wo_dn", l, dc), 1536, 4, dc)
        for g in range(4):
            for j in range(16):
                plan.append((a, ("ff1", l, g, j), "w_ff1", l, 0, 16, (g * 16 + j) * 128, 128))
            for dc in range(16):
                plan.append((a, ("ff2", l, g, dc), "w_ff2", l, g * 2048, 16, dc * 128, 128))
    offs = {}
    out = []
    for (a, key, src, l, row0, nk, col0, ncols) in plan:
        off = offs.get(a, 0)
        out.append((a, key, src, l, row0, nk, col0, ncols, off))
        offs[a] = off + 128 * nk * 128
    return out, offs


CF = {}
_o = 0
for _n, _w in [("ident", 128), ("cond", 16), ("bmod0", 96), ("bmod1", 96), ("n1g0", 16), ("n1g1", 16),
               ("n2g0", 16), ("n2g1", 16), ("glag0", 1), ("glag1", 1), ("qg0", 1), ("qg1", 1),
               ("kg0", 1), ("kg1", 1), ("dng0", 1), ("dng1", 1), ("conv0", 36), ("conv1", 36),
               ("carry", 1), ("cflag", 1), ("onesf", 128), ("identb", 128), ("Rm", 128)]:
    CF[_n] = (_o, _w)
    _o += _w
NCF = _o
C64 = {}
_o = 0
for _n, _w in [("TriL", 64), ("TriU", 64), ("ones", 128), ("TriCL", 64), ("TriCU", 64), ("TriS0", 64),
               ("TriS1", 64), ("mbS0", 64), ("mbS1", 64), ("mbIT0", 64), ("mbIT1", 64), ("mT0", 64),
               ("mT1", 64), ("alog0", 8), ("alog1", 8), ("dtb0", 8), ("dtb1", 8), ("negones", 64),
               ("nTriL", 64), ("nTriU", 64), ("ident", 64)]:
    C64[_n] = (_o, _w)
    _o += _w
NC64 = _o


def build_program():
    nc = bass.Bass("TRN2", target_bir_lowering=False)
    plan, woffs = weight_plan()
    S = Sched()
    dbg_outs = {}

    def din(name, shape, dt=F32):
        return nc.dram_tensor(name, list(shape), dt, kind="ExternalInput").ap()

    def dout(name, shape, dt=F32):
        return nc.dram_tensor(name, list(shape), dt, kind="ExternalOutput").ap()

    wdram = {a: din(a, [n]) for a, n in woffs.items()}
    x_in = din("x", [T, D])
    cf_in = din("cf", [128, NCF])
    c64_in = din("c64", [64, NC64])
    mz_in = din("mz", [64, 2 * 6 * 128])
    w2aug_in = din("w2aug", [2, 33, 512])
    cs_in = din("cossin", [128, 2, T])
    atab_in = din("atab", [128, 1280])
    btab_in = din("btab", [128, T])
    ckT_in = din("ckT", [2, 128, 2, 256])
    cv_in = din("cv", [2, 128, 2, 256])
    sg_in = din("sgla", [2, 64, 2, 512])
    sd_in = din("sdn", [2, 128, 2, 512])
    y_out = dout("y", [T, D])
    kT_out = dout("kTo", [2, 2, 2, 128, 512])
    v_out = dout("vo", [2, 2, 8, 128, 128])
    sg_out = dout("sgo", [2, 2, 4, 64, 512])
    sd_out = dout("sdo", [2, 2, 4, 128, 512])

    with ExitStack() as st:
        def sbt(name, shape, dt):
            return st.enter_context(nc.sbuf_tensor(name, list(shape), dt))

        xT = sbt("xT", [128, KC, T], F32)
        wring = sbt("wring", [128, NSLOT, 16, 128], BF16)
        cf = sbt("cf_sb", [128, NCF], F32)
        c64 = sbt("c64_sb", [64, NC64], F32)
        cb = sbt("cb", [128, 3, 128], BF16)
        c64b = sbt("c64b", [64, 128], BF16)
        mzb_t = sbt("mzb", [64, 2 * 6 * 128], BF16)
        mzb = mzb_t[:].rearrange("p (z m v j) -> p z m v j", z=2, m=6, v=2)
        mods = sbt("mods", [128, 2, 96], F32)
        geff = sbt("geff", [128, 2, 2, 16], F32)
        gsm = sbt("gsm", [128, 16], F32)
        s_bf = sbt("s_bf", [128, 16], BF16)
        ARW = (nc.sbuf_bytes_remaining - 2048) // 4
        big = sbt("big", [128, ARW], F32)
        ps_t = [st.enter_context(nc.psum_tensor(f"ps{i}", [128, 512], F32)) for i in range(8)]
        ps_r = [Res() for _ in range(8)]
        ps_i = [0]

        def nextps():
            i = ps_i[0] % 7
            ps_i[0] += 1
            return ps_t[i], ps_r[i]

        class Arena:
            def __init__(self, base_words):
                self.p = base_words

            def alloc(self, nelem, dt, pat=None, **kw):
                sz = 4 if dt == F32 else 2
                nw = (nelem * sz + 3) // 4
                assert self.p + nw <= ARW, ("arena overflow", self.p + nw, ARW)
                ap = big[:, self.p:self.p + nw]
                self.p += nw
                if dt != F32:
                    ap = ap.bitcast(dt)
                    if ap.shape[1] != nelem:
                        ap = ap[:, 0:nelem]
                if pat:
                    ap = ap.rearrange(pat, **kw)
                return ap
        HTW = KC * T // 2
        hT = big[:, 0:HTW].bitcast(BF16).rearrange("p (k t) -> p k t", k=KC)

        def MM(out, lhsT, rhs, start=True, stop=True, rd=(), wr=()):
            S.op("pe", lambda e: e.matmul(out, lhsT=lhsT, rhs=rhs, start=start, stop=stop), rd, wr)

        def TR(out, in_, ident, rd=(), wr=()):
            S.op("pe", lambda e: e.transpose(out, in_, ident), rd, wr)

        def ACT(out, in_, func, rd=(), wr=(), scale=1.0, bias=0.0):
            S.op("act", lambda e: e.activation(out=out, in_=in_, func=func, bias=bias, scale=scale), rd, wr)

        def TT(out, in0, in1, op, rd=(), wr=(), eng="dve"):
            S.op(eng, lambda e: e.tensor_tensor(out=out, in0=in0, in1=in1, op=op), rd, wr)

        def TS(out, in0, s1, s2, op0, op1=None, rd=(), wr=(), eng="dve"):
            if op1 is None:
                S.op(eng, lambda e: e.tensor_scalar(out=out, in0=in0, scalar1=s1, scalar2=None, op0=op0), rd, wr)
            else:
                S.op(eng, lambda e: e.tensor_scalar(out=out, in0=in0, scalar1=s1, scalar2=s2, op0=op0, op1=op1), rd, wr)

        def STT(out, in0, scalar, in1, op0, op1, rd=(), wr=(), eng="dve"):
            S.op(eng, lambda e: e.scalar_tensor_tensor(out=out, in0=in0, scalar=scalar, in1=in1, op0=op0, op1=op1), rd, wr)

        def CP(out, in_, rd=(), wr=(), eng="dve"):
            if eng == "act":
                S.op("act", lambda e: e.copy(out=out, in_=in_), rd, wr)
            else:
                S.op(eng, lambda e: e.tensor_copy(out=out, in_=in_), rd, wr)

        def DMA(q, out, in_, rd=(), wr=(), sem="init"):
            S.op(q, lambda e: e.dma_start(out=out, in_=in_), rd, wr, dma_sem=sem)

        def dump(name, ap, rd, dt=F32):
            if name not in KDBG.split(","):
                return
            o = dout("dbg_" + name, list(ap.shape), dt)
            dbg_outs[name] = True
            DMA("sp", o, ap, rd=rd, sem="dbg")

        def cfs(name, a=None, b=None):
            o, w = CF[name]
            return cf[:, o + (a or 0): o + (w if b is None else b)]

        def c6(name, a=None, b=None, rows=64):
            o, w = C64[name]
            return c64[0:rows, o + (a or 0): o + (w if b is None else b)]

        r_c = Res()
        epsb = sbt("epsb", [128, 4], F32)
        r_eps = Res()
        for _i, _v in enumerate((D * EPS, 128.0 * EPS, EPS, 1.0)):
            S.op("dve", (lambda t_, v_: (lambda e: e.memset(t_, v_)))(epsb[:, _i:_i + 1], float(_v)), [], [r_eps])
        EPSCOL = {float(D * EPS): 0, float(128.0 * EPS): 1, float(EPS): 2}

        def RSTD(out, in_, eps_tot, rd, wr):
            c_ = EPSCOL[float(eps_tot)]
            ACT(out, in_, AF.Ln, rd=list(rd) + [r_eps], wr=wr, bias=epsb[:, c_:c_ + 1])
            ACT(out, out, AF.Exp, rd=wr, wr=wr, scale=-0.5)
        wres = [Res() for _ in range(NSLOT)]
        wi_ = [0]

        plan_by_key = {e[1]: e for e in plan}
        plan_rec = []
        offs_rec = {}

        def next_w(key):
            i = wi_[0]
            wi_[0] += 1
            a, k2, src, l, row0, nk, col0, ncols, _off = plan_by_key[key]
            off = offs_rec.get(a, 0)
            offs_rec[a] = off + 128 * nk * 128
            plan_rec.append((a, key, src, l, row0, nk, col0, ncols, off))
            slot = i % NSLOT
            srcap = wdram[a][off: off + 128 * nk * 128].rearrange("(p k n) -> p k n", p=128, k=nk)
            DMA("pool", wring[:, slot, 0:nk, :], srcap, wr=[wres[slot]], sem=f"w{slot}")
            return wring[:, slot], wres[slot]

        DMA("sp", cf[:], cf_in, wr=[r_c])
        DMA("sp", c64[:], c64_in, wr=[r_c])
        r_cb = Res()
        CP(cb[:, 0, :], cfs("onesf"), rd=[r_c], wr=[r_cb])
        CP(cb[:, 1, :], cfs("identb"), rd=[r_c], wr=[r_cb])
        CP(cb[:, 2, :], cfs("Rm"), rd=[r_c], wr=[r_cb])
        CP(c64b[:, 0:64], c6("mT0"), rd=[r_c], wr=[r_cb])
        CP(c64b[:, 64:128], c6("mT1"), rd=[r_c], wr=[r_cb])
        DMA("pool", mzb_t[:], mz_in, wr=[r_cb], sem="mz")
        ones_bf = cb[:, 0, :]
        ident_bf = cb[:, 1, :]
        RmT_bf = cb[:, 2, :]
        ident_f = cfs("ident")

        xT_r = [[Res(), Res()] for _ in range(KC)]
        A = Arena(HTW)
        xin = [A.alloc(D, F32) for _ in range(2)]
        xin_r = [Res(), Res()]
        for tt in range(8):
            sl = tt % 2
            DMA("sp", xin[sl], x_in[tt * 128:(tt + 1) * 128, :], wr=[xin_r[sl]], sem=f"xin{sl}")
            for c4 in range(4):
                ps, pr = nextps()
                for j in range(4):
                    c = c4 * 4 + j
                    TR(ps[:, j * 128:(j + 1) * 128], xin[sl][:, c * 128:(c + 1) * 128], ident_f,
                       rd=[xin_r[sl], r_c], wr=[pr])
                half = tt // 4
                eng = "act" if c4 % 2 else "dve"
                CP(xT[:, c4 * 4:(c4 + 1) * 4, tt * 128:(tt + 1) * 128],
                   ps[:].rearrange("p (j t) -> p j t", j=4), rd=[pr],
                   wr=[xT_r[c4 * 4 + j][half] for j in range(4)], eng=eng)

        NL = 0 if KSTOP in ('x', 'mod') else 2
        NMOD = 0 if KSTOP == 'x' else 2
        r_s = Res()
        sg_t = A.alloc(16, F32)
        ACT(sg_t, cfs("cond"), AF.Silu, rd=[r_c], wr=[r_s])
        CP(s_bf[:], sg_t, rd=[r_s], wr=[r_s])
        mod_r = [[Res() for _ in range(6)] for _ in range(2)]
        geff_r = [[Res(), Res()] for _ in range(2)]
        psM, prM = ps_t[7], ps_r[7]
        mod_next = [0, 0]

        def mod_finalize(l, m):
            TT(mods[:, l, m * 16:(m + 1) * 16], psM[:, l * 96 + m * 16: l * 96 + (m + 1) * 16],
               cfs(f"bmod{l}", m * 16, (m + 1) * 16), ALU.add, rd=[prM, r_c], wr=[mod_r[l][m]])
            if m in (1, 4):
                ni = 0 if m == 1 else 1
                gn = f"n1g{l}" if m == 1 else f"n2g{l}"
                STT(geff[:, l, ni, :], mods[:, l, m * 16:(m + 1) * 16], 1.0, cfs(gn), ALU.add, ALU.mult,
                    rd=[mod_r[l][m], r_c], wr=[geff_r[l][ni]])
                TS(geff[:, l, ni, :], geff[:, l, ni, :], float(np.sqrt(D)), None, ALU.mult,
                   rd=[geff_r[l][ni]], wr=[geff_r[l][ni]])

        def pump_mod(l, n):
            if NMOD == 0:
                return
            for _ in range(n):
                j = mod_next[l]
                if j >= 96:
                    return
                mod_next[l] = j + 1
                wt, wr_ = next_w(("mod", l, j))
                col = l * 96 + j
                for kc in range(KC):
                    MM(psM[:, col:col + 1], wt[:, kc, :], s_bf[:, kc:kc + 1], start=(kc == 0), stop=(kc == KC - 1),
                       rd=[wr_, r_s], wr=[prM])
                if j % 16 == 15:
                    mod_finalize(l, j // 16)

        pump_mod(0, 32 if LAZYMOD else 96)
        if not LAZYMOD:
            pump_mod(0, 96)
            pump_mod(1, 96)

        def mod_gen(n, lag=5):
            for _ in range(n):
                l_ = 0 if mod_next[0] < 96 else 1
                j = mod_next[l_]
                if NMOD == 0 or j >= 96:
                    return
                mod_next[l_] = j + 1
                wt, wr_ = next_w(("mod", l_, j))
                for _k in range(lag):
                    yield
                col = l_ * 96 + j
                for kc in range(KC):
                    MM(psM[:, col:col + 1], wt[:, kc, :], s_bf[:, kc:kc + 1], start=(kc == 0), stop=(kc == KC - 1),
                       rd=[wr_, r_s], wr=[prM])
                if j % 16 == 15:
                    mod_finalize(l_, j // 16)
                yield
        r_g = Res()
        for l in range(2):
            CP(gsm[:, l:l + 1], cfs(f"qg{l}"), rd=[r_c], wr=[r_g])
            TS(gsm[:, 2 + l:3 + l], cfs(f"kg{l}"), float(np.sqrt(128.0)), None, ALU.mult, rd=[r_c], wr=[r_g])
            TS(gsm[:, 4 + l:5 + l], cfs(f"glag{l}"), float(np.sqrt(128.0)), None, ALU.mult, rd=[r_c], wr=[r_g])
            TS(gsm[:, 6 + l:7 + l], cfs(f"dng{l}"), float(np.sqrt(128.0)), None, ALU.mult, rd=[r_c], wr=[r_g])
        if KSTOP in ('mod', 'n1') or 'mods' in KDBG:
            pump_mod(0, 96)
        dump("mods", mods[:].rearrange("p l m -> p (l m)"), mod_r[0])

        hT_r = [Res(), Res()]

        def rmsnorm_to_hT(l, ni, A):
            sh = 0 if ni == 0 else 3
            sq = [A.alloc(512, BF16) for _ in range(2)]
            sq_r = [Res(), Res()]
            rstd = [A.alloc(512, F32) for _ in range(2)]
            rstd_r = [Res(), Res()]
            tmp = [A.alloc(512, F32) for _ in range(2)]
            tmp_r = [Res(), Res()]
            for half in range(2):
                hs = slice(half * 512, (half + 1) * 512)
                ps, pr = nextps()
                for c in range(KC):
                    k = c % 2
                    if c % 2:
                        TT(sq[k], xT[:, c, hs], xT[:, c, hs], ALU.mult, rd=[xT_r[c][half]], wr=[sq_r[k]])
                    else:
                        ACT(sq[k], xT[:, c, hs], AF.Square, rd=[xT_r[c][half]], wr=[sq_r[k]])
                    MM(ps[:], ones_bf, sq[k], start=(c == 0), stop=(c == KC - 1), rd=[sq_r[k], r_cb], wr=[pr])
                RSTD(rstd[half], ps[:], D * EPS, [pr], [rstd_r[half]])
                for c in range(KC):
                    k = c % 2
                    if c % 4 == 3:
                        STT(tmp[k], xT[:, c, hs], geff[:, l, ni, c:c + 1], rstd[half], ALU.mult, ALU.mult,
                            rd=[xT_r[c][half], rstd_r[half], geff_r[l][ni]], wr=[tmp_r[k]])
                        TS(hT[:, c, hs], tmp[k], mods[:, l, sh * 16 + c: sh * 16 + c + 1], None, ALU.add,
                           rd=[tmp_r[k], mod_r[l][sh]], wr=[hT_r[half]])
                    else:
                        TT(tmp[k], xT[:, c, hs], rstd[half], ALU.mult, rd=[xT_r[c][half], rstd_r[half]], wr=[tmp_r[k]])
                        ACT(hT[:, c, hs], tmp[k], AF.Identity, rd=[tmp_r[k], geff_r[l][ni], mod_r[l][sh]], wr=[hT_r[half]],
                            scale=geff[:, l, ni, c:c + 1], bias=mods[:, l, sh * 16 + c: sh * 16 + c + 1])

        def proj_fm(wt, wr_, half, ps, pr, mcols=128):
            hs = slice(half * 512, (half + 1) * 512)
            for kc in range(KC):
                MM(ps[0:mcols, :], wt[:, kc, 0:mcols], hT[:, kc, hs], start=(kc == 0), stop=(kc == KC - 1),
                   rd=[wr_, hT_r[half]], wr=[pr])

        def wout_apply(l, key, nk, mixT, mix_r):
            for dc in range(16):
                wt, wr_ = next_w((key, l, dc))
                for half in range(2):
                    hs = slice(half * 512, (half + 1) * 512)
                    ps, pr = nextps()
                    for k in range(nk):
                        MM(ps[:], wt[:, k, :], mixT[:, k, hs], start=(k == 0), stop=(k == nk - 1),
                           rd=[wr_, mix_r[k][half]], wr=[pr])
                    STT(xT[:, dc, hs], ps[:], mods[:, l, 2 * 16 + dc: 2 * 16 + dc + 1], xT[:, dc, hs], ALU.mult, ALU.add,
                        rd=[pr, mod_r[l][2], xT_r[dc][half]], wr=[xT_r[dc][half]])

        def headnorm_fm(src_ap, src_r, A_slots, eps_tot):
            sq, sq_r, rstd, rstd_r = A_slots
            ACT(sq, src_ap, AF.Square, rd=src_r, wr=[sq_r])
            ps2, pr2 = nextps()
            MM(ps2[:], ones_bf, sq, rd=[sq_r, r_cb], wr=[pr2])
            RSTD(rstd, ps2[:], eps_tot, [pr2], [rstd_r])
            return rstd, rstd_r


        def run_il(gens):
            act_ = list(gens)
            while act_:
                for g_ in list(act_):
                    try:
                        next(g_)
                    except StopIteration:
                        act_.remove(g_)

        def mix_epilogue(A2, oacc, oacc_r, mixT, mix_r, gcol):
            NS = 3
            slots = [(A2.alloc(512, BF16), Res(), A2.alloc(512, F32), Res(), A2.alloc(512, F32), Res()) for _ in range(NS)]

            def it(h, half, k):
                hs = slice(half * 512, (half + 1) * 512)
                orr = oacc_r[half * 8:(half + 1) * 8]
                sq, sq_r, rstd, rstd_r, tmp, tmp_r = slots[k]
                ACT(sq, oacc[:, h, hs], AF.Square, rd=orr, wr=[sq_r])
                yield
                ps2, pr2 = nextps()
                MM(ps2[:], ones_bf, sq, rd=[sq_r, r_cb], wr=[pr2])
                yield
                ACT(rstd, ps2[:], AF.Ln, rd=[pr2, r_eps], wr=[rstd_r], bias=epsb[:, 1:2])
                yield
                ACT(rstd, rstd, AF.Exp, rd=[rstd_r], wr=[rstd_r], scale=-0.5)
                yield
                TT(tmp, oacc[:, h, hs], rstd, ALU.mult, rd=orr + [rstd_r], wr=[tmp_r])
                yield
                STT(mixT[:, h, hs], tmp, gcol, mixT[:, h, hs], ALU.mult, ALU.mult,
                    rd=[tmp_r, r_g, mix_r[h][half]], wr=[mix_r[h][half]])
            items = [(h, half) for h in range(4) for half in range(2)]
            for g0 in range(0, 8, NS):
                run_il([it(h, half, k) for k, (h, half) in enumerate(items[g0:g0 + NS])])

        def gla_phase(l):
            A = Arena(HTW)
            mixT = A.alloc(4 * T, BF16, "p (k t) -> p k t", k=4)
            mix_r = [[Res(), Res()] for _ in range(4)]
            qraw = A.alloc(4 * T, BF16, "p (h t) -> p h t", h=4)
            kraw = A.alloc(4 * T, BF16, "p (h t) -> p h t", h=4)
            qk_r = Res()
            vtok = A.alloc(16 * 512, BF16, "p (c n) -> p c n", c=16)
            vt_r = [Res() for _ in range(16)]
            oacc = A.alloc(4 * T, F32, "p (h t) -> p h t", h=4)
            oacc_r = [Res() for _ in range(16)]
            Sg = A.alloc(2 * 512, F32, "p (z n) -> p z n", z=2)
            Sg_r = [Res(), Res()]
            Sgb = A.alloc(2 * 512, BF16, "p (z n) -> p z n", z=2)
            Sgb_r = [Res(), Res()]
            lrT = A.alloc(T, F32)
            lr_r = Res()
            w2a = A.alloc(512, F32)
            r_w2 = Res()
            e1 = [A.alloc(256, F32) for _ in range(2)]
            sp = [A.alloc(256, F32) for _ in range(2)]
            eb = [A.alloc(256, F32) for _ in range(2)]
            enb = [A.alloc(256, F32) for _ in range(2)]
            qt = [A.alloc(256, BF16) for _ in range(2)]
            kt = [A.alloc(256, BF16) for _ in range(2)]
            ATb = [A.alloc(256, BF16) for _ in range(2)]
            ktok = [A.alloc(256, BF16) for _ in range(2)]
            sst = [A.alloc(512, F32) for _ in range(2)]
            sst_r = [Res(), Res()]
            rr = {n: [Res(), Res()] for n in ("e1", "sp", "eb", "enb", "qt", "kt", "AT", "ktok")}
            DMA("sp", Sg[0:64], sg_in[l], wr=Sg_r, sem="sgi")
            for z in range(2):
                CP(Sgb[0:64, z, :], Sg[0:64, z, :], rd=[Sg_r[z]], wr=[Sgb_r[z]], eng="act")
            DMA("sp", w2a[0:33, :], w2aug_in[l], wr=[r_w2], sem="g2i")
            S.op("dve", lambda e: e.memset(lrT[32:33, :], 1.0), [], [lr_r])
            ei = 0
            for kind, dst in (("gq", qraw), ("gk", kraw)):
                for p in range(2):
                    wt, wr_ = next_w((kind, l, p))
                    for hp in range(2):
                        h = 2 * p + hp
                        for half in range(2):
                            hs = slice(half * 512, (half + 1) * 512)
                            ps, pr = nextps()
                            for kc in range(KC):
                                MM(ps[0:64, :], wt[:, kc, hp * 64:(hp + 1) * 64], hT[:, kc, hs], start=(kc == 0),
                                   stop=(kc == KC - 1), rd=[wr_, hT_r[half]], wr=[pr])
                            CP(dst[0:64, h, hs], ps[0:64, :], rd=[pr], wr=[qk_r], eng=("act" if ei % 2 else "dve"))
                            ei += 1
            for h in range(4):
                wt, wr_ = next_w(("gr", l, h))
                for half in range(2):
                    hs = slice(half * 512, (half + 1) * 512)
                    ps, pr = nextps()
                    proj_fm(wt, wr_, half, ps, pr)
                    ACT(mixT[:, h, hs], ps[:], AF.Silu, rd=[pr], wr=[mix_r[h][half]])
            wt, wr_ = next_w(("glr", l))
            for half in range(2):
                hs = slice(half * 512, (half + 1) * 512)
                ps, pr = nextps()
                proj_fm(wt, wr_, half, ps, pr, mcols=32)
                CP(lrT[0:32, hs], ps[0:32, :], rd=[pr], wr=[lr_r])
            for h in range(4):
                wt, wr_ = next_w(("gv", l, h))
                for c in range(16):
                    ps, pr = nextps()
                    for kc in range(KC):
                        MM(ps[0:64, 0:128], hT[:, kc, c * 64:(c + 1) * 64], wt[:, kc, :], start=(kc == 0),
                           stop=(kc == KC - 1), rd=[wr_, hT_r[c // 8]], wr=[pr])
                    CP(vtok[0:64, c, h * 128:(h + 1) * 128], ps[0:64, 0:128], rd=[pr], wr=[vt_r[c]],
                       eng=("act" if c % 2 else "dve"))
            if KSTOP == 'gp':
                return
            owritten = [False] * 16
            sso = [0]
            h4 = lambda ap: ap.rearrange("p (h i) -> p h i", h=4)
            id64 = ident_bf[0:64, 0:64]

            def cut(n):
                return KGS != "" and n >= int(KGS)

            def step(z, c):
                cs_ = slice(c * 64, (c + 1) * 64)
                ps1, pr1 = nextps()
                MM(ps1[0:64, 0:256], lrT[0:33, cs_], w2a[0:33, z * 256:(z + 1) * 256], rd=[lr_r, r_w2], wr=[pr1])
                ACT(e1[z][0:64, :], ps1[0:64, 0:256], AF.Exp, rd=[pr1], wr=[rr["e1"][z]], scale=-1.0)
                ACT(sp[z][0:64, :], e1[z][0:64, :], AF.Ln, rd=[rr["e1"][z], r_eps], wr=[rr["sp"][z]], bias=epsb[0:64, 3:4])
                if cut(1):
                    return
                yield
                psb, prb = nextps()
                for h in range(4):
                    MM(psb[0:64, h * 64:(h + 1) * 64], sp[z][0:64, h * 64:(h + 1) * 64], c6(f"TriS{z}"),
                       rd=[rr["sp"][z], r_c], wr=[prb])
                ACT(eb[z][0:64, :], psb[0:64, 0:256], AF.Exp, rd=[prb], wr=[rr["eb"][z]])
                ACT(enb[z][0:64, :], psb[0:64, 0:256], AF.Exp, rd=[prb], wr=[rr["enb"][z]], scale=-1.0)
                if cut(2):
                    return
                yield
                STT(h4(qt[z][0:64, :]), h4(eb[z][0:64, :]), 0.125, qraw[0:64, :, cs_], ALU.mult, ALU.mult,
                    rd=[rr["eb"][z], qk_r], wr=[rr["qt"][z]])
                TT(h4(kt[z][0:64, :]), h4(enb[z][0:64, :]), kraw[0:64, :, cs_], ALU.mult,
                   rd=[rr["enb"][z], qk_r], wr=[rr["kt"][z]])
                if cut(3):
                    return
                yield
                psA, prA = nextps()
                for h in range(4):
                    hc = slice(h * 64, (h + 1) * 64)
                    MM(psA[0:64, hc], kt[z][0:64, hc], qt[z][0:64, hc], rd=[rr["kt"][z], rr["qt"][z]], wr=[prA])
                yield
                TT(h4(ATb[z][0:64, :]), h4(psA[0:64, 0:256]),
                   c64b[:, z * 64:(z + 1) * 64].unsqueeze(1).to_broadcast([64, 4, 64]), ALU.mult,
                   rd=[prA, r_cb], wr=[rr["AT"][z]])
                if cut(4):
                    return
                yield
                pst, prt = nextps()
                pstb = pst[:].bitcast(BF16)
                for h in range(4):
                    hc = slice(h * 64, (h + 1) * 64)
                    TR(pstb[0:64, hc], kt[z][0:64, hc], id64, rd=[rr["kt"][z], r_cb], wr=[prt])
                yield
                CP(ktok[z][0:64, :], pstb[0:64, 0:256], rd=[prt], wr=[rr["ktok"][z]], eng="act")
                if cut(5):
                    return
                if (z == 0 and c % 4 == 0 and c > 0) or (z == 1 and c % 4 == 3 and c < 15):
                    TS(Sg[0:64, z, :], Sg[0:64, z, :], cfs("carry")[0:64, :], None, ALU.mult, rd=[Sg_r[z], r_c], wr=[Sg_r[z]])
                    CP(Sgb[0:64, z, :], Sg[0:64, z, :], rd=[Sg_r[z]], wr=[Sgb_r[z]], eng="act")
                yield
                psO, prO = nextps()
                for h in range(4):
                    hc = slice(h * 64, (h + 1) * 64)
                    MM(psO[:, hc], vtok[0:64, c, h * 128:(h + 1) * 128], ATb[z][0:64, hc], start=True, stop=False,
                       rd=[vt_r[c], rr["AT"][z]], wr=[prO])
                    MM(psO[:, hc], Sgb[0:64, z, h * 128:(h + 1) * 128], qt[z][0:64, hc], start=False, stop=True,
                       rd=[Sgb_r[z], rr["qt"][z]], wr=[prO])
                yield
                ov = oacc[:, :, cs_]
                pv = h4(psO[:, 0:256])
                if not owritten[c]:
                    CP(ov, pv, rd=[prO], wr=[oacc_r[c]])
                    owritten[c] = True
                else:
                    TT(ov, pv, ov, ALU.add, rd=[prO, oacc_r[c]], wr=[oacc_r[c]])
                if cut(6):
                    return
                yield
                psS, prS = nextps()
                for h in range(4):
                    MM(psS[0:64, h * 128:(h + 1) * 128], ktok[z][0:64, h * 64:(h + 1) * 64],
                       vtok[0:64, c, h * 128:(h + 1) * 128], rd=[rr["ktok"][z], vt_r[c]], wr=[prS])
                yield
                TT(Sg[0:64, z, :], psS[0:64, :], Sg[0:64, z, :], ALU.add, rd=[prS, Sg_r[z]], wr=[Sg_r[z]])
                col = 63 if z == 0 else 0
                hd = lambda ap: ap.rearrange("p (h d) -> p h d", h=4)
                TT(hd(Sg[0:64, z, :]), hd(Sg[0:64, z, :]),
                   h4(eb[z][0:64, :])[:, :, col:col + 1].to_broadcast([64, 4, 128]), ALU.mult,
                   rd=[Sg_r[z], rr["eb"][z]], wr=[Sg_r[z]])
                CP(Sgb[0:64, z, :], Sg[0:64, z, :], rd=[Sg_r[z]], wr=[Sgb_r[z]], eng="act")
                if (z == 0 and c % 4 == 3) or (z == 1 and c % 4 == 0):
                    k = sso[0] % 2
                    sso[0] += 1
                    CP(sst[k][0:64, :], Sg[0:64, z, :], rd=[Sg_r[z]], wr=[sst_r[k]])
                    DMA("sp", sg_out[l, z, c // 4], sst[k][0:64, :], rd=[sst_r[k]], sem=f"sso{k}")

            for s in range(16):
                run_il([step(0, s), step(1, 15 - s)])
            S.barrier()
            if KGS != "":
                return
            A2 = Arena(HTW + 4 * T // 2)
            mix_epilogue(A2, oacc, oacc_r, mixT, mix_r, gsm[:, 4 + l:5 + l])
            if l == 0:
                dump("mgla", mixT[:, 0, :], [mix_r[0][0], mix_r[0][1]], BF16)
            wout_apply(l, "wo_gla", 4, mixT, mix_r)

        def dn_phase(l):
            A = Arena(HTW)
            mixT = A.alloc(4 * T, BF16, "p (k t) -> p k t", k=4)
            mix_r = [[Res(), Res()] for _ in range(4)]
            qkvT = A.alloc(8 * T, BF16, "p (c t) -> p c t", c=8)
            qkv_r = [Res() for _ in range(12)]
            oacc = A.alloc(4 * T, F32, "p (h t) -> p h t", h=4)
            oacc_r = [Res() for _ in range(16)]
            Sd = A.alloc(2 * 512, F32, "p (z n) -> p z n", z=2)
            Sd_r = [Res(), Res()]
            Sdb = A.alloc(2 * 512, BF16, "p (z n) -> p z n", z=2)
            Sdb_r = [Res(), Res()]
            gab = A.alloc(256, F32, "p (c n) -> p c n", c=16)
            gtmp = A.alloc(128, F32, "p (c n) -> p c n", c=16)
            gall = A.alloc(128, F32, "p (c n) -> p c n", c=16)
            beta = A.alloc(128, F32, "p (c n) -> p c n", c=16)
            nbeta = A.alloc(128, F32, "p (c n) -> p c n", c=16)
            nexpA = A.alloc(8, F32)
            nb = A.alloc(36, F32)
            r_gb = Res()
            vT = A.alloc(4 * T, BF16, "p (c t) -> p c t", c=4)
            mark = A.p
            raw = [A.alloc(1026, F32) for _ in range(2)]
            raw_r = [Res(), Res()]
            yv = [A.alloc(1024, F32) for _ in range(2)]
            yv_r = [Res(), Res()]
            sq_ = A.alloc(512, BF16)
            rstd_ = A.alloc(512, F32)
            nsl = (sq_, Res(), rstd_, Res())
            DMA("sp", Sd, sd_in[l], wr=Sd_r, sem="sdi")
            for z in range(2):
                CP(Sdb[:, z, :], Sd[:, z, :], rd=[Sd_r[z]], wr=[Sdb_r[z]], eng="act")
            TS(nb, cfs(f"conv{l}"), cfs("cflag"), -1.0, ALU.mult, ALU.mult, rd=[r_c], wr=[r_gb])
            for k in range(2):
                S.op("dve", (lambda t_: (lambda e: e.memset(t_, 0.0)))(raw[k][:, 0:1]), [], [raw_r[k]])
                S.op("dve", (lambda t_: (lambda e: e.memset(t_, 0.0)))(raw[k][:, 1025:1026]), [], [raw_r[k]])
            cw = cfs(f"conv{l}")
            def dn_proj(cc, wt, wr_):
                k = cc % 2
                for half in range(2):
                    ps, pr = nextps()
                    proj_fm(wt, wr_, half, ps, pr)
                    CP(raw[k][:, 1 + half * 512: 1 + (half + 1) * 512], ps[:], rd=[pr], wr=[raw_r[k]],
                       eng=("act" if half else "dve"))

            def dn_tail(cc):
                k = cc % 2
                TS(yv[k], raw[k][:, 1:1025], cw[:, cc * 3 + 1: cc * 3 + 2], None, ALU.mult, rd=[raw_r[k], r_c], wr=[yv_r[k]])
                STT(yv[k], raw[k][:, 0:1024], cw[:, cc * 3: cc * 3 + 1], yv[k], ALU.mult, ALU.add,
                    rd=[raw_r[k], r_c, yv_r[k]], wr=[yv_r[k]])
                STT(yv[k], raw[k][:, 2:1026], cw[:, cc * 3 + 2: cc * 3 + 3], yv[k], ALU.mult, ALU.add,
                    rd=[raw_r[k], r_c, yv_r[k]], wr=[yv_r[k]])
                STT(yv[k][:, 256:1024:256], raw[k][:, 256:1024:256], nb[:, cc * 3: cc * 3 + 1], yv[k][:, 256:1024:256],
                    ALU.mult, ALU.add, rd=[raw_r[k], r_gb, yv_r[k]], wr=[yv_r[k]])
                STT(yv[k][:, 255:1023:256], raw[k][:, 257:1025:256], nb[:, cc * 3 + 2: cc * 3 + 3],
                    yv[k][:, 255:1023:256], ALU.mult, ALU.add, rd=[raw_r[k], r_gb, yv_r[k]], wr=[yv_r[k]])
                if cc >= 8:
                    ACT(vT[:, cc - 8, :], yv[k], AF.Silu, rd=[yv_r[k]], wr=[qkv_r[cc]])
                else:
                    ACT(yv[k], yv[k], AF.Silu, rd=[yv_r[k]], wr=[yv_r[k]])
                    for half in range(2):
                        hs = slice(half * 512, (half + 1) * 512)
                        rstd, rstd_r = headnorm_fm(yv[k][:, hs], [yv_r[k]], nsl, EPS)
                        if cc < 4:
                            STT(qkvT[:, cc, hs], yv[k][:, hs], float(128.0 ** -0.5), rstd, ALU.mult, ALU.mult,
                                rd=[yv_r[k], rstd_r], wr=[qkv_r[cc]])
                        else:
                            TT(qkvT[:, cc, hs], yv[k][:, hs], rstd, ALU.mult, rd=[yv_r[k], rstd_r], wr=[qkv_r[cc]])

            w_ = next_w(("dqkv", l, 0))
            dn_proj(0, *w_)
            for cc in range(12):
                if cc + 1 < 12:
                    w_ = next_w(("dqkv", l, cc + 1))
                    dn_proj(cc + 1, *w_)
                dn_tail(cc)
            wt, wr_ = next_w(("dab", l))
            psg, prg = nextps()
            for c in range(16):
                for kc in range(KC):
                    MM(psg[0:64, c * 16:(c + 1) * 16], hT[:, kc, c * 64:(c + 1) * 64], wt[:, kc, 0:16],
                       start=(kc == 0), stop=(kc == KC - 1), rd=[wr_, hT_r[c // 8]], wr=[prg])
            CP(gab[0:64], psg[0:64, 0:256].rearrange("p (c n) -> p c n", c=16), rd=[prg], wr=[r_gb])
            bc8 = lambda ap: ap.unsqueeze(1).to_broadcast([64, 16, 8])
            TT(gtmp[0:64], gab[0:64, :, 0:8], bc8(c6(f"dtb{l}")), ALU.add, rd=[r_gb, r_c], wr=[r_gb])
            ACT(gtmp[0:64], gtmp[0:64], AF.Exp, rd=[r_gb], wr=[r_gb])
            ACT(gtmp[0:64].rearrange("p c n -> p (c n)"), gtmp[0:64].rearrange("p c n -> p (c n)"), AF.Ln, rd=[r_gb, r_eps], wr=[r_gb], bias=epsb[0:64, 3:4])
            ACT(nexpA[0:64, :], c6(f"alog{l}"), AF.Exp, rd=[r_c], wr=[r_gb])
            TS(nexpA[0:64, :], nexpA[0:64, :], -1.0, None, ALU.mult, rd=[r_gb], wr=[r_gb])
            TT(gall[0:64], gtmp[0:64], bc8(nexpA[0:64, :]), ALU.mult, rd=[r_gb], wr=[r_gb])
            ACT(beta[0:64], gab[0:64, :, 8:16], AF.Exp, rd=[r_gb], wr=[r_gb], scale=-1.0)
            TS(beta[0:64], beta[0:64], 1.0, None, ALU.add, rd=[r_gb], wr=[r_gb])
            S.op("dve", lambda e: e.reciprocal(out=beta[0:64], in_=beta[0:64]), [r_gb], [r_gb])
            TS(nbeta[0:64], beta[0:64], -1.0, None, ALU.mult, rd=[r_gb], wr=[r_gb])
            for h in range(4):
                wt, wr_ = next_w(("dz", l, h))
                for half in range(2):
                    hs = slice(half * 512, (half + 1) * 512)
                    ps, pr = nextps()
                    proj_fm(wt, wr_, half, ps, pr)
                    ACT(mixT[:, h, hs], ps[:], AF.Silu, rd=[pr], wr=[mix_r[h][half]])
            if l == 0:
                dump("dq", qkvT[:, 0, :], [qkv_r[0]], BF16)
                dump("dk", qkvT[:, 4, :], [qkv_r[4]], BF16)
                dump("dv", vT[:, 0, :], [qkv_r[8]], BF16)
                dump("dg", gall[0:64].rearrange("p c n -> p (c n)"), [r_gb])
                dump("dbeta", beta[0:64].rearrange("p c n -> p (c n)"), [r_gb])
            S.barrier()
            AH = Arena(0)
            AR = Arena(mark)
            NSL = 3
            NLN = 2

            def alloc2(n, dt):
                nw = (n * (4 if dt == F32 else 2) + 3) // 4
                if AH.p + nw <= HTW:
                    return AH.alloc(n, dt)
                return AR.alloc(n, dt)
            WK = {}
            for z in range(2):
                for ln in range(NLN):
                    for name, n, dt in (("gTri", 256, F32), ("gbc", 256, F32), ("eGbc", 256, F32), ("AA", 512, BF16),
                                        ("Xs", 512, BF16)):
                        WK[(name, z, ln)] = (alloc2(n, dt), Res())
                    for al, sname in (("d1", "gTri"), ("a1", "gTri"), ("d2", "gbc"), ("tmpm", "Xs")):
                        WK[(al, z, ln)] = WK[(sname, z, ln)]
            HO = {}
            for z in range(2):
                for sl in range(NSL):
                    for name, n, dt in (("TTT", 512, BF16), ("attT", 256, BF16), ("qte", 256, BF16), ("kd", 512, BF16),
                                        ("bv", 512, BF16), ("sm", 12, F32), ("eGl", 4, F32)):
                        HO[(name, z, sl)] = (alloc2(n, dt), Res())
                    S.op("dve", (lambda t_: (lambda e: e.memset(t_, 0.0)))(HO[("attT", z, sl)][0][64:128, :]), [],
                         [HO[("attT", z, sl)][1]])
            SW = {}
            for z in range(2):
                for name, n, dt in (("tmpr", 512, F32), ("r", 512, BF16), ("vn", 512, BF16)):
                    SW[(name, z)] = (alloc2(n, dt), Res())
                S.op("dve", (lambda t_: (lambda e: e.memset(t_, 0.0)))(SW[("vn", z)][0][64:128, :]), [], [SW[("vn", z)][1]])
            _s1 = alloc2(512, F32)
            _r1 = Res()
            print("DN ring end", AH.p, HTW, AR.p, ARW)
            owritten = [False] * 16
            h4 = lambda ap: ap.rearrange("p (h i) -> p h i", h=4)
            hd = lambda ap: ap.rearrange("p (h d) -> p h d", h=4)
            v4 = lambda ap: ap.rearrange("p (v h j) -> p v h j", v=2, h=4)
            id64 = ident_bf[0:64, 0:64]

            ps_busy = [False] * 7

            def getps():
                while True:
                    for k_ in range(7):
                        i_ = (ps_i[0] + k_) % 7
                        if not ps_busy[i_]:
                            ps_busy[i_] = True
                            ps_i[0] = i_ + 1
                            return ps_t[i_], ps_r[i_], i_
                    yield

            def relps(i_):
                ps_busy[i_] = False

            def pre(z, c, ln, sl):
                cs_ = slice(c * 64, (c + 1) * 64)
                g = gall[0:64, c, z * 4:(z + 1) * 4]
                bz = beta[0:64, c, z * 4:(z + 1) * 4]
                nbz = nbeta[0:64, c, z * 4:(z + 1) * 4]
                Tri = c6("TriL") if z == 0 else c6("TriU")
                nTri = c6("nTriL") if z == 0 else c6("nTriU")
                TriC = c6("TriCL") if z == 0 else c6("TriCU")
                w = lambda n: WK[(n, z, ln)][0]
                wq = lambda n: WK[(n, z, ln)][1]
                o = lambda n: HO[(n, z, sl)][0]
                oq = lambda n: HO[(n, z, sl)][1]
                mz = lambda m: mzb[:, z, m].unsqueeze(2).to_broadcast([64, 2, 4, 64])
                gb64 = g.unsqueeze(2).to_broadcast([64, 4, 64])
                TT(h4(w("gTri")[0:64, :]), Tri.unsqueeze(1).to_broadcast([64, 4, 64]), gb64, ALU.mult,
                   rd=[r_c, r_gb], wr=[wq("gTri")])
                CP(h4(w("gbc")[0:64, :]), gb64, rd=[r_gb], wr=[wq("gbc")])
                yield
                psD, prD, bD = yield from getps()
                MM(psD[0:64, 0:256], Tri, w("gbc")[0:64, :], start=True, stop=False, rd=[r_c, wq("gbc")], wr=[prD])
                MM(psD[0:64, 0:256], c6("negones"), w("gTri")[0:64, :], start=False, stop=True, rd=[r_c, wq("gTri")], wr=[prD])
                MM(psD[0:64, 256:512], c6("ones", 0, 64), w("gTri")[0:64, :], start=True, stop=False, rd=[r_c, wq("gTri")], wr=[prD])
                MM(psD[0:64, 256:512], nTri, w("gbc")[0:64, :], start=False, stop=True, rd=[r_c, wq("gbc")], wr=[prD])
                psX, prX, bX = yield from getps()
                MM(psX[:, 0:256], c6("ones"), w("gTri")[0:64, :], rd=[r_c, wq("gTri")], wr=[prX])
                MM(psX[0:64, 256:260], Tri, g, rd=[r_c, r_gb], wr=[prX])
                MM(psX[:, 260:264], c6("ones"), g, rd=[r_c, r_gb], wr=[prX])
                MM(psX[0:64, 264:268], TriC, g, rd=[r_c, r_gb], wr=[prX])
                yield
                TT(h4(w("d1")[0:64, :]), h4(psD[0:64, 0:256]), c6(f"mbS{z}").unsqueeze(1).to_broadcast([64, 4, 64]),
                   ALU.add, rd=[prD, r_c], wr=[wq("d1")])
                TT(h4(w("d2")[0:64, :]), h4(psD[0:64, 256:512]), c6(f"mbIT{z}").unsqueeze(1).to_broadcast([64, 4, 64]),
                   ALU.add, rd=[prD, r_c], wr=[wq("d2")])
                ACT(w("eGbc"), psX[:, 0:256], AF.Exp, rd=[prX], wr=[wq("eGbc")])
                ACT(o("sm")[0:64, 0:4], psX[0:64, 256:260], AF.Exp, rd=[prX], wr=[oq("sm")])
                ACT(o("eGl"), psX[:, 260:264], AF.Exp, rd=[prX], wr=[oq("eGl")])
                ACT(o("sm")[0:64, 4:8], psX[0:64, 264:268], AF.Exp, rd=[prX], wr=[oq("sm")])
                relps(bD)
                relps(bX)
                yield
                ACT(w("d1")[0:64, :], w("d1")[0:64, :], AF.Exp, rd=[wq("d1")], wr=[wq("d1")])
                ACT(w("d2")[0:64, :], w("d2")[0:64, :], AF.Exp, rd=[wq("d2")], wr=[wq("d2")])
                TT(o("sm")[0:64, 8:12], o("sm")[0:64, 0:4], nbz, ALU.mult, rd=[oq("sm"), r_gb], wr=[oq("sm")])
                psK, prK, bK = yield from getps()
                for h in range(4):
                    MM(psK[0:64, h * 64:(h + 1) * 64], qkvT[:, 4 + h, cs_], qkvT[:, 4 + h, cs_], rd=[qkv_r[4 + h]], wr=[prK])
                for h in range(4):
                    MM(psK[0:64, 256 + h * 64:256 + (h + 1) * 64], qkvT[:, 4 + h, cs_], qkvT[:, h, cs_],
                       rd=[qkv_r[4 + h], qkv_r[h]], wr=[prK])
                yield
                TT(w("a1")[0:64, :], psK[0:64, 0:256], w("d1")[0:64, :], ALU.mult, rd=[prK, wq("d1")], wr=[wq("a1")])
                TT(h4(w("AA")[0:64, 0:256]), h4(w("a1")[0:64, :]), bz.unsqueeze(2).to_broadcast([64, 4, 64]), ALU.mult,
                   rd=[wq("a1"), r_gb], wr=[wq("AA")])
                TT(o("attT")[0:64, :], psK[0:64, 256:512], w("d2")[0:64, :], ALU.mult, rd=[prK, wq("d2")], wr=[oq("attT")])
                relps(bK)
                yield
                pst, prt, bt = yield from getps()
                pst_v = pst[:].bitcast(BF16)
                for h in range(4):
                    TR(pst_v[0:64, h * 64:(h + 1) * 64], w("AA")[0:64, h * 64:(h + 1) * 64], id64,
                       rd=[wq("AA"), r_cb], wr=[prt])
                psk, prk, bk = yield from getps()
                pskb = psk[:].bitcast(BF16)
                for h in range(4):
                    TR(pskb[0:64, h * 128:(h + 1) * 128], qkvT[:, 4 + h, cs_], ident_bf, rd=[qkv_r[4 + h], r_cb], wr=[prk])
                prv = prk
                for h in range(4):
                    TR(pskb[0:64, 512 + h * 128:512 + (h + 1) * 128], vT[:, h, cs_], ident_bf, rd=[qkv_r[8 + h], r_cb], wr=[prv])
                yield
                CP(w("AA")[0:64, 256:512], pst_v[0:64, 0:256], rd=[prt], wr=[wq("AA")], eng="act")
                TT(hd(o("kd")[0:64, :]), hd(pskb[0:64, 0:512]), o("sm")[0:64, 4:8].unsqueeze(2).to_broadcast([64, 4, 128]),
                   ALU.mult, rd=[prk, oq("sm")], wr=[oq("kd")])
                TT(hd(o("bv")[0:64, :]), hd(pskb[0:64, 512:1024]), bz.unsqueeze(2).to_broadcast([64, 4, 128]), ALU.mult,
                   rd=[prv, r_gb], wr=[oq("bv")])
                TT(h4(o("qte")), qkvT[:, 0:4, cs_], h4(w("eGbc")), ALU.mult, rd=qkv_r[0:4] + [wq("eGbc")], wr=[oq("qte")])
                relps(bt)
                relps(bk)
                yield
                TT(v4(w("tmpm")[0:64, :]), v4(w("AA")[0:64, :]), mz(0), ALU.mult, rd=[wq("AA"), r_cb], wr=[wq("tmpm")])
                TT(v4(o("TTT")[0:64, :]), c6("ident").unsqueeze(1).unsqueeze(1).to_broadcast([64, 2, 4, 64]),
                   v4(w("tmpm")[0:64, :]), ALU.subtract, rd=[wq("tmpm"), r_c], wr=[oq("TTT")])
                for m in range(1, 6):
                    yield
                    psXX, prXX, bXX = yield from getps()
                    for h in range(4):
                        hc = slice(h * 64, (h + 1) * 64)
                        hc2 = slice(256 + h * 64, 256 + (h + 1) * 64)
                        MM(psXX[0:64, hc], w("AA")[0:64, hc2], o("TTT")[0:64, hc], rd=[wq("AA"), oq("TTT")], wr=[prXX])
                        MM(psXX[0:64, hc2], w("AA")[0:64, hc], o("TTT")[0:64, hc2], rd=[wq("AA"), oq("TTT")], wr=[prXX])
                    yield
                    CP(w("Xs")[0:64, :], psXX[0:64, :], rd=[prXX], wr=[wq("Xs")], eng="act")
                    relps(bXX)
                    yield
                    psY, prY, bY = yield from getps()
                    for h in range(4):
                        hc = slice(h * 64, (h + 1) * 64)
                        hc2 = slice(256 + h * 64, 256 + (h + 1) * 64)
                        MM(psY[0:64, hc], o("TTT")[0:64, hc2], w("Xs")[0:64, hc], rd=[wq("Xs"), oq("TTT")], wr=[prY])
                        MM(psY[0:64, hc2], o("TTT")[0:64, hc], w("Xs")[0:64, hc2], rd=[wq("Xs"), oq("TTT")], wr=[prY])
                    yield
                    TT(v4(w("tmpm")[0:64, :]), v4(psY[0:64, :]), mz(m), ALU.mult, rd=[prY, r_cb], wr=[wq("tmpm")])
                    relps(bY)
                    yield
                    TT(o("TTT")[0:64, :], o("TTT")[0:64, :], w("tmpm")[0:64, :], ALU.subtract,
                       rd=[oq("TTT"), wq("tmpm")], wr=[oq("TTT")])

            sso = [0]

            def ser(z, c, sl):
                cs_ = slice(c * 64, (c + 1) * 64)
                o = lambda n: HO[(n, z, sl)][0]
                oq = lambda n: HO[(n, z, sl)][1]
                s_ = lambda n: SW[(n, z)][0]
                sq = lambda n: SW[(n, z)][1]
                if (z == 0 and c % 4 == 0 and c > 0) or (z == 1 and c % 4 == 3 and c < 15):
                    TS(Sd[:, z, :], Sd[:, z, :], cfs("carry"), None, ALU.mult, rd=[Sd_r[z], r_c], wr=[Sd_r[z]])
                    CP(Sdb[:, z, :], Sd[:, z, :], rd=[Sd_r[z]], wr=[Sdb_r[z]], eng="act")
                    yield
                psKS, prKS, bKS = yield from getps()
                for h in range(4):
                    MM(psKS[0:64, h * 128:(h + 1) * 128], qkvT[:, 4 + h, cs_], Sdb[:, z, h * 128:(h + 1) * 128],
                       rd=[qkv_r[4 + h], Sdb_r[z]], wr=[prKS])
                yield
                TT(hd(s_("tmpr")[0:64, :]), hd(psKS[0:64, :]), o("sm")[0:64, 8:12].unsqueeze(2).to_broadcast([64, 4, 128]),
                   ALU.mult, rd=[prKS, oq("sm")], wr=[sq("tmpr")])
                relps(bKS)
                yield
                TT(s_("r")[0:64, :], s_("tmpr")[0:64, :], o("bv")[0:64, :], ALU.add, rd=[sq("tmpr"), oq("bv")], wr=[sq("r")])
                yield
                psV, prV, bV = yield from getps()
                for h in range(4):
                    MM(psV[0:64, h * 128:(h + 1) * 128], o("TTT")[0:64, 256 + h * 64:256 + (h + 1) * 64],
                       s_("r")[0:64, h * 128:(h + 1) * 128], rd=[oq("TTT"), sq("r")], wr=[prV])
                yield
                CP(s_("vn")[0:64, :], psV[0:64, :], rd=[prV], wr=[sq("vn")], eng="act")
                relps(bV)
                yield
                psO, prO, bO = yield from getps()
                for h in range(4):
                    MM(psO[:, h * 64:(h + 1) * 64], Sdb[:, z, h * 128:(h + 1) * 128], o("qte")[:, h * 64:(h + 1) * 64],
                       start=True, stop=False, rd=[Sdb_r[z], oq("qte")], wr=[prO])
                    MM(psO[:, h * 64:(h + 1) * 64], s_("vn")[:, h * 128:(h + 1) * 128], o("attT")[:, h * 64:(h + 1) * 64],
                       start=False, stop=True, rd=[sq("vn"), oq("attT")], wr=[prO])
                psS, prS, bS = yield from getps()
                for h in range(4):
                    MM(psS[:, h * 128:(h + 1) * 128], o("kd")[0:64, h * 128:(h + 1) * 128], s_("vn")[0:64, h * 128:(h + 1) * 128],
                       rd=[oq("kd"), sq("vn")], wr=[prS])
                yield
                TT(hd(Sd[:, z, :]), hd(Sd[:, z, :]), o("eGl").unsqueeze(2).to_broadcast([128, 4, 128]), ALU.mult,
                   rd=[Sd_r[z], oq("eGl")], wr=[Sd_r[z]])
                TT(Sd[:, z, :], psS[:], Sd[:, z, :], ALU.add, rd=[prS, Sd_r[z]], wr=[Sd_r[z]])
                relps(bS)
                yield
                CP(Sdb[:, z, :], Sd[:, z, :], rd=[Sd_r[z]], wr=[Sdb_r[z]], eng="act")
                ov = oacc[:, :, cs_]
                pv = h4(psO[:, 0:256])
                if not owritten[c]:
                    CP(ov, pv, rd=[prO], wr=[oacc_r[c]])
                    owritten[c] = True
                else:
                    TT(ov, pv, ov, ALU.add, rd=[prO, oacc_r[c]], wr=[oacc_r[c]])
                relps(bO)
                if (z == 0 and c % 4 == 3) or (z == 1 and c % 4 == 0):
                    CP(_s1, Sd[:, z, :], rd=[Sd_r[z]], wr=[_r1])
                    k = sso[0] % 2
                    sso[0] += 1
                    DMA("sp", sd_out[l, z, c // 4], _s1, rd=[_r1], sem=f"sso{k}")

            order = [list(range(16)), list(range(15, -1, -1))]
            pre_i = [0, 0]
            pre_done = [0, 0]
            ser_i = [0, 0]
            ser_done = [0, 0]
            active = []
            modg = [mod_gen(48, lag=4), mod_gen(48, lag=4)] if l == 0 else []
            while ser_done[0] < 16 or ser_done[1] < 16:
                for z in range(2):
                    n_pre = sum(1 for a_ in active if a_[1] == "pre" and a_[2] == z)
                    while (n_pre < NLN and pre_i[z] < 16 and pre_i[z] - ser_done[z] < NSL):
                        i_ = pre_i[z]
                        lanes_busy = [a_[4] for a_ in active if a_[1] == "pre" and a_[2] == z]
                        ln = 0 if 0 not in lanes_busy else 1
                        active.append([pre(z, order[z][i_], ln, i_ % NSL), "pre", z, i_, ln])
                        pre_i[z] += 1
                        n_pre += 1
                    if not any(a_[1] == "ser" and a_[2] == z for a_ in active) and ser_i[z] < 16 and pre_done[z] > ser_i[z]:
                        i_ = ser_i[z]
                        active.append([ser(z, order[z][i_], i_ % NSL), "ser", z, i_, -1])
                        ser_i[z] += 1
                for a_ in list(active):
                    try:
                        next(a_[0])
                    except StopIteration:
                        active.remove(a_)
                        if a_[1] == "pre":
                            pre_done[a_[2]] += 1
                        else:
                            ser_done[a_[2]] += 1
                for g_ in list(modg):
                    try:
                        next(g_)
                    except StopIteration:
                        modg.remove(g_)
            for g_ in modg:
                for _ in g_:
                    pass
            S.barrier()
            A2 = Arena(mark)
            mix_epilogue(A2, oacc, oacc_r, mixT, mix_r, gsm[:, 6 + l:7 + l])
            if l == 0:
                dump("mdn", mixT[:, 0, :], [mix_r[0][0], mix_r[0][1]], BF16)
            wout_apply(l, "wo_dn", 4, mixT, mix_r)

        for l in range(NL):
            S.barrier()
            A = Arena(HTW)
            rmsnorm_to_hT(l, 0, A)
            if l == 0:
                dump("h1", hT[:, 0, :], hT_r, BF16)
            S.barrier()
            if KSTOP == 'n1':
                break
            A = Arena(HTW)
            mixT = A.alloc(8 * T, BF16, "p (k t) -> p k t", k=8)
            mix_r = [[Res(), Res()] for _ in range(8)]
            qT = A.alloc(8 * T, BF16, "p (h t) -> p h t", h=8)
            qT_r = [[Res(), Res()] for _ in range(8)]
            kTa = A.alloc(2 * 1280, BF16, "p (h t) -> p h t", h=2)
            kTa_r = [Res(), Res()]
            vall = A.alloc(10 * 256, BF16, "p (t c) -> p t c", t=10)
            vall_r = Res()
            cs = A.alloc(2 * T, F32, "p (a t) -> p a t", a=2)
            r_cs = Res()
            atab = A.alloc(1280, BF16)
            btab = A.alloc(T, BF16)
            r_ab = Res()
            r_ab_b = Res()
            sq_ = A.alloc(512, BF16)
            rstd_ = A.alloc(512, F32)
            nslots = (sq_, Res(), rstd_, Res())
            qn = [A.alloc(512, F32) for _ in range(2)]
            qn_r = [Res(), Res()]
            qnb = [A.alloc(512, BF16) for _ in range(2)]
            qnb_r = [Res(), Res()]
            t1 = [A.alloc(512, F32) for _ in range(2)]
            t1_r = [Res(), Res()]
            t2 = [A.alloc(512, F32) for _ in range(2)]
            t2_r = [Res(), Res()]
            vst = [A.alloc(256, F32) for _ in range(2)]
            vst_r = [Res(), Res()]
            PT = [A.alloc(512, BF16) for _ in range(3)]
            PT_r = [Res() for _ in range(3)]
            rec = [A.alloc(512, F32) for _ in range(2)]
            rec_r = [Res(), Res()]
            if 'c' not in KSK:
                DMA("sp", cs, cs_in, wr=[r_cs], sem="cs")
            if 'a' not in KSK:
                DMA("pool", atab, atab_in, wr=[r_ab], sem="ab0")
                DMA("pool", btab, btab_in, wr=[r_ab_b], sem="ab1")
                DMA("pool", kTa[:, :, 1024:1280], ckT_in[l], wr=kTa_r, sem="ab2")
                DMA("pool", vall[:, 8:10, :], cv_in[l], wr=[vall_r], sem="ab3")
            sq2_ = A.alloc(512, BF16)
            rstd2_ = A.alloc(512, F32)
            nsl2 = [nslots, (sq2_, Res(), rstd2_, Res())]

            qk_cnt = [0]

            def qk_proj(kind, h, wt, wr_, par):
                for half in range(2):
                    b_ = par * 2 + half
                    proj_fm(wt, wr_, half, ps_t[b_], ps_r[b_])

            def tmp_ps():
                b_ = 4 + qk_cnt[0] % 3
                qk_cnt[0] += 1
                return ps_t[b_], ps_r[b_]

            def qk_iter(kind, h, half, par):
                hs = slice(half * 512, (half + 1) * 512)
                k = half
                ps, pr = ps_t[par * 2 + half], ps_r[par * 2 + half]
                sq, sq_r, rstd, rstd_r = nsl2[k]
                ACT(sq, ps[:], AF.Square, rd=[pr], wr=[sq_r])
                yield
                ps2, pr2 = tmp_ps()
                MM(ps2[:], ones_bf, sq, rd=[sq_r, r_cb], wr=[pr2])
                yield
                ACT(rstd, ps2[:], AF.Ln, rd=[pr2, r_eps], wr=[rstd_r], bias=epsb[:, 1:2])
                yield
                ACT(rstd, rstd, AF.Exp, rd=[rstd_r], wr=[rstd_r], scale=-0.5)
                yield
                gcol = gsm[:, l:l + 1] if kind == "aq" else gsm[:, 2 + l:3 + l]
                STT(qn[k], ps[:], gcol, rstd, ALU.mult, ALU.mult, rd=[pr, rstd_r, r_g], wr=[qn_r[k]])
                yield
                if kind == "ak" and 'k' not in KSK:
                    DMA("sp", kT_out[l, h, half], qn[k], rd=[qn_r[k]], sem=f"ko{k}")
                CP(qnb[k], qn[k], rd=[qn_r[k]], wr=[qnb_r[k]], eng="act")
                TT(t1[k], qn[k], cs[:, 0, hs], ALU.mult, rd=[qn_r[k], r_cs], wr=[t1_r[k]])
                yield
                ps3, pr3 = tmp_ps()
                MM(ps3[:], RmT_bf, qnb[k], rd=[qnb_r[k], r_cb], wr=[pr3])
                yield
                TT(t2[k], ps3[:], cs[:, 1, hs], ALU.mult, rd=[pr3, r_cs], wr=[t2_r[k]])
                yield
                if kind == "aq":
                    TT(qT[:, h, hs], t1[k], t2[k], ALU.add, rd=[t1_r[k], t2_r[k]], wr=[qT_r[h][half]])
                else:
                    TT(kTa[:, h, hs], t1[k], t2[k], ALU.add, rd=[t1_r[k], t2_r[k]], wr=[kTa_r[h]])

            heads = [("aq", h) for h in range(8)] + [("ak", h) for h in range(2)]
            wt0 = next_w((heads[0][0], l, heads[0][1]))
            qk_proj(heads[0][0], heads[0][1], wt0[0], wt0[1], 0)
            for hi, (kind, h) in enumerate(heads):
                if hi + 1 < len(heads):
                    kn, hn = heads[hi + 1]
                    wtn = next_w((kn, l, hn))
                    qk_proj(kn, hn, wtn[0], wtn[1], (hi + 1) % 2)
                run_il([qk_iter(kind, h, 0, hi % 2), qk_iter(kind, h, 1, hi % 2)]
                       + ([mod_gen(2, lag=3)] if (l == 0 and mod_next[0] < 48) else []))
            for h in range(2):
                wt, wr_ = next_w(("av", l, h))
                for tt in range(0 if 'v' in KSK else 8):
                    ps, pr = nextps()
                    for kc in range(KC):
                        MM(ps[:, 0:128], hT[:, kc, tt * 128:(tt + 1) * 128], wt[:, kc, :], start=(kc == 0),
                           stop=(kc == KC - 1), rd=[wr_, hT_r[tt // 4]], wr=[pr])
                    k = tt % 2
                    CP(vst[k][:, 0:128], ps[:, 0:128], rd=[pr], wr=[vst_r[k]], eng="act")
                    CP(vall[:, tt, h * 128:(h + 1) * 128], vst[k][:, 0:128], rd=[vst_r[k]], wr=[vall_r])
                    DMA("sp", v_out[l, h, tt], vst[k][:, 0:128], rd=[vst_r[k]], sem=f"vo{k}")
            if l == 0:
                dump("qT0", qT[:, 0, :], [qT_r[0][0], qT_r[0][1]], BF16)
                dump("kT0", kTa[:, 0, :], kTa_r, BF16)
            if KSTOP == 'ap':
                break
            items = [(hq, half, kt) for hq in range(8) for half in range(2) for kt in range(10)]
            sbank = [0, 1, 2]
            obank = [(3, 4), (5, 6)]

            def s_stage(i):
                hq, half, kt = items[i]
                kv = hq // 4
                hs = slice(half * 512, (half + 1) * 512)
                psS, prS = ps_t[sbank[i % 3]], ps_r[sbank[i % 3]]
                MM(psS[:], kTa[:, kv, kt * 128:(kt + 1) * 128], qT[:, hq, hs], start=True, stop=False,
                   rd=[kTa_r[kv], qT_r[hq][half]], wr=[prS])
                MM(psS[:], atab[:, kt * 128:(kt + 1) * 128], btab[:, hs], start=False, stop=True,
                   rd=[r_ab, r_ab_b], wr=[prS])
                ACT(PT[i % 3], psS[:], AF.Exp, rd=[prS], wr=[PT_r[i % 3]])

            def pv_stage(i):
                hq, half, kt = items[i]
                kv = hq // 4
                hs = slice(half * 512, (half + 1) * 512)
                g_ = (i // 10) % 2
                psO, prO = ps_t[obank[g_][0]], ps_r[obank[g_][0]]
                psD, prD = ps_t[obank[g_][1]], ps_r[obank[g_][1]]
                p = i % 3
                MM(psO[:], vall[:, kt, kv * 128:(kv + 1) * 128], PT[p], start=(kt == 0), stop=(kt == 9),
                   rd=[vall_r, PT_r[p]], wr=[prO])
                MM(psD[:], ones_bf, PT[p], start=(kt == 0), stop=(kt == 9), rd=[PT_r[p], r_cb], wr=[prD])
                if kt == 9:
                    S.op("dve", (lambda o_, i_: (lambda e: e.reciprocal(out=o_, in_=i_)))(rec[g_], psD[:]),
                         [prD], [rec_r[g_]])
                    TT(mixT[:, hq, hs], psO[:], rec[g_], ALU.mult, rd=[prO, rec_r[g_]], wr=[mix_r[hq][half]])

            for i in range(len(items) + 1):
                if i < len(items):
                    s_stage(i)
                if i >= 1:
                    pv_stage(i - 1)
                pass
            if l == 0:
                dump("matt", mixT[:, 0, :], [mix_r[0][0], mix_r[0][1]], BF16)
            if l == 0:
                pump_mod(0, 48 - mod_next[0])
            wout_apply(l, "wo_att", 8, mixT, mix_r)
            S.barrier()
            if KSTOP == "att":
                break
            gla_phase(l)
            S.barrier()
            if KSTOP in ("gla", "gp"):
                break
            dn_phase(l)
            S.barrier()
            if KSTOP == "dn":
                break
            A = Arena(HTW)
            if l == 0:
                pump_mod(0, 96)
            rmsnorm_to_hT(l, 1, A)
            S.barrier()
            A = Arena(HTW)
            actT = A.alloc(16 * T, BF16, "p (k t) -> p k t", k=16)
            act_r = [[Res(), Res()] for _ in range(16)]
            rl = [A.alloc(512, F32) for _ in range(2)]
            rl_r = [Res(), Res()]
            ri = 0
            for g in range(4):
                for j in range(16):
                    if l == 0:
                        pump_mod(1, 1)
                    wt, wr_ = next_w(("ff1", l, g, j))
                    for half in range(2):
                        hs = slice(half * 512, (half + 1) * 512)
                        ps, pr = nextps()
                        proj_fm(wt, wr_, half, ps, pr)
                        k = ri % 2
                        ri += 1
                        ACT(rl[k], ps[:], AF.Relu, rd=[pr], wr=[rl_r[k]])
                        TT(actT[:, j, hs], rl[k], rl[k], ALU.mult, rd=[rl_r[k]], wr=[act_r[j][half]])
                for dc in range(16):
                    if l == 0:
                        pump_mod(1, 1)
                    wt, wr_ = next_w(("ff2", l, g, dc))
                    for half in range(2):
                        hs = slice(half * 512, (half + 1) * 512)
                        ps, pr = nextps()
                        for j in range(16):
                            MM(ps[:], wt[:, j, :], actT[:, j, hs], start=(j == 0), stop=(j == 15),
                               rd=[wr_, act_r[j][half]], wr=[pr])
                        STT(xT[:, dc, hs], ps[:], mods[:, l, 5 * 16 + dc: 5 * 16 + dc + 1], xT[:, dc, hs],
                            ALU.mult, ALU.add, rd=[pr, mod_r[l][5], xT_r[dc][half]], wr=[xT_r[dc][half]])

        S.barrier()
        A = Arena(0)
        yst = [A.alloc(D, F32) for _ in range(2)]
        yst_r = [Res(), Res()]
        for tt in range(8):
            sl = tt % 2
            half = tt // 4
            for c4 in range(4):
                ps, pr = nextps()
                for j in range(4):
                    c = c4 * 4 + j
                    TR(ps[:, j * 128:(j + 1) * 128], xT[:, c, tt * 128:(tt + 1) * 128], ident_f,
                       rd=[xT_r[c][half], r_c], wr=[pr])
                eng = "act" if c4 % 2 else "dve"
                CP(yst[sl][:, c4 * 512:(c4 + 1) * 512], ps[:], rd=[pr], wr=[yst_r[sl]], eng=eng)
            DMA("sp", y_out[tt * 128:(tt + 1) * 128, :], yst[sl], rd=[yst_r[sl]], sem=f"yo{sl}")
        S.wait_all_dma("sp")
        stats = S.emit(nc, st)
        print("ops", stats, "weights", wi_[0], "/", len(plan))
    assert len(plan_rec) == len(plan) and offs_rec == woffs, (len(plan_rec), len(plan))
    return nc, list(dbg_outs), plan_rec


_CACHE = {}
LAST_DBG = {}


def _host_consts():
    t = np.arange(64)
    f32 = np.float32
    TriL = (t[:, None] <= t[None, :]).astype(f32)
    TriU = (t[:, None] >= t[None, :]).astype(f32)
    TriCL = (t[:, None] > t[None, :]).astype(f32)
    TriCU = (t[:, None] < t[None, :]).astype(f32)
    NEG = f32(-1e4)
    i_, j_ = t[:, None], t[None, :]
    d = dict(TriL=TriL, TriU=TriU, ones=np.ones((64, 128), f32), TriCL=TriCL, TriCU=TriCU,
             TriS0=-TriL / 16.0, TriS1=-TriU / 16.0,
             mbS0=np.where(j_ < i_, 0, NEG).astype(f32), mbS1=np.where(j_ > i_, 0, NEG).astype(f32),
             mbIT0=np.where(i_ <= j_, 0, NEG).astype(f32), mbIT1=np.where(i_ >= j_, 0, NEG).astype(f32),
             mT0=(i_ <= j_).astype(f32), mT1=(i_ >= j_).astype(f32),
             negones=-np.ones((64, 64), f32), nTriL=-TriL, nTriU=-TriU, ident=np.eye(64, dtype=f32))
    return d


def _mz_const():
    ii = np.arange(64)
    ML = np.zeros((64, 6, 64), np.float32)
    for m in range(6):
        b = 1 << m
        same = (ii[:, None] // (2 * b)) == (ii[None, :] // (2 * b))
        ML[:, m, :] = (same & ((ii[:, None] % (2 * b)) >= b) & ((ii[None, :] % (2 * b)) < b)).astype(np.float32)
    MU = ML.transpose(2, 1, 0)
    mz = np.zeros((64, 2, 6, 2, 64), np.float32)
    mz[:, 0, :, 0], mz[:, 0, :, 1] = ML, MU
    mz[:, 1, :, 0], mz[:, 1, :, 1] = MU, ML
    return np.ascontiguousarray(mz.reshape(64, 2 * 6 * 128))


def kernel(**inp):
    f32 = np.float32
    inp = {k: np.asarray(v) for k, v in inp.items()}
    if "nc" not in _CACHE:
        _CACHE["nc"] = build_program()
    nc, dbg_names, plan = _CACHE["nc"]
    _p0, woffs = weight_plan()
    warr = {a: np.zeros(n, f32) for a, n in woffs.items()}
    for (a, key, src, l, row0, nk, col0, ncols, off) in plan:
        W = inp[src][l]
        blk = W[row0:row0 + nk * 128, col0:col0 + ncols].reshape(nk, 128, ncols).transpose(1, 0, 2)
        dst = warr[a][off: off + 128 * nk * 128].reshape(128, nk, 128)
        dst[:, :, :ncols] = blk
    hc = _host_consts()
    tpos = np.arange(T)
    inv = (np.float32(10000.0) ** (-np.arange(0, 64, 2, dtype=f32) / np.float32(64))).astype(f32)
    ang = np.zeros((128, T), f32)
    for dd in range(128):
        pos = (tpos // 64) if dd < 64 else (tpos % 64)
        ang[dd] = pos.astype(f32) * inv[dd % 32]
    cos_s, sin_s = np.cos(ang).astype(f32), np.sin(ang).astype(f32)
    RmT = np.zeros((128, 128), f32)
    for dd in range(128):
        if dd % 64 < 32:
            RmT[dd + 32, dd] = -1.0
        else:
            RmT[dd - 32, dd] = 1.0
    w2aug = np.zeros((2, 33, 512), f32)
    for l in range(2):
        for z in range(2):
            w2aug[l, z * 16:(z + 1) * 16, z * 256:(z + 1) * 256] = inp["gla_w2"][l, z]
            w2aug[l, 32, z * 256:(z + 1) * 256] = inp["gla_b"][l, z]
    in_maps = []
    for core in range(8):
        ctx = core < 4
        m = dict(warr)
        if ctx:
            m["x"] = np.ascontiguousarray(inp["x_prompt"][4 * core:4 * core + 4].reshape(T, D))
            cond = inp["c_ctx"]
        else:
            b = core - 4
            m["x"] = np.ascontiguousarray(inp["x_sample"][b])
            cond = inp["c"][b]
        cfa = np.zeros((128, NCF), f32)

        def put(name, arr):
            o, w = CF[name]
            cfa[:, o:o + w] = arr
        put("ident", np.eye(128, dtype=f32))
        put("cond", cond.reshape(16, 128).T)
        for l in range(2):
            put(f"bmod{l}", inp["b_mod"][l].reshape(96, 128).T)
            put(f"n1g{l}", inp["norm1_g"][l].reshape(16, 128).T)
            put(f"n2g{l}", inp["norm2_g"][l].reshape(16, 128).T)
            put(f"glag{l}", inp["gla_norm_g"][l][:, None])
            put(f"qg{l}", inp["q_norm_g"][l][:, None])
            put(f"kg{l}", inp["k_norm_g"][l][:, None])
            put(f"dng{l}", inp["dn_norm_g"][l][:, None])
            put(f"conv{l}", inp["dn_conv"][l].reshape(3, 12, 128).transpose(2, 1, 0).reshape(128, 36))
        put("carry", 0.0 if ctx else 1.0)
        put("cflag", 1.0 if ctx else 0.0)
        put("onesf", 1.0)
        put("identb", np.eye(128, dtype=f32))
        put("Rm", RmT)
        m["cf"] = cfa
        c64a = np.zeros((64, NC64), f32)
        for name, arr in hc.items():
            o, w = C64[name]
            c64a[:, o:o + w] = arr
        for l in range(2):
            o, w = C64[f"alog{l}"]
            c64a[:, o:o + w] = inp["dn_a_log"][l].reshape(8)[None, :]
            o, w = C64[f"dtb{l}"]
            c64a[:, o:o + w] = inp["dn_dt_bias"][l].reshape(8)[None, :]
        m["c64"] = c64a
        m["mz"] = _mz_const()
        m["w2aug"] = w2aug
        cs = np.zeros((128, 2, T), f32)
        at = np.zeros((128, 1280), f32)
        bt = np.zeros((128, T), f32)
        at[4, 1024:] = 1.0
        if ctx:
            cs[:, 0, :] = 1.0
            for s in range(4):
                at[s, s * 256:(s + 1) * 256] = 1.0
                bt[s, :] = -30000.0
                bt[s, s * 256:(s + 1) * 256] = 0.0
            bt[4, :] = -30000.0
            m["ckT"] = np.zeros((2, 128, 2, 256), f32)
            m["cv"] = np.zeros((2, 128, 2, 256), f32)
            m["sgla"] = np.zeros((2, 64, 2, 512), f32)
            m["sdn"] = np.zeros((2, 128, 2, 512), f32)
        else:
            b = core - 4
            cs[:, 0, :] = cos_s
            cs[:, 1, :] = sin_s
            at[0, :1024] = 1.0
            m["ckT"] = np.ascontiguousarray(inp["cache_k"][b].transpose(0, 3, 2, 1))
            m["cv"] = np.ascontiguousarray(
                inp["cache_v"][b].reshape(2, 2, 128, 256).transpose(0, 2, 1, 3))
            m["sgla"] = np.ascontiguousarray(inp["state_gla"][b].transpose(0, 3, 1, 2, 4)).reshape(2, 64, 2, 512)
            m["sdn"] = np.ascontiguousarray(inp["state_dn"][b].transpose(0, 3, 1, 2, 4)).reshape(2, 128, 2, 512)
        m["cossin"] = cs
        m["atab"] = at
        m["btab"] = bt
        in_maps.append(m)
    kc_ = os.environ.get("KCORES", "")
    if kc_:
        sel = [int(s) for s in kc_.split(",")]
        res = run_bass_kernel_spmd(nc, [in_maps[c] for c in sel], core_ids=list(range(len(sel))))
        R = [res.results[sel.index(c)] if c in sel else res.results[0] for c in range(8)]
    else:
        res = run_bass_kernel_spmd(nc, in_maps, core_ids=list(range(8)))
        R = res.results
    for n in dbg_names:
        LAST_DBG[n] = [np.asarray(R[c]["dbg_" + n]) for c in range(8)]
    y_prompt = np.stack([np.asarray(R[c]["y"]) for c in range(4)]).reshape(16, 256, D).astype(f32)
    y_sample = np.stack([np.asarray(R[c]["y"]) for c in range(4, 8)]).astype(f32)
    kT = np.stack([np.asarray(R[c]["kTo"]) for c in range(4)])
    kT = kT.transpose(0, 1, 2, 4, 3, 5).reshape(4, 2, 2, 128, 4, 256)
    nk = kT.transpose(0, 4, 1, 5, 2, 3).reshape(16, 2, 256, 2, 128)
    vo = np.stack([np.asarray(R[c]["vo"]) for c in range(4)])
    vo = vo.reshape(4, 2, 2, 4, 256, 128)
    nv = vo.transpose(0, 3, 1, 4, 2, 5).reshape(16, 2, 256, 2, 128)
    sgo = np.stack([np.asarray(R[c]["sgo"]) for c in range(4)])
    nsg = sgo.reshape(4, 2, 2, 4, 64, 4, 128).transpose(0, 3, 1, 2, 5, 4, 6).reshape(16, 2, 2, 4, 64, 128)
    sdo = np.stack([np.asarray(R[c]["sdo"]) for c in range(4)])
    nsd = sdo.reshape(4, 2, 2, 4, 128, 4, 128).transpose(0, 3, 1, 2, 5, 4, 6).reshape(16, 2, 2, 4, 128, 128)
    return (y_prompt, y_sample, np.ascontiguousarray(nk, dtype=f32), np.ascontiguousarray(nv, dtype=f32),
            np.ascontiguousarray(nsg, dtype=f32), np.ascontiguousarray(nsd, dtype=f32))
```
